# Optimizing a Trainium2 kernel written in Bass

```python
import jax, jax.numpy as jnp
from jax import lax
import numpy as np

D_MODEL = 1024
BATCH = 8
SEQ = 4096
DEPTH = 2

GLA_HEADS = 4
GLA_DK = 128
GLA_DV = 128
GLA_RANK = 16
GLA_TAU = 16.0
GLA_CHUNK = 64
CONV_CH = D_MODEL // 2
CONV_K = 3
SWA_Q_HEADS = 8
SWA_KV_HEADS = 2
SWA_HEAD_DIM = 64
SWA_WINDOW = 128
D_FF = 2816
FFN_CONV_K = 3
N_BRANCH = 3
EPS = 1e-6

MIX_SIZES = (
    GLA_HEADS * GLA_DK,
    GLA_HEADS * GLA_DK,
    GLA_HEADS * GLA_DV,
    GLA_HEADS * GLA_DV,
    GLA_RANK,
    CONV_CH,
    CONV_CH,
    CONV_CH,
    SWA_Q_HEADS * SWA_HEAD_DIM,
    SWA_KV_HEADS * SWA_HEAD_DIM,
    SWA_KV_HEADS * SWA_HEAD_DIM,
    N_BRANCH * D_MODEL,
)
D_IN_TOTAL = sum(MIX_SIZES)

kernel_name = 'hybrid_gla_shortconv_swa_sink_block'


def rmsnorm(x, g):
    xf = x.astype(jnp.float32)
    y = xf * lax.rsqrt(jnp.mean(xf * xf, axis=-1, keepdims=True) + EPS)
    return (y * g.astype(jnp.float32)).astype(x.dtype)


def split_columns(z):
    out = []
    off = 0
    for n in MIX_SIZES:
        out.append(z[..., off:off + n])
        off += n
    return out


def causal_dwconv(u, w):
    K = w.shape[0]
    S = u.shape[1]
    up = jnp.pad(u, ((0, 0), (K - 1, 0), (0, 0)))
    y = up[:, 0:S] * w[0]
    for i in range(1, K):
        y = y + up[:, i:i + S] * w[i]
    return y


def alibi_slopes(n_heads):
    return jnp.exp2(-(8.0 / n_heads) * jnp.arange(1, n_heads + 1, dtype=jnp.float32))


def gla_chunked(q, k, v, log_a):
    f32 = jnp.float32
    Bsz, S, H, dk = q.shape
    dv = v.shape[-1]
    C = GLA_CHUNK
    N = S // C
    q = (q.astype(f32) * dk ** -0.5).reshape(Bsz, N, C, H, dk)
    k = k.astype(f32).reshape(Bsz, N, C, H, dk)
    v = v.astype(f32).reshape(Bsz, N, C, H, dv)
    b = jnp.cumsum(log_a.astype(f32).reshape(Bsz, N, C, H, dk), axis=2)
    b_ref = b[:, :, C // 2 - 1:C // 2]
    b_last = b[:, :, C - 1:]
    causal = jnp.tril(jnp.ones((C, C), dtype=bool))
    attn = jnp.einsum('bnihk,bnjhk->bnhij', q * jnp.exp(b - b_ref), k * jnp.exp(b_ref - b))
    attn = jnp.where(causal, attn, 0.0)
    o_intra = jnp.einsum('bnhij,bnjhv->bnihv', attn, v)
    kv = jnp.einsum('bnjhk,bnjhv->bnhkv', k * jnp.exp(b_last - b), v)
    decay = jnp.exp(b_last[:, :, 0])

    def step(state, inp):
        d, kv_n = inp
        return d[..., None] * state + kv_n, state

    s0 = jnp.zeros((Bsz, H, dk, dv), f32)
    _, s_prev = lax.scan(step, s0, (jnp.moveaxis(decay, 1, 0), jnp.moveaxis(kv, 1, 0)))
    s_prev = jnp.moveaxis(s_prev, 0, 1)
    o_inter = jnp.einsum('bnihk,bnhkv->bnihv', q * jnp.exp(b), s_prev)
    return (o_intra + o_inter).reshape(Bsz, S, H, dv)


def swa_sink_attention(q, k, v, sinks, slopes):
    f32 = jnp.float32
    Bsz, S, Hq, hd = q.shape
    Hkv = k.shape[2]
    G = Hq // Hkv
    W = SWA_WINDOW
    Nb = S // W
    qb = (q.astype(f32) * hd ** -0.5).reshape(Bsz, Nb, W, Hkv, G, hd)

    def windows(t):
        tp = jnp.pad(t.astype(f32), ((0, 0), (W, 0), (0, 0), (0, 0))).reshape(Bsz, Nb + 1, W, Hkv, hd)
        return jnp.concatenate([tp[:, :-1], tp[:, 1:]], axis=2)

    kw = windows(k)
    vw = windows(v)
    scores = jnp.einsum('bnqhgd,bnkhd->bnhgqk', qb, kw)
    iq = jnp.arange(W)[:, None]
    jk = jnp.arange(2 * W)[None, :]
    dist = W + iq - jk
    blk = jnp.arange(Nb)[:, None, None]
    valid = (dist >= 0) & (dist < W) & (blk * W - W + jk >= 0)
    alibi = -slopes.reshape(Hkv, G)[:, :, None, None] * dist.astype(f32)
    logits = jnp.where(valid[None, :, None, None], scores + alibi, -jnp.inf)
    sink = sinks.astype(f32).reshape(Hkv, G)[None, None, :, :, None]
    m = jnp.maximum(logits.max(axis=-1), sink)
    p = jnp.exp(logits - m[..., None])
    denom = p.sum(axis=-1) + jnp.exp(sink - m)
    o = jnp.einsum('bnhgqk,bnkhd->bnqhgd', p, vw) / jnp.moveaxis(denom, 4, 2)[..., None]
    return o.reshape(Bsz, S, Hq * hd)


def setup_inputs(seed: int = 0) -> dict:
    key = jax.random.key(seed)
    ks = jax.random.split(key, 17)
    f32 = jnp.float32

    def nrm(k, shape, scale):
        return jax.random.normal(k, shape, f32) * scale

    return {
        'x': nrm(ks[0], (BATCH, SEQ, D_MODEL), 1.0),
        'g_mix': 1.0 + nrm(ks[1], (DEPTH, D_MODEL), 0.02),
        'w_in': nrm(ks[2], (DEPTH, D_MODEL, D_IN_TOTAL), D_MODEL ** -0.5),
        'gla_w_alpha': nrm(ks[3], (DEPTH, GLA_RANK, GLA_HEADS * GLA_DK), GLA_RANK ** -0.5),
        'gla_b_alpha': nrm(ks[4], (DEPTH, GLA_HEADS * GLA_DK), 0.02),
        'gla_norm_g': 1.0 + nrm(ks[5], (DEPTH, GLA_HEADS * GLA_DV), 0.02),
        'conv_w': nrm(ks[6], (DEPTH, CONV_K, CONV_CH), CONV_K ** -0.5),
        'swa_sinks': nrm(ks[7], (DEPTH, SWA_Q_HEADS), 0.5),
        'w_gla_o': nrm(ks[8], (DEPTH, GLA_HEADS * GLA_DV, D_MODEL), (GLA_HEADS * GLA_DV) ** -0.5),
        'w_conv_o': nrm(ks[9], (DEPTH, CONV_CH, D_MODEL), CONV_CH ** -0.5),
        'w_swa_o': nrm(ks[10], (DEPTH, SWA_Q_HEADS * SWA_HEAD_DIM, D_MODEL), (SWA_Q_HEADS * SWA_HEAD_DIM) ** -0.5),
        'w_o': nrm(ks[11], (DEPTH, D_MODEL, D_MODEL), D_MODEL ** -0.5),
        'g_ffn': 1.0 + nrm(ks[12], (DEPTH, D_MODEL), 0.02),
        'w_up': nrm(ks[13], (DEPTH, D_MODEL, 2 * D_FF), D_MODEL ** -0.5),
        'ffn_conv_w': nrm(ks[14], (DEPTH, FFN_CONV_K, 2 * D_FF), FFN_CONV_K ** -0.5),
        'w_down': nrm(ks[15], (DEPTH, D_FF, D_MODEL), D_FF ** -0.5),
        'g_final': 1.0 + nrm(ks[16], (D_MODEL,), 0.02),
    }


def reference(x, g_mix, w_in, gla_w_alpha, gla_b_alpha, gla_norm_g, conv_w, swa_sinks, w_gla_o, w_conv_o, w_swa_o, w_o, g_ffn, w_up, ffn_conv_w, w_down, g_final):
    f32 = jnp.float32
    Bsz, S, _ = x.shape
    slopes = alibi_slopes(SWA_Q_HEADS)
    h = x
    for l in range(DEPTH):
        u = rmsnorm(h, g_mix[l])
        z = u @ w_in[l]
        gq, gk, gv, gr, ga, cx, cb, cc, sq, sk, sv, gates = split_columns(z)

        log_a = jax.nn.log_sigmoid((ga @ gla_w_alpha[l] + gla_b_alpha[l]).astype(f32)) / GLA_TAU
        o = gla_chunked(gq.reshape(Bsz, S, GLA_HEADS, GLA_DK),
                        gk.reshape(Bsz, S, GLA_HEADS, GLA_DK),
                        gv.reshape(Bsz, S, GLA_HEADS, GLA_DV),
                        log_a.reshape(Bsz, S, GLA_HEADS, GLA_DK))
        o = o * lax.rsqrt(jnp.mean(o * o, axis=-1, keepdims=True) + EPS)
        o = o.reshape(Bsz, S, GLA_HEADS * GLA_DV) * gla_norm_g[l].astype(f32)
        y_gla = (o * jax.nn.silu(gr.astype(f32))).astype(x.dtype) @ w_gla_o[l]

        y_conv = (cb * causal_dwconv(cc * cx, conv_w[l])) @ w_conv_o[l]

        o = swa_sink_attention(sq.reshape(Bsz, S, SWA_Q_HEADS, SWA_HEAD_DIM),
                               sk.reshape(Bsz, S, SWA_KV_HEADS, SWA_HEAD_DIM),
                               sv.reshape(Bsz, S, SWA_KV_HEADS, SWA_HEAD_DIM),
                               swa_sinks[l], slopes)
        y_swa = o.astype(x.dtype) @ w_swa_o[l]

        gt = jax.nn.sigmoid(gates)
        merged = (gt[..., :D_MODEL] * y_gla
                  + gt[..., D_MODEL:2 * D_MODEL] * y_conv
                  + gt[..., 2 * D_MODEL:] * y_swa)
        h = h + merged @ w_o[l]

        u = rmsnorm(h, g_ffn[l])
        hid = causal_dwconv(u @ w_up[l], ffn_conv_w[l])
        h = h + (jax.nn.silu(hid[..., :D_FF]) * hid[..., D_FF:]) @ w_down[l]
    return rmsnorm(h, g_final)
```

```python
from contextlib import ExitStack
import numpy as np
import ml_dtypes
import concourse.bass as bass
import concourse.mybir as mybir
from concourse.bass_utils import run_bass_kernel_spmd

F32 = mybir.dt.float32
BF16 = mybir.dt.bfloat16
AF = mybir.ActivationFunctionType
ALU = mybir.AluOpType
AX = mybir.AxisListType

ENGS = ("pe", "act", "dve", "pool", "sp")
SAME_ENGINE_SYNC = {"pe": False, "act": True, "dve": True, "pool": True, "sp": True}

D = 1024
DIN = 7440
DFF = 2816
NT = 512
EPS = 1e-6
NEG = -30000.0


class Buf:
    __slots__ = ("name", "last_writer", "readers", "sem", "sem_cnt")

    def __init__(self, name):
        self.name = name
        self.last_writer = None
        self.readers = []
        self.sem = None
        self.sem_cnt = 0


class V:
    __slots__ = ("ap", "bufs")

    def __init__(self, ap, bufs):
        self.ap = ap
        self.bufs = tuple(bufs)

    def __getitem__(self, k):
        return V(self.ap[k], self.bufs)


class Op:
    __slots__ = ("eng", "fn", "idx", "pidx", "preds", "succs", "waits", "dma_waits", "signal", "signo", "is_dma",
                 "sem", "sem_val", "dur", "tab", "nbytes", "npend", "ready", "finish", "tag", "start", "prio")

    def __init__(self, eng, fn, is_dma=False):
        self.eng = eng
        self.fn = fn
        self.idx = -1
        self.pidx = -1
        self.preds = []
        self.succs = []
        self.waits = []
        self.dma_waits = []
        self.signal = False
        self.signo = 0
        self.is_dma = is_dma
        self.sem = None
        self.sem_val = 0
        self.dur = 100.0
        self.tab = None
        self.nbytes = 0
        self.npend = 0
        self.ready = 0.0
        self.finish = 0.0
        self.tag = None
        self.start = 0.0
        self.prio = 0.0


_ACT_SETS = {}


def _act_set(func):
    if func in (AF.Exp, AF.Ln):
        return "explog"
    if func == AF.Silu:
        return "silu"
    if func == AF.Sigmoid:
        return "sigmoid"
    return None


def _fsize(ap):
    n = 1
    for d in ap.shape[1:]:
        n *= d
    return n


class Prog:
    REORDER = ("pe", "act", "dve")

    def __init__(self, nc, stack):
        self.nc = nc
        self.stack = stack
        self.all_ops = []
        self.streams = {e: [] for e in ENGS}
        self.esem = {}
        for e in ("pe", "act", "dve", "pool"):
            self.esem[e] = stack.enter_context(nc.semaphore("es_" + e))
        self.final_dma = []
        self.marks = []
        self.sched = True
        self.prio_mode = "prog"
        self.cur_tag = None

    def mark(self, label):
        self.marks.append((label, len(self.streams["pe"])))
        self.cur_tag = (label, len(self.marks))

    def new_sem(self, name):
        return self.stack.enter_context(self.nc.semaphore(name))

    def op(self, eng, fn, reads=(), writes=(), dma=False, sem_buf=None, nowaw=False, dur=100.0, tab=None, nbytes=0):
        X = Op(eng, fn, is_dma=dma)
        X.pidx = len(self.all_ops)
        X.dur = dur
        X.tab = tab
        X.nbytes = nbytes
        X.tag = self.cur_tag
        if dma:
            b = sem_buf
            if b.sem is None:
                b.sem = self.new_sem("ds_" + b.name)
            b.sem_cnt += 16
            X.sem = b.sem
            X.sem_val = b.sem_cnt
        deps = []
        for r in reads:
            deps.append(r.last_writer)
        for w in writes:
            if not nowaw:
                deps.append(w.last_writer)
            deps.extend(w.readers)
        seen = set()
        for Y in deps:
            if Y is None or Y is X or id(Y) in seen:
                continue
            seen.add(id(Y))
            X.preds.append(Y)
        for r in reads:
            r.readers.append(X)
        for w in writes:
            w.last_writer = X
            w.readers = []
        self.all_ops.append(X)
        self.streams[eng].append(X)
        return X

    def dma(self, out, in_, queue="sp", final=False, nowaw=False):
        nb = 1
        for d in out.ap.shape:
            nb *= d
        nb *= 2 if out.ap.dtype == BF16 else 4
        X = self.op(queue, lambda e: e.dma_start(out=out.ap, in_=in_.ap), reads=in_.bufs, writes=out.bufs,
                    dma=True, sem_buf=out.bufs[0], nowaw=nowaw, dur=60.0, nbytes=nb)
        if final:
            self.final_dma.append(X)
        return X

    def mm(self, out, lhsT, rhs, start=True, stop=True):
        n = _fsize(rhs.ap)
        passes = 4 if rhs.ap.dtype == F32 else 1
        return self.op("pe", lambda e: e.matmul(out.ap, lhsT.ap, rhs.ap, start=start, stop=stop),
                       reads=lhsT.bufs + rhs.bufs, writes=out.bufs, dur=passes * max(n, 64) / 2.4 + 12)

    def tr(self, out, in_, ident):
        return self.op("pe", lambda e: e.transpose(out.ap, in_.ap, ident.ap), reads=in_.bufs + ident.bufs, writes=out.bufs,
                       dur=110.0)

    def act(self, out, in_, func, bias=None, scale=None, accum=None):
        reads = list(in_.bufs)
        kw = {}
        if bias is not None:
            if isinstance(bias, V):
                reads += bias.bufs
                kw["bias"] = bias.ap
            else:
                kw["bias"] = bias
        if scale is not None:
            if isinstance(scale, V):
                reads += scale.bufs
                kw["scale"] = scale.ap
            else:
                kw["scale"] = scale
        writes = list(out.bufs)
        d = 180.0 + _fsize(in_.ap) / 1.2
        if accum is not None:
            writes += accum.bufs
            kw["accum_out"] = accum.ap
            d += 100
        return self.op("act", lambda e: e.activation(out=out.ap, in_=in_.ap, func=func, **kw), reads=reads, writes=writes,
                       dur=d, tab=_act_set(func))

    def copy(self, eng, out, in_):
        n = _fsize(in_.ap)
        if eng == "act":
            return self.op("act", lambda e: e.copy(out.ap, in_.ap), reads=in_.bufs, writes=out.bufs, dur=180.0 + n / 1.2)
        d = (100.0 + 1.15 * n) if eng == "dve" else (350.0 + 1.0 * n)
        return self.op(eng, lambda e: e.tensor_copy(out.ap, in_.ap), reads=in_.bufs, writes=out.bufs, dur=d)

    def tt(self, eng, out, in0, in1, op):
        n = _fsize(in0.ap)
        d = (100.0 + 1.15 * n) if eng == "dve" else (150.0 + 2.2 * n)
        return self.op(eng, lambda e: e.tensor_tensor(out=out.ap, in0=in0.ap, in1=in1.ap, op=op),
                       reads=in0.bufs + in1.bufs, writes=out.bufs, dur=d)

    def ts(self, eng, out, in0, s1, op0, s2=None, op1=None):
        reads = list(in0.bufs)
        a1 = s1
        if isinstance(s1, V):
            reads += s1.bufs
            a1 = s1.ap
        a2 = s2
        if isinstance(s2, V):
            reads += s2.bufs
            a2 = s2.ap
        n = _fsize(in0.ap)
        d = (100.0 + 1.15 * n) if eng == "dve" else (300.0 + 11.0 * n)
        if op1 is None:
            return self.op(eng, lambda e: e.tensor_scalar(out=out.ap, in0=in0.ap, scalar1=a1, scalar2=None, op0=op0),
                           reads=reads, writes=out.bufs, dur=d)
        return self.op(eng, lambda e: e.tensor_scalar(out=out.ap, in0=in0.ap, scalar1=a1, scalar2=a2, op0=op0, op1=op1),
                       reads=reads, writes=out.bufs, dur=d)

    def stt(self, out, in0, scalar, in1, op0, op1):
        reads = list(in0.bufs) + list(in1.bufs)
        sc = scalar
        if isinstance(scalar, V):
            reads += scalar.bufs
            sc = scalar.ap
        return self.op("dve", lambda e: e.scalar_tensor_tensor(out=out.ap, in0=in0.ap, scalar=sc, in1=in1.ap, op0=op0, op1=op1),
                       reads=reads, writes=out.bufs, dur=120.0 + 1.2 * _fsize(in0.ap))

    def reduce(self, out, in_, op, axis=AX.X):
        return self.op("dve", lambda e: e.tensor_reduce(out=out.ap, in_=in_.ap, axis=axis, op=op), reads=in_.bufs, writes=out.bufs,
                       dur=100.0 + 1.1 * _fsize(in_.ap))

    def recip(self, out, in_):
        return self.op("dve", lambda e: e.reciprocal(out.ap, in_.ap), reads=in_.bufs, writes=out.bufs,
                       dur=100.0 + 8.4 * _fsize(in_.ap))

    def memset(self, eng, out, val):
        return self.op(eng, lambda e: e.memset(out.ap, val), writes=out.bufs, dur=200.0)

    def schedule(self):
        import heapq
        ops = self.all_ops
        for X in ops:
            X.succs = []
        for X in ops:
            for Y in X.preds:
                Y.succs.append(X)
        fixed_prev = {}
        extra = {}
        for X in ops:
            if X.eng not in self.REORDER:
                pv = fixed_prev.get(X.eng)
                if pv is not None:
                    extra[id(X)] = pv
                    pv.succs.append(X)
                fixed_prev[X.eng] = X
        for X in ops:
            X.npend = len(X.preds) + (1 if id(X) in extra else 0)
            X.ready = 0.0
        if self.prio_mode == "bl":
            bl = {}
            for X in reversed(ops):
                m = 0.0
                for Z in X.succs:
                    v = bl[id(Z)] + 200.0
                    if v > m:
                        m = v
                bl[id(X)] = m + (X.dur if not X.is_dma else X.nbytes / 260.0 + 2000.0)
            for X in ops:
                X.prio = -bl[id(X)]
        else:
            for X in ops:
                X.prio = float(X.pidx)
        HOP = 200.0
        free_at = {e: 0.0 for e in ENGS}
        pending = {e: [] for e in ENGS}
        avail = {e: [] for e in ENGS}
        last_tab = {"act": None}
        dma_free = [0.0]
        for X in ops:
            if X.npend == 0:
                heapq.heappush(pending[X.eng], (0.0, X.pidx, X))
        order = {e: [] for e in ENGS}
        nleft = len(ops)
        while nleft:
            best = None
            for e in ENGS:
                T = free_at[e]
                pq, av = pending[e], avail[e]
                while pq and pq[0][0] <= T:
                    r, pi, X = heapq.heappop(pq)
                    heapq.heappush(av, (X.prio, pi, X))
                if av:
                    st = T
                elif pq:
                    st = pq[0][0]
                else:
                    continue
                if best is None or st < best[0]:
                    best = (st, e)
            st, e = best
            if avail[e]:
                pr, pi, X = heapq.heappop(avail[e])
            else:
                r, pi, X = heapq.heappop(pending[e])
            d = X.dur
            if e == "act" and X.tab is not None and X.tab != last_tab["act"]:
                d += 1300.0
                last_tab["act"] = X.tab
            if X.is_dma:
                free_at[e] = st + d
                t0 = max(dma_free[0], st + d)
                t1 = t0 + X.nbytes / 260.0
                dma_free[0] = t1
                X.finish = t1 + 2000.0
            else:
                X.finish = st + d
                free_at[e] = X.finish
            X.start = st
            order[e].append(X)
            nleft -= 1
            for Z in X.succs:
                Z.npend -= 1
                rt = X.finish + (HOP if Z.eng != X.eng or X.is_dma else 60.0)
                if rt > Z.ready:
                    Z.ready = rt
                if Z.npend == 0:
                    heapq.heappush(pending[Z.eng], (Z.ready, Z.pidx, Z))
        self.streams = order
        self.est_ns = max(free_at.values())

    def resolve(self):
        for e in ENGS:
            for i, X in enumerate(self.streams[e]):
                X.idx = i
        waited = {e: {f: -1 for f in ENGS} for e in ENGS}
        waited_dma = {e: {} for e in ENGS}
        for e in ENGS:
            for X in self.streams[e]:
                best = {}
                for Y in X.preds:
                    if Y.is_dma:
                        cur = waited_dma[e].get(Y.sem, 0)
                        if cur < Y.sem_val:
                            waited_dma[e][Y.sem] = Y.sem_val
                            X.dma_waits.append((Y.sem, Y.sem_val))
                        continue
                    if Y.eng == e:
                        assert Y.idx < X.idx, "same-engine order violated"
                        if not SAME_ENGINE_SYNC[e]:
                            continue
                    cur = best.get(Y.eng)
                    if cur is None or Y.idx > cur.idx:
                        best[Y.eng] = Y
                for f, Y in best.items():
                    if waited[e][f] >= Y.idx:
                        continue
                    waited[e][f] = Y.idx
                    Y.signal = True
                    X.waits.append(Y)

    def emit(self):
        nc = self.nc
        if self.sched:
            self.schedule()
        self.resolve()
        for e in ("pe", "act", "dve", "pool"):
            n = 0
            for X in self.streams[e]:
                if X.signal:
                    n += 1
                    X.signo = n
        with nc.Block() as block:
            def make(e):
                def body(eng):
                    for X in self.streams[e]:
                        for Y in X.waits:
                            eng.wait_ge(self.esem[Y.eng], Y.signo)
                        for (s, v) in X.dma_waits:
                            eng.wait_ge(s, v)
                        ins = X.fn(eng)
                        if X.is_dma:
                            ins.then_inc(X.sem, 16)
                        elif X.signal:
                            ins.then_inc(self.esem[e], 1)
                    if e == "sp":
                        for X in self.final_dma:
                            eng.wait_ge(X.sem, X.sem_val)
                return body
            block.tensor(make("pe"))
            block.scalar(make("act"))
            block.vector(make("dve"))
            block.gpsimd(make("pool"))
            block.sync(make("sp"))


class Ring:
    def __init__(self, items):
        self.items = items
        self.i = 0

    def __call__(self):
        v = self.items[self.i % len(self.items)]
        self.i += 1
        return v


def _consts():
    c = {}
    c["identf"] = np.eye(128, dtype=np.float32)
    c["identb"] = np.eye(128, dtype=np.float32).astype(ml_dtypes.bfloat16)
    c["onesf"] = np.ones((128, 128), np.float32)
    j = np.arange(128)[:, None]
    i = np.arange(128)[None, :]
    same = (j // 64) == (i // 64)
    tri = (same & (j <= i)).astype(np.float32)
    ref = (i // 64) * 64 + 31
    last = (i // 64) * 64 + 63
    t_ref = (same & (j <= ref)).astype(np.float32)
    t_last = (same & (j <= last)).astype(np.float32)
    sc = -1.0 / 16.0
    T0 = sc * tri
    T1 = sc * (tri - t_ref)
    T2 = sc * (t_last - tri)
    c["tri"] = np.ascontiguousarray(np.stack([T0, T1, T2], axis=1)).astype(np.float32)
    c["gmask"] = np.ascontiguousarray(np.tile(tri, (1, 4))).astype(np.float32)
    slopes = np.exp2(-(np.arange(1, 9, dtype=np.float64))).astype(np.float32)
    iq = np.arange(128)[:, None]
    jk = np.arange(256)[None, :]
    dist = 128 + iq - jk
    valid = (dist >= 0) & (dist < 128)
    bias = np.zeros((128, 2, 4, 256), np.float32)
    for g in range(2):
        for jh in range(4):
            h = 4 * g + jh
            bias[:, g, jh, :] = np.where(valid, -slopes[h] * dist.astype(np.float32), NEG)
    c["swab"] = np.ascontiguousarray(bias.reshape(128, 2, 1024))
    return c


def _chunkcols(v):
    n = v.shape[0] // 128
    return np.ascontiguousarray(v.reshape(n, 128).T)


def _layout_params(L, g_mix, g_ffn, g_final, gla_norm_g, conv_w, ffn_conv_w, swa_sinks, gla_w_alpha, gla_b_alpha):
    gv = np.concatenate([_chunkcols(g_mix[l]) for l in range(L)] + [_chunkcols(g_ffn[l]) for l in range(L)]
                        + [_chunkcols(g_final)], axis=1).astype(np.float32)
    glag = np.concatenate([_chunkcols(gla_norm_g[l]) for l in range(L)], axis=1).astype(np.float32)
    cw = np.stack([np.stack([_chunkcols(conv_w[l, k]) for k in range(3)], axis=2) for l in range(L)], axis=1)
    fw = np.stack([np.stack([_chunkcols(ffn_conv_w[l, k]) for k in range(3)], axis=2) for l in range(L)], axis=1)
    sinks = np.ascontiguousarray(np.broadcast_to(swa_sinks[:L].reshape(1, L * 8), (128, L * 8))).astype(np.float32)
    wal = np.concatenate([gla_w_alpha[:L], gla_b_alpha[:L, None, :]], axis=1).astype(np.float32)
    wal = np.ascontiguousarray(np.transpose(wal, (1, 0, 2)))
    return {"gv": gv, "glag": glag, "convw": np.ascontiguousarray(cw.astype(np.float32)),
            "fconvw": np.ascontiguousarray(fw.astype(np.float32)), "sinks": sinks, "walpha": wal}


class _Stop(Exception):
    pass


_MARKS = {}


def build_nc(S, L, final_norm=True, dbg=None, stage=None):
    assert S % NT == 0
    NTILES = S // NT
    nc = bass.Bass("TRN2", target_bir_lowering=False)

    def dram(name, shape, dt, kind="ExternalInput"):
        return nc.dram_tensor(name, list(shape), dt, kind=kind).ap()

    x_d = dram("x", [S, D], F32)
    out_d = dram("out", [S, D], F32, kind="ExternalOutput")
    wspec = {"w_in": (D, DIN), "w_gla_o": (512, D), "w_conv_o": (512, D), "w_swa_o": (512, D),
             "w_o": (D, D), "w_up": (D, 2 * DFF), "w_down": (DFF, D)}
    w_d = {k: dram(k, [L, r, c], F32) for k, (r, c) in wspec.items()}
    wb_d = {k: dram(k + "_bf", [L, r, c], BF16, kind="Internal") for k, (r, c) in wspec.items()}
    gv_d = dram("gv", [128, (2 * L + 1) * 8], F32)
    glag_d = dram("glag", [128, L * 4], F32)
    convw_d = dram("convw", [128, L, 4, 3], F32)
    fconvw_d = dram("fconvw", [128, L, 44, 3], F32)
    sinks_d = dram("sinks", [128, L * 8], F32)
    walpha_d = dram("walpha", [17, L, 512], F32)
    identf_d = dram("identf", [128, 128], F32)
    identb_d = dram("identb", [128, 128], BF16)
    onesf_d = dram("onesf", [128, 128], F32)
    tri_d = dram("tri", [128, 3, 128], F32)
    gmask_d = dram("gmask", [128, 512], F32)
    swab_d = dram("swab", [128, 2, 1024], F32)
    dbg_d = {}
    if dbg:
        for name, (shape, dt) in dbg.items():
            dbg_d[name] = dram("dbg_" + name, shape, dt, kind="ExternalOutput")

    with ExitStack() as st:
        P = Prog(nc, st)

        def chk(n):
            P.mark(n)
            if stage is not None and stage == n:
                raise _Stop()

        def sb(name, shape, dt):
            return st.enter_context(nc.sbuf_tensor("s_" + name, list(shape), dt))

        def ps(name, shape, dt):
            return st.enter_context(nc.psum_tensor(name, list(shape), dt))

        def VT(name, shape, dt):
            t = sb(name, shape, dt)
            return V(t[:], [Buf(name)])

        def chunked(name, n, cols, dt):
            t = sb(name, [128, n, cols], dt)
            return t, [V(t[:, i, :], [Buf("%s%d" % (name, i))]) for i in range(n)]

        hT_t, hT = chunked("hT", 8, NT, F32)
        uT_t, uT = chunked("uT", 8, NT, BF16)
        ar_t, ar = chunked("arena", 22, NT, BF16)
        goT, cvT, oswT, mgT, gT = ar[0:4], ar[4:8], ar[8:12], ar[12:20], ar
        qs_t, qs = chunked("qs", 4, NT, BF16)
        sg_t, sigb = chunked("sigb", 24, NT, BF16)
        vt_t, vt = chunked("vt", 4, NT, BF16)
        kT = [[VT("kT%d_%d" % (l, g), [128, 640], BF16) for g in range(2)] for l in range(L)]
        Vr = []
        for l in range(L):
            t = sb("Vr%d" % l, [128, 5, 128], BF16)
            Vr.append([V(t[:, i, :], [Buf("Vr%d_%d" % (l, i))]) for i in range(5)])
        swab = VT("swab", [128, 2, 1024], F32)
        Sst = [[VT("Sst%d_%d" % (l, h), [128, 128], F32) for h in range(4)] for l in range(L)]
        Sbf = []
        for i in range(2):
            t = sb("Sbf%d" % i, [128, 9, 128], BF16)
            Sbf.append([V(t[:, c, :], [Buf("Sbf%d_%d" % (i, c))]) for c in range(9)])
        hc_t = sb("halo_c", [128, L, 4, 2], F32)
        halo_c = [[V(hc_t[:, l, c, :], [Buf("hc%d_%d" % (l, c))]) for c in range(4)] for l in range(L)]
        hf_t = sb("halo_f", [128, L, 44, 2], F32)
        halo_f = [[V(hf_t[:, l, c, :], [Buf("hf%d_%d" % (l, c))]) for c in range(44)] for l in range(L)]
        halo_all = [V(hc_t[:], [b for l in range(L) for c in range(4) for b in halo_c[l][c].bufs]),
                    V(hf_t[:], [b for l in range(L) for c in range(44) for b in halo_f[l][c].bufs])]
        identf = VT("identf", [128, 128], F32)
        identb = VT("identb", [128, 128], BF16)
        onesf = VT("onesf", [128, 128], F32)
        tri = VT("tri", [128, 3, 128], F32)
        gmask = VT("gmask", [128, 512], F32)
        gv = VT("gv", [128, (2 * L + 1) * 8], F32)
        glag = VT("glag", [128, L * 4], F32)
        convw = VT("convw", [128, L, 4, 3], F32)
        fconvw = VT("fconvw", [128, L, 44, 3], F32)
        sinks = VT("sinks", [128, L * 8], F32)
        walpha = VT("walpha", [17, L, 512], F32)
        gaaug = VT("gaaug", [32, 512], F32)
        xin = VT("xin", [128, 1024], F32)
        epsb = VT("epsb", [128, 1], F32)
        NF = 9
        fp_t = sb("fpool", [128, NF, 514], F32)
        fpool = Ring([V(fp_t[:, i, :], [Buf("fp%d" % i)]) for i in range(NF)])
        NB = 20
        bp_t = sb("bpool", [128, NB, 512], BF16)
        bpool = Ring([V(bp_t[:, i, :], [Buf("bp%d" % i)]) for i in range(NB)])
        sc_t = sb("scpool", [128, 2, 1024], F32)
        scV = [V(sc_t[:, i, :], [Buf("sc%d" % i)]) for i in range(2)]
        scpool = Ring(scV)
        ltokb = [scV[tb // 2][:, (tb % 2) * 512:(tb % 2 + 1) * 512] for tb in range(4)]
        pp_t = sb("ppool", [128, 2, 1024], BF16)
        ppool = Ring([V(pp_t[:, i, :], [Buf("pp%d" % i)]) for i in range(2)])
        pt_t = sb("ptpool", [128, 2, 1024], BF16)
        ptpool = Ring([V(pt_t[:, i, :], [Buf("pt%d" % i)]) for i in range(2)])
        kt_t = sb("ktok", [128, 2, 4, 128], BF16)
        ktpool = Ring([(i, [V(kt_t[:, i, tb, :], [Buf("ktok%d_%d" % (i, tb))]) for tb in range(4)]) for i in range(2)])
        NS = 24
        sm_t = sb("small", [128, NS, 8], F32)
        small = Ring([V(sm_t[:, i, :], [Buf("sm%d" % i)]) for i in range(NS)])
        NW = 4
        wr_t = sb("wring", [128, NW, 8, 512], BF16)
        wring = Ring([V(wr_t[:, i, :, :], [Buf("wr%d" % i)]) for i in range(NW)])
        NPS = 7
        psb = [ps("ps%d" % i, [128, 512], F32) for i in range(NPS)]
        psV = [V(psb[i][:], [Buf("ps%d" % i)]) for i in range(NPS)]
        pspool = Ring(psV)
        mixring = Ring(psV[0:5])
        poring = Ring(psV[5:6])
        gatering = Ring(psV[6:7])
        pbt = ps("psbf", [128, 1024], BF16)
        PB = V(pbt[:], [Buf("psbf")])

        def _body():
            for dst, src in ((identf, identf_d), (identb, identb_d), (onesf, onesf_d), (tri, tri_d), (gmask, gmask_d),
                             (swab, swab_d), (gv, gv_d), (glag, glag_d), (convw, convw_d), (fconvw, fconvw_d),
                             (sinks, sinks_d), (walpha, walpha_d)):
                P.dma(dst, V(src, []))
            P.memset("pool", epsb, EPS)
            P.memset("pool", gaaug, 1.0)
            P.memset("pool", halo_all[0], 0.0)
            P.memset("pool", halo_all[1], 0.0)
            for l in range(L):
                for h in range(4):
                    P.memset("pool", Sst[l][h], 0.0)
                for g in range(2):
                    P.memset("pool", kT[l][g], 0.0)
                P.memset("pool", Vr[l][0], 0.0)

            chk(1)
            wbuf = {}
            order = ["w_in", "w_gla_o", "w_conv_o", "w_swa_o", "w_o", "w_up", "w_down"]
            for l in range(L):
                for k in order:
                    r, c = wspec[k]
                    b = Buf("%s_bf%d" % (k, l))
                    wbuf[(k, l)] = b
                    for r0 in range(0, r, 128):
                        P.dma(V(wb_d[k][l, r0:r0 + 128, :], [b]), V(w_d[k][l, r0:r0 + 128, :], []), queue="pool", nowaw=True)

            chk(2)
            def wsrc(k, l, r0, nk, c0, ncol):
                ap = wb_d[k][l, r0 * 128:(r0 + nk) * 128, c0:c0 + ncol].rearrange("(kc p) n -> p kc n", p=128)
                return V(ap, [wbuf[(k, l)]])

            def wload(k, l, r0, nk, c0, ncol):
                slot = wring()
                P.dma(slot[:, 0:nk, 0:ncol], wsrc(k, l, r0, nk, c0, ncol), final=(stage in (61, 62)))
                return slot

            def dump(name, v):
                if dbg and name in dbg_d:
                    P.dma(V(dbg_d[name], [Buf("dbg_" + name)]), v, final=True)

            def rms_stats(srcs, nparts_scale, ring=None):
                pst = (ring or pspool)()
                n = len(srcs)
                for i, s in enumerate(srcs):
                    sq = fpool()
                    P.act(sq[:, 0:NT], s, AF.Square)
                    P.mm(pst, onesf, sq[:, 0:NT], start=(i == 0), stop=(i == n - 1))
                ln = fpool()
                P.act(ln[:, 0:NT], pst, AF.Ln, bias=epsb, scale=nparts_scale)
                r = fpool()
                P.act(r[:, 0:NT], ln[:, 0:NT], AF.Exp, scale=-0.5)
                return r

            def norm_to_uT(gcol0):
                r = rms_stats(hT, 1.0 / D)
                for c in range(8):
                    P.stt(uT[c], hT[c], gv[:, gcol0 + c:gcol0 + c + 1], r[:, 0:NT], ALU.mult, ALU.mult)

            def proj(wslot, col0, ncols_m, kcs=8, rhs_list=None, ring=None):
                rhs_list = rhs_list if rhs_list is not None else uT
                pt = (ring or pspool)()
                for kc in range(kcs):
                    P.mm(pt[0:ncols_m, :], wslot[:, kc, col0:col0 + ncols_m], rhs_list[kc], start=(kc == 0), stop=(kc == kcs - 1))
                return pt

            for t in range(NTILES):
                tok0 = t * NT
                for tb in range(4):
                    if tb == 0:
                        xs = xin
                        if t == 0:
                            P.dma(xs, V(x_d[0:128, :], []))
                    else:
                        xs = scpool()
                        P.dma(xs, V(x_d[tok0 + tb * 128: tok0 + (tb + 1) * 128, :], []))
                    for half in range(2):
                        pt = pspool()
                        for cc in range(4):
                            c = half * 4 + cc
                            P.tr(pt[:, cc * 128:(cc + 1) * 128], xs[:, c * 128:(c + 1) * 128], identf)
                        dst = V(hT_t[:, half * 4:(half + 1) * 4, tb * 128:(tb + 1) * 128],
                                [hT[half * 4 + cc].bufs[0] for cc in range(4)])
                        src = V(pt.ap.rearrange("p (c n) -> p c n", c=4), pt.bufs)
                        P.copy("act" if half == 0 else "dve", dst, src)

                chk(3)
                for l in range(L):
                    first = (t == 0)
                    norm_to_uT(l * 8)
                    if t == 0 and l == 0:
                        dump("uT0", V(uT_t[:, 0, :], uT[0].bufs))

                    chk(4)
                    gring = mixring
                    wsm = wring()
                    P.dma(wsm[:, :, 0:16], wsrc("w_in", l, 0, 8, 2048, 16), nowaw=True)
                    for g in range(2):
                        for d2 in range(2):
                            P.dma(wsm[:, :, 16 + g * 128 + d2 * 64: 16 + g * 128 + (d2 + 1) * 64],
                                  wsrc("w_in", l, 0, 8, 4112 + g * 64, 64), nowaw=True)
                    P.dma(wsm[:, :, 272:400], wsrc("w_in", l, 0, 8, 4240, 128), nowaw=True)
                    pga = gring()
                    for kc in range(8):
                        P.mm(pga[0:16, :], wsm[:, kc, 0:16], uT[kc], start=(kc == 0), stop=(kc == 7))
                    P.copy("act", gaaug[0:16, :], pga[0:16, :])
                    ltok = []
                    for tb in range(4):
                        px = gring()
                        P.mm(px, gaaug[0:17, tb * 128:(tb + 1) * 128], walpha[0:17, l, :])
                        e = ltokb[tb]
                        P.act(e, px, AF.Exp, scale=-1.0)
                        ltok.append(e)
                    for tb in range(4):
                        P.act(ltok[tb], ltok[tb], AF.Ln, bias=1.0)
                    chk(5)
                    for g in range(2):
                        pk = gring()
                        for kc in range(8):
                            P.mm(pk, wsm[:, kc, 16 + g * 128:16 + (g + 1) * 128], uT[kc], start=(kc == 0), stop=(kc == 7))
                        P.copy("act", kT[l][g][:, 128:640], pk)
                    for tb in range(4):
                        pv = gring()
                        for kc in range(8):
                            P.mm(pv[:, 0:128], uT[kc][:, tb * 128:(tb + 1) * 128], wsm[:, kc, 272:400], start=(kc == 0), stop=(kc == 7))
                        P.copy("act", Vr[l][tb + 1], pv[:, 0:128])
                    w_sq = wload("w_in", l, 0, 8, 3600, 512)
                    for c4 in range(4):
                        pq = proj(w_sq, c4 * 128, 128, ring=gring)
                        P.act(qs[c4], pq, AF.Copy, scale=0.125)
                    chk(8)
                    w_cx = wload("w_in", l, 0, 8, 2064, 512)
                    w_cb = wload("w_in", l, 0, 8, 2576, 512)
                    w_cc = wload("w_in", l, 0, 8, 3088, 512)
                    for c4 in range(4):
                        pcx = proj(w_cx, c4 * 128, 128, ring=gring)
                        cxs = fpool()
                        P.copy("act", cxs[:, 0:NT], pcx)
                        pcc = proj(w_cc, c4 * 128, 128, ring=gring)
                        pbuf = fpool()
                        P.copy("pool", pbuf[:, 0:2], halo_c[l][c4])
                        P.tt("dve", pbuf[:, 2:2 + NT], pcc, cxs[:, 0:NT], ALU.mult)
                        P.copy("pool", halo_c[l][c4], pbuf[:, NT:NT + 2])
                        acc = fpool()
                        P.act(acc[:, 0:NT], pbuf[:, 0:NT], AF.Copy, scale=convw[:, l, c4, 0:1])
                        P.stt(acc[:, 0:NT], pbuf[:, 1:1 + NT], convw[:, l, c4, 1:2], acc[:, 0:NT], ALU.mult, ALU.add)
                        P.stt(acc[:, 0:NT], pbuf[:, 2:2 + NT], convw[:, l, c4, 2:3], acc[:, 0:NT], ALU.mult, ALU.add)
                        pcb = proj(w_cb, c4 * 128, 128, ring=gring)
                        P.tt("dve", cvT[c4], pcb, acc[:, 0:NT], ALU.mult)
                    chk(6)
                    w_gv = wload("w_in", l, 0, 8, 1024, 512)
                    for tb in range(4):
                        pv = gring()
                        for kc in range(8):
                            P.mm(pv, uT[kc][:, tb * 128:(tb + 1) * 128], w_gv[:, kc, :], start=(kc == 0), stop=(kc == 7))
                        P.copy("act", vt[tb], pv)
                    w_gq = wload("w_in", l, 0, 8, 0, 512)
                    w_gk = wload("w_in", l, 0, 8, 512, 512)
                    HS = [slice(hh * 128, (hh + 1) * 128) for hh in range(4)]
                    gl = [dict() for _ in range(4)]

                    def G1(hh):
                        hs = HS[hh]
                        d = gl[hh]
                        pA, pB, pb = gring(), gring(), gring()
                        for tb in range(4):
                            ts_ = slice(tb * 128, (tb + 1) * 128)
                            P.mm(pA[:, ts_], ltok[tb][:, hs], tri[:, 1, :])
                            P.mm(pB[:, ts_], ltok[tb][:, hs], tri[:, 2, :])
                            P.mm(pb[:, ts_], ltok[tb][:, hs], tri[:, 0, :])
                        E1, E2, E3, E4 = fpool(), fpool(), fpool(), fpool()
                        P.act(E1[:, 0:NT], pA, AF.Exp)
                        P.act(E2[:, 0:NT], pA, AF.Exp, scale=-1.0)
                        P.act(E3[:, 0:NT], pB, AF.Exp)
                        P.act(E4[:, 0:NT], pb, AF.Exp)
                        pq = proj(w_gq, hh * 128, 128, ring=gring)
                        d["qa"], d["qb"] = bpool(), bpool()
                        P.stt(d["qa"], pq, 128.0 ** -0.5, E1[:, 0:NT], ALU.mult, ALU.mult)
                        P.stt(d["qb"], pq, 128.0 ** -0.5, E4[:, 0:NT], ALU.mult, ALU.mult)
                        pk = proj(w_gk, hh * 128, 128, ring=gring)
                        d["ka"], d["kb"] = bpool(), bpool()
                        P.tt("dve", d["ka"], pk, E2[:, 0:NT], ALU.mult)
                        P.tt("dve", d["kb"], pk, E3[:, 0:NT], ALU.mult)
                        d["dec"] = small()
                        P.copy("pool", d["dec"], V(E4.ap[:, 0:NT].rearrange("p (c k) -> p c k", c=8)[:, :, 63], E4.bufs))
                        if t == 0 and l == 0 and hh == 0:
                            dump("E4", E4[:, 0:NT])

                    def G2(hh):
                        hs = HS[hh]
                        d = gl[hh]
                        ktok = ktpool()
                        for tb in range(4):
                            P.tr(PB[:, tb * 128:(tb + 1) * 128], d["kb"][:, tb * 128:(tb + 1) * 128], identb)
                        P.copy("act", V(kt_t[:, ktok[0], :, :], [b for tb in range(4) for b in ktok[1][tb].bufs]),
                               V(PB.ap[:, 0:512].rearrange("p (a b) -> p a b", a=4), PB.bufs))
                        ktok = ktok[1]
                        pat = gring()
                        for tb in range(4):
                            ts_ = slice(tb * 128, (tb + 1) * 128)
                            P.mm(pat[:, ts_], d["ka"][:, ts_], d["qa"][:, ts_])
                        d["am"] = bpool()
                        P.tt("dve", d["am"], pat, gmask, ALU.mult)
                        pkv = [gring(), gring()]
                        for c in range(8):
                            tb, r0 = c // 2, (c % 2) * 64
                            P.mm(pkv[c % 2][:, tb * 128:(tb + 1) * 128], ktok[tb][r0:r0 + 64, :], vt[tb][r0:r0 + 64, hs])
                        d["pkv"] = pkv

                    def G3(hh):
                        d = gl[hh]
                        sb_ = Sbf[hh % 2]
                        pkv = d["pkv"]
                        P.copy("dve", sb_[0], Sst[l][hh])
                        for c in range(8):
                            tb = c // 2
                            P.stt(Sst[l][hh], Sst[l][hh], d["dec"][:, c:c + 1], pkv[c % 2][:, tb * 128:(tb + 1) * 128],
                                  ALU.mult, ALU.add)
                            P.copy("act", sb_[c + 1], Sst[l][hh])

                    def G4(hh):
                        hs = HS[hh]
                        d = gl[hh]
                        sb_ = Sbf[hh % 2]
                        po = poring()
                        for tb in range(4):
                            ts_ = slice(tb * 128, (tb + 1) * 128)
                            P.mm(po[:, ts_], vt[tb][:, hs], d["am"][:, ts_], start=True, stop=False)
                            for c2 in range(2):
                                c = tb * 2 + c2
                                cs = slice(c * 64, (c + 1) * 64)
                                P.mm(po[:, cs], sb_[c], d["qb"][:, cs], start=False, stop=(c2 == 1))
                        d["po"] = po

                    def G5(hh, w_gr):
                        d = gl[hh]
                        po = d["po"]
                        pg = proj(w_gr, hh * 128, 128, ring=gring)
                        sg = fpool()
                        P.act(sg[:, 0:NT], pg, AF.Exp, scale=-1.0)
                        P.act(sg[:, 0:NT], sg[:, 0:NT], AF.Ln, bias=1.0)
                        P.act(sg[:, 0:NT], sg[:, 0:NT], AF.Exp, scale=-1.0)
                        P.stt(sg[:, 0:NT], pg, 1.0, sg[:, 0:NT], ALU.mult, ALU.mult)
                        r = rms_stats([po], 1.0 / 128, ring=gring)
                        tmp = fpool()
                        P.stt(tmp[:, 0:NT], po, glag[:, l * 4 + hh:l * 4 + hh + 1], r[:, 0:NT], ALU.mult, ALU.mult)
                        P.tt("pool", goT[hh], tmp[:, 0:NT], sg[:, 0:NT], ALU.mult)
                        if t == 0 and l == 0 and hh == 0:
                            dump("tmp_o", tmp[:, 0:NT])

                    for hh in range(4):
                        G1(hh)
                    chk(7)
                    w_gr = wload("w_in", l, 0, 8, 1536, 512)
                    G2(0); G3(0)
                    G2(1); G3(1)
                    G4(0); G5(0, w_gr)
                    G2(2); G3(2)
                    G4(1); G5(1, w_gr)
                    G2(3); G3(3)
                    G4(2); G5(2, w_gr)
                    G4(3); G5(3, w_gr)
                    chk(9)
                    sw = {}

                    def SA(i):
                        tb, g = i // 2, i % 2
                        sA, sB = psV[(i % 2) * 2], psV[(i % 2) * 2 + 1]
                        for j in range(4):
                            h = 4 * g + j
                            ch, po_ = h // 2, (h % 2) * 64
                            bank = sA if j % 2 == 0 else sB
                            P.mm(bank[:, (j // 2) * 256:(j // 2 + 1) * 256], qs[ch][po_:po_ + 64, tb * 128:(tb + 1) * 128],
                                 kT[l][g][po_:po_ + 64, tb * 128:tb * 128 + 256])
                        sc = scpool()
                        sc4 = V(sc.ap.rearrange("p (j k) -> p j k", j=4), sc.bufs)
                        sw4 = V(swab.ap[:, g, :].rearrange("p (j k) -> p j k", j=4), swab.bufs)
                        P.tt("dve", sc4[:, 0::2, :], V(sA.ap.rearrange("p (j k) -> p j k", j=2), sA.bufs), sw4[:, 0::2, :], ALU.add)
                        P.tt("dve", sc4[:, 1::2, :], V(sB.ap.rearrange("p (j k) -> p j k", j=2), sB.bufs), sw4[:, 1::2, :], ALU.add)
                        if first and tb == 0:
                            P.ts("pool", sc4[:, :, 0:128], sc4[:, :, 0:128], NEG, ALU.add)
                        sm, sm2, sm3 = small(), small(), small()
                        mx, negm = sm[:, 0:4], sm[:, 4:8]
                        rsum, dd = sm2[:, 0:4], sm2[:, 4:8]
                        es, rinv = sm3[:, 0:4], sm3[:, 4:8]
                        sk_ = sinks[:, l * 8 + g * 4:l * 8 + g * 4 + 4]
                        P.reduce(mx, sc4, ALU.max)
                        P.tt("dve", mx, mx, sk_, ALU.max)
                        P.ts("dve", negm, mx, -1.0, ALU.mult)
                        pn = ppool()
                        for j in range(4):
                            P.act(pn[:, j * 256:(j + 1) * 256], sc[:, j * 256:(j + 1) * 256], AF.Exp,
                                  bias=negm[:, j:j + 1], accum=rsum[:, j:j + 1])
                        P.tt("dve", dd, sk_, mx, ALU.subtract)
                        P.act(es, dd, AF.Exp)
                        P.tt("dve", es, es, rsum, ALU.add)
                        P.recip(rinv, es)
                        pn4 = V(pn.ap.rearrange("p (j k) -> p j k", j=4), pn.bufs)
                        P.tt("dve", pn4, pn4, V(rinv.ap.unsqueeze(2).broadcast_to([128, 4, 256]), rinv.bufs), ALU.mult)
                        sw[i] = pn

                    def SB(i):
                        tb, g = i // 2, i % 2
                        pn = sw.pop(i)
                        posw = psV[4 + (tb % 2)]
                        for j in range(4):
                            for kb in range(2):
                                P.tr(PB[:, (kb * 4 + j) * 128:(kb * 4 + j + 1) * 128],
                                     pn[:, j * 256 + kb * 128: j * 256 + (kb + 1) * 128], identb)
                        ptt = ptpool()
                        P.copy("act", ptt, PB)
                        for j in range(4):
                            h = 4 * g + j
                            ch, po_ = h // 2, (h % 2) * 64
                            for kb in range(2):
                                P.mm(posw[po_:po_ + 64, ch * 128:(ch + 1) * 128], Vr[l][tb + kb][:, g * 64:(g + 1) * 64],
                                     ptt[:, (kb * 4 + j) * 128:(kb * 4 + j + 1) * 128], start=(kb == 0), stop=(kb == 1))
                        if g == 1:
                            dst = V(ar_t[:, 8:12, tb * 128:(tb + 1) * 128], [oswT[c4].bufs[0] for c4 in range(4)])
                            P.copy("act", dst, V(posw.ap.rearrange("p (c n) -> p c n", c=4), posw.bufs))

                    SA(0)
                    for i in range(8):
                        if i + 1 < 8:
                            SA(i + 1)
                        SB(i)
                    for g in range(2):
                        P.copy("pool", kT[l][g][:, 0:128], kT[l][g][:, 512:640])
                    P.copy("pool", Vr[l][0], Vr[l][4])
                    if t == 0 and l == 0:
                        dump("go0", V(ar_t[:, 0, :], goT[0].bufs))
                        dump("cv0", V(ar_t[:, 4, :], cvT[0].bufs))
                        dump("osw0", V(ar_t[:, 8, :], oswT[0].bufs))


                    bo_names = ["w_gla_o", "w_conv_o", "w_swa_o"]
                    brs = [goT, cvT, oswT]
                    for b in range(3):
                        for mgp in range(2):
                            wg = wload("w_in", l, 0, 8, 4368 + b * 1024 + mgp * 512, 512)
                            for m4 in range(4):
                                pgt = proj(wg, m4 * 128, 128, ring=gatering)
                                e = fpool()
                                P.act(e[:, 0:NT], pgt, AF.Exp, scale=-1.0)
                                P.act(e[:, 0:NT], e[:, 0:NT], AF.Ln, bias=1.0)
                                P.act(sigb[b * 8 + mgp * 4 + m4], e[:, 0:NT], AF.Exp, scale=-1.0)
                    for mgp in range(2):
                        wbo = [wload(bo_names[b], l, 0, 4, mgp * 512, 512) for b in range(3)]
                        for m4 in range(4):
                            m = mgp * 4 + m4
                            terms = []
                            for b in range(3):
                                py = proj(wbo[b], m4 * 128, 128, kcs=4, rhs_list=brs[b])
                                tm = fpool()
                                P.tt("dve", tm[:, 0:NT], py, sigb[b * 8 + m], ALU.mult)
                                terms.append(tm)
                            P.tt("pool", terms[0][:, 0:NT], terms[0][:, 0:NT], terms[1][:, 0:NT], ALU.add)
                            P.tt("pool", mgT[m], terms[0][:, 0:NT], terms[2][:, 0:NT], ALU.add)
                    chk(11)
                    for mgp in range(2):
                        wo = wload("w_o", l, 0, 8, mgp * 512, 512)
                        for m4 in range(4):
                            m = mgp * 4 + m4
                            pt = proj(wo, m4 * 128, 128, rhs_list=mgT)
                            P.tt("dve", hT[m], pt, hT[m], ALU.add)
                    if t == 0 and l == 0:
                        dump("h1", V(hT_t[:, 0, :], hT[0].bufs))

                    chk(12)
                    if l == L - 1 and t + 1 < NTILES:
                        P.dma(xin, V(x_d[tok0 + NT: tok0 + NT + 128, :], []))
                    norm_to_uT((L + l) * 8)
                    for pg in range(6):
                        npair = 4 if pg < 5 else 2
                        wa = wload("w_up", l, 0, 8, pg * 512, npair * 128)
                        wb = wload("w_up", l, 0, 8, DFF + pg * 512, npair * 128)
                        for pi in range(npair):
                            c = pg * 4 + pi
                            accs = []
                            for (wsl, cidx) in ((wa, c), (wb, c + 22)):
                                ph = proj(wsl, pi * 128, 128)
                                hb = fpool()
                                P.copy("pool", hb[:, 0:2], halo_f[l][cidx])
                                P.copy("act", hb[:, 2:2 + NT], ph)
                                P.copy("pool", halo_f[l][cidx], hb[:, NT:NT + 2])
                                acc = fpool()
                                P.act(acc[:, 0:NT], ph, AF.Copy, scale=fconvw[:, l, cidx, 2:3])
                                P.stt(acc[:, 0:NT], hb[:, 0:NT], fconvw[:, l, cidx, 0:1], acc[:, 0:NT], ALU.mult, ALU.add)
                                P.stt(acc[:, 0:NT], hb[:, 1:1 + NT], fconvw[:, l, cidx, 1:2], acc[:, 0:NT], ALU.mult, ALU.add)
                                accs.append(acc)
                            sa = fpool()
                            P.act(sa[:, 0:NT], accs[0][:, 0:NT], AF.Silu)
                            P.tt("dve", gT[c], sa[:, 0:NT], accs[1][:, 0:NT], ALU.mult)
                    chk(13)
                    for mgp in range(2):
                        banks = [pspool() for _ in range(4)]
                        for (k0, nk) in ((0, 8), (8, 8), (16, 6)):
                            wd = wload("w_down", l, k0, nk, mgp * 512, 512)
                            for m4 in range(4):
                                for kk in range(nk):
                                    k = k0 + kk
                                    P.mm(banks[m4], wd[:, kk, m4 * 128:(m4 + 1) * 128], gT[k], start=(k == 0), stop=(k == 21))
                        for m4 in range(4):
                            m = mgp * 4 + m4
                            P.tt("dve", hT[m], banks[m4], hT[m], ALU.add)
                    if t == 0 and l == 0:
                        dump("h2", V(hT_t[:, 0, :], hT[0].bufs))

                chk(14)
                if final_norm:
                    r = rms_stats(hT, 1.0 / D)
                ofm = []
                for c in range(8):
                    o = fpool()
                    if final_norm:
                        P.stt(o[:, 0:NT], hT[c], gv[:, 2 * L * 8 + c:2 * L * 8 + c + 1], r[:, 0:NT], ALU.mult, ALU.mult)
                    else:
                        P.copy("pool", o[:, 0:NT], hT[c])
                    ofm.append(o)
                for tb in range(4):
                    xo = scpool()
                    for half in range(2):
                        pt = pspool()
                        for cc in range(4):
                            c = half * 4 + cc
                            P.tr(pt[:, cc * 128:(cc + 1) * 128], ofm[c][:, tb * 128:(tb + 1) * 128], identf)
                        P.copy("act" if half == 0 else "dve", xo[:, half * 512:(half + 1) * 512], pt)
                    P.dma(V(out_d[tok0 + tb * 128: tok0 + (tb + 1) * 128, :], [Buf("out%d_%d" % (t, tb))]), xo, final=True)

        try:
            _body()
        except _Stop:
            pass
        P.emit()
        nc_marks = P.marks
        nc_counts = {e: len(v) for e, v in P.streams.items()}
    _MARKS[id(nc)] = (nc_marks, nc_counts)
    return nc


_WNAMES = ["w_in", "w_gla_o", "w_conv_o", "w_swa_o", "w_o", "w_up", "w_down"]


def make_in_maps(x, params, L):
    consts = _consts()
    lay = _layout_params(L, params["g_mix"], params["g_ffn"], params["g_final"], params["gla_norm_g"], params["conv_w"],
                         params["ffn_conv_w"], params["swa_sinks"], params["gla_w_alpha"], params["gla_b_alpha"])
    shared = {}
    shared.update(consts)
    shared.update(lay)
    for k in _WNAMES:
        shared[k] = np.ascontiguousarray(params[k][:L], dtype=np.float32)
    maps = []
    for b in range(x.shape[0]):
        m = dict(shared)
        m["x"] = np.ascontiguousarray(x[b], dtype=np.float32)
        maps.append(m)
    return maps


_NC_CACHE = {}


def kernel(x, g_mix, w_in, gla_w_alpha, gla_b_alpha, gla_norm_g, conv_w, swa_sinks, w_gla_o, w_conv_o, w_swa_o, w_o,
           g_ffn, w_up, ffn_conv_w, w_down, g_final):
    x = np.asarray(x)
    B, S, _ = x.shape
    L = int(np.asarray(g_mix).shape[0])
    params = dict(g_mix=g_mix, w_in=w_in, gla_w_alpha=gla_w_alpha, gla_b_alpha=gla_b_alpha, gla_norm_g=gla_norm_g,
                  conv_w=conv_w, swa_sinks=swa_sinks, w_gla_o=w_gla_o, w_conv_o=w_conv_o, w_swa_o=w_swa_o, w_o=w_o,
                  g_ffn=g_ffn, w_up=w_up, ffn_conv_w=ffn_conv_w, w_down=w_down, g_final=g_final)
    params = {k: np.asarray(v, dtype=np.float32) for k, v in params.items()}
    key = (S, L)
    if key not in _NC_CACHE:
        _NC_CACHE[key] = build_nc(S, L)
    nc = _NC_CACHE[key]
    in_maps = make_in_maps(x, params, L)
    res = run_bass_kernel_spmd(nc, in_maps, core_ids=list(range(B)))
    out = np.stack([np.asarray(r["out"]) for r in res.results], axis=0)
    return out.astype(np.float32)
```

```python
from contextlib import ExitStack
import numpy as np
import ml_dtypes
import concourse.bass as bass
import concourse.mybir as mybir
from concourse.bass_utils import run_bass_kernel_spmd

F32 = mybir.dt.float32
BF16 = mybir.dt.bfloat16
AF = mybir.ActivationFunctionType
ALU = mybir.AluOpType
AX = mybir.AxisListType

ENGS = ("pe", "act", "dve", "pool", "sp")
SAME_ENGINE_SYNC = {"pe": False, "act": True, "dve": True, "pool": True, "sp": True}

D = 1024
DIN = 7440
DFF = 2816
NT = 512
EPS = 1e-6
NEG = -30000.0


class Buf:
    __slots__ = ("name", "last_writer", "readers", "sem", "sem_cnt")

    def __init__(self, name):
        self.name = name
        self.last_writer = None
        self.readers = []
        self.sem = None
        self.sem_cnt = 0


class V:
    __slots__ = ("ap", "bufs")

    def __init__(self, ap, bufs):
        self.ap = ap
        self.bufs = tuple(bufs)

    def __getitem__(self, k):
        return V(self.ap[k], self.bufs)


class Op:
    __slots__ = ("eng", "fn", "idx", "pidx", "preds", "succs", "waits", "dma_waits", "signal", "signo", "is_dma",
                 "sem", "sem_val", "dur", "tab", "nbytes", "npend", "ready", "finish", "tag", "start", "prio")

    def __init__(self, eng, fn, is_dma=False):
        self.eng = eng
        self.fn = fn
        self.idx = -1
        self.pidx = -1
        self.preds = []
        self.succs = []
        self.waits = []
        self.dma_waits = []
        self.signal = False
        self.signo = 0
        self.is_dma = is_dma
        self.sem = None
        self.sem_val = 0
        self.dur = 100.0
        self.tab = None
        self.nbytes = 0
        self.npend = 0
        self.ready = 0.0
        self.finish = 0.0
        self.tag = None
        self.start = 0.0
        self.prio = 0.0


_ACT_SETS = {}


def _act_set(func):
    if func in (AF.Exp, AF.Ln):
        return "explog"
    if func == AF.Silu:
        return "silu"
    if func == AF.Sigmoid:
        return "sigmoid"
    return None


def _fsize(ap):
    n = 1
    for d in ap.shape[1:]:
        n *= d
    return n


class Prog:
    REORDER = ("pe", "act", "dve")

    def __init__(self, nc, stack):
        self.nc = nc
        self.stack = stack
        self.all_ops = []
        self.streams = {e: [] for e in ENGS}
        self.esem = {}
        for e in ("pe", "act", "dve", "pool"):
            self.esem[e] = stack.enter_context(nc.semaphore("es_" + e))
        self.final_dma = []
        self.marks = []
        self.sched = True
        self.prio_mode = "prog"
        self.cur_tag = None

    def mark(self, label):
        self.marks.append((label, len(self.streams["pe"])))
        self.cur_tag = (label, len(self.marks))

    def new_sem(self, name):
        return self.stack.enter_context(self.nc.semaphore(name))

    def op(self, eng, fn, reads=(), writes=(), dma=False, sem_buf=None, nowaw=False, dur=100.0, tab=None, nbytes=0):
        X = Op(eng, fn, is_dma=dma)
        X.pidx = len(self.all_ops)
        X.dur = dur
        X.tab = tab
        X.nbytes = nbytes
        X.tag = self.cur_tag
        if dma:
            b = sem_buf
            if b.sem is None:
                b.sem = self.new_sem("ds_" + b.name)
            b.sem_cnt += 16
            X.sem = b.sem
            X.sem_val = b.sem_cnt
        deps = []
        for r in reads:
            deps.append(r.last_writer)
        for w in writes:
            if not nowaw:
                deps.append(w.last_writer)
            deps.extend(w.readers)
        seen = set()
        for Y in deps:
            if Y is None or Y is X or id(Y) in seen:
                continue
            seen.add(id(Y))
            X.preds.append(Y)
        for r in reads:
            r.readers.append(X)
        for w in writes:
            w.last_writer = X
            w.readers = []
        self.all_ops.append(X)
        self.streams[eng].append(X)
        return X

    def dma(self, out, in_, queue="sp", final=False, nowaw=False):
        nb = 1
        for d in out.ap.shape:
            nb *= d
        nb *= 2 if out.ap.dtype == BF16 else 4
        X = self.op(queue, lambda e: e.dma_start(out=out.ap, in_=in_.ap), reads=in_.bufs, writes=out.bufs,
                    dma=True, sem_buf=out.bufs[0], nowaw=nowaw, dur=60.0, nbytes=nb)
        if final:
            self.final_dma.append(X)
        return X

    def mm(self, out, lhsT, rhs, start=True, stop=True):
        n = _fsize(rhs.ap)
        passes = 4 if rhs.ap.dtype == F32 else 1
        return self.op("pe", lambda e: e.matmul(out.ap, lhsT.ap, rhs.ap, start=start, stop=stop),
                       reads=lhsT.bufs + rhs.bufs, writes=out.bufs, dur=passes * max(n, 64) / 2.4 + 12)

    def tr(self, out, in_, ident):
        return self.op("pe", lambda e: e.transpose(out.ap, in_.ap, ident.ap), reads=in_.bufs + ident.bufs, writes=out.bufs,
                       dur=110.0)

    def act(self, out, in_, func, bias=None, scale=None, accum=None):
        reads = list(in_.bufs)
        kw = {}
        if bias is not None:
            if isinstance(bias, V):
                reads += bias.bufs
                kw["bias"] = bias.ap
            else:
                kw["bias"] = bias
        if scale is not None:
            if isinstance(scale, V):
                reads += scale.bufs
                kw["scale"] = scale.ap
            else:
                kw["scale"] = scale
        writes = list(out.bufs)
        d = 180.0 + _fsize(in_.ap) / 1.2
        if accum is not None:
            writes += accum.bufs
            kw["accum_out"] = accum.ap
            d += 100
        return self.op("act", lambda e: e.activation(out=out.ap, in_=in_.ap, func=func, **kw), reads=reads, writes=writes,
                       dur=d, tab=_act_set(func))

    def copy(self, eng, out, in_):
        n = _fsize(in_.ap)
        if eng == "act":
            return self.op("act", lambda e: e.copy(out.ap, in_.ap), reads=in_.bufs, writes=out.bufs, dur=180.0 + n / 1.2)
        d = (100.0 + 1.15 * n) if eng == "dve" else (350.0 + 1.0 * n)
        return self.op(eng, lambda e: e.tensor_copy(out.ap, in_.ap), reads=in_.bufs, writes=out.bufs, dur=d)

    def tt(self, eng, out, in0, in1, op):
        n = _fsize(in0.ap)
        d = (100.0 + 1.15 * n) if eng == "dve" else (150.0 + 2.2 * n)
        return self.op(eng, lambda e: e.tensor_tensor(out=out.ap, in0=in0.ap, in1=in1.ap, op=op),
                       reads=in0.bufs + in1.bufs, writes=out.bufs, dur=d)

    def ts(self, eng, out, in0, s1, op0, s2=None, op1=None):
        reads = list(in0.bufs)
        a1 = s1
        if isinstance(s1, V):
            reads += s1.bufs
            a1 = s1.ap
        a2 = s2
        if isinstance(s2, V):
            reads += s2.bufs
            a2 = s2.ap
        n = _fsize(in0.ap)
        d = (100.0 + 1.15 * n) if eng == "dve" else (300.0 + 11.0 * n)
        if op1 is None:
            return self.op(eng, lambda e: e.tensor_scalar(out=out.ap, in0=in0.ap, scalar1=a1, scalar2=None, op0=op0),
                           reads=reads, writes=out.bufs, dur=d)
        return self.op(eng, lambda e: e.tensor_scalar(out=out.ap, in0=in0.ap, scalar1=a1, scalar2=a2, op0=op0, op1=op1),
                       reads=reads, writes=out.bufs, dur=d)

    def stt(self, out, in0, scalar, in1, op0, op1):
        reads = list(in0.bufs) + list(in1.bufs)
        sc = scalar
        if isinstance(scalar, V):
            reads += scalar.bufs
            sc = scalar.ap
        return self.op("dve", lambda e: e.scalar_tensor_tensor(out=out.ap, in0=in0.ap, scalar=sc, in1=in1.ap, op0=op0, op1=op1),
                       reads=reads, writes=out.bufs, dur=120.0 + 1.2 * _fsize(in0.ap))

    def reduce(self, out, in_, op, axis=AX.X):
        return self.op("dve", lambda e: e.tensor_reduce(out=out.ap, in_=in_.ap, axis=axis, op=op), reads=in_.bufs, writes=out.bufs,
                       dur=100.0 + 1.1 * _fsize(in_.ap))

    def recip(self, out, in_):
        return self.op("dve", lambda e: e.reciprocal(out.ap, in_.ap), reads=in_.bufs, writes=out.bufs,
                       dur=100.0 + 8.4 * _fsize(in_.ap))

    def memset(self, eng, out, val):
        return self.op(eng, lambda e: e.memset(out.ap, val), writes=out.bufs, dur=200.0)

    def schedule(self):
        import heapq
        ops = self.all_ops
        for X in ops:
            X.succs = []
        for X in ops:
            for Y in X.preds:
                Y.succs.append(X)
        fixed_prev = {}
        extra = {}
        for X in ops:
            if X.eng not in self.REORDER:
                pv = fixed_prev.get(X.eng)
                if pv is not None:
                    extra[id(X)] = pv
                    pv.succs.append(X)
                fixed_prev[X.eng] = X
        for X in ops:
            X.npend = len(X.preds) + (1 if id(X) in extra else 0)
            X.ready = 0.0
        if self.prio_mode == "bl":
            bl = {}
            for X in reversed(ops):
                m = 0.0
                for Z in X.succs:
                    v = bl[id(Z)] + 200.0
                    if v > m:
                        m = v
                bl[id(X)] = m + (X.dur if not X.is_dma else X.nbytes / 260.0 + 2000.0)
            for X in ops:
                X.prio = -bl[id(X)]
        else:
            for X in ops:
                X.prio = float(X.pidx)
        HOP = 200.0
        free_at = {e: 0.0 for e in ENGS}
        pending = {e: [] for e in ENGS}
        avail = {e: [] for e in ENGS}
        last_tab = {"act": None}
        dma_free = [0.0]
        for X in ops:
            if X.npend == 0:
                heapq.heappush(pending[X.eng], (0.0, X.pidx, X))
        order = {e: [] for e in ENGS}
        nleft = len(ops)
        while nleft:
            best = None
            for e in ENGS:
                T = free_at[e]
                pq, av = pending[e], avail[e]
                while pq and pq[0][0] <= T:
                    r, pi, X = heapq.heappop(pq)
                    heapq.heappush(av, (X.prio, pi, X))
                if av:
                    st = T
                elif pq:
                    st = pq[0][0]
                else:
                    continue
                if best is None or st < best[0]:
                    best = (st, e)
            st, e = best
            if avail[e]:
                pr, pi, X = heapq.heappop(avail[e])
            else:
                r, pi, X = heapq.heappop(pending[e])
            d = X.dur
            if e == "act" and X.tab is not None and X.tab != last_tab["act"]:
                d += 1300.0
                last_tab["act"] = X.tab
            if X.is_dma:
                free_at[e] = st + d
                t0 = max(dma_free[0], st + d)
                t1 = t0 + X.nbytes / 260.0
                dma_free[0] = t1
                X.finish = t1 + 2000.0
            else:
                X.finish = st + d
                free_at[e] = X.finish
            X.start = st
            order[e].append(X)
            nleft -= 1
            for Z in X.succs:
                Z.npend -= 1
                rt = X.finish + (HOP if Z.eng != X.eng or X.is_dma else 60.0)
                if rt > Z.ready:
                    Z.ready = rt
                if Z.npend == 0:
                    heapq.heappush(pending[Z.eng], (Z.ready, Z.pidx, Z))
        self.streams = order
        self.est_ns = max(free_at.values())

    def resolve(self):
        for e in ENGS:
            for i, X in enumerate(self.streams[e]):
                X.idx = i
        waited = {e: {f: -1 for f in ENGS} for e in ENGS}
        waited_dma = {e: {} for e in ENGS}
        for e in ENGS:
            for X in self.streams[e]:
                best = {}
                for Y in X.preds:
                    if Y.is_dma:
                        cur = waited_dma[e].get(Y.sem, 0)
                        if cur < Y.sem_val:
                            waited_dma[e][Y.sem] = Y.sem_val
                            X.dma_waits.append((Y.sem, Y.sem_val))
                        continue
                    if Y.eng == e:
                        assert Y.idx < X.idx, "same-engine order violated"
                        if not SAME_ENGINE_SYNC[e]:
                            continue
                    cur = best.get(Y.eng)
                    if cur is None or Y.idx > cur.idx:
                        best[Y.eng] = Y
                for f, Y in best.items():
                    if waited[e][f] >= Y.idx:
                        continue
                    waited[e][f] = Y.idx
                    Y.signal = True
                    X.waits.append(Y)

    def emit(self):
        nc = self.nc
        if self.sched:
            self.schedule()
        self.resolve()
        for e in ("pe", "act", "dve", "pool"):
            n = 0
            for X in self.streams[e]:
                if X.signal:
                    n += 1
                    X.signo = n
        with nc.Block() as block:
            def make(e):
                def body(eng):
                    for X in self.streams[e]:
                        for Y in X.waits:
                            eng.wait_ge(self.esem[Y.eng], Y.signo)
                        for (s, v) in X.dma_waits:
                            eng.wait_ge(s, v)
                        ins = X.fn(eng)
                        if X.is_dma:
                            ins.then_inc(X.sem, 16)
                        elif X.signal:
                            ins.then_inc(self.esem[e], 1)
                    if e == "sp":
                        for X in self.final_dma:
                            eng.wait_ge(X.sem, X.sem_val)
                return body
            block.tensor(make("pe"))
            block.scalar(make("act"))
            block.vector(make("dve"))
            block.gpsimd(make("pool"))
            block.sync(make("sp"))


class Ring:
    def __init__(self, items):
        self.items = items
        self.i = 0

    def __call__(self):
        v = self.items[self.i % len(self.items)]
        self.i += 1
        return v


def _consts():
    c = {}
    c["identf"] = np.eye(128, dtype=np.float32)
    c["identb"] = np.eye(128, dtype=np.float32).astype(ml_dtypes.bfloat16)
    c["onesf"] = np.ones((128, 128), np.float32)
    j = np.arange(128)[:, None]
    i = np.arange(128)[None, :]
    same = (j // 64) == (i // 64)
    tri = (same & (j <= i)).astype(np.float32)
    ref = (i // 64) * 64 + 31
    last = (i // 64) * 64 + 63
    t_ref = (same & (j <= ref)).astype(np.float32)
    t_last = (same & (j <= last)).astype(np.float32)
    sc = -1.0 / 16.0
    T0 = sc * tri
    T1 = sc * (tri - t_ref)
    T2 = sc * (t_last - tri)
    c["tri"] = np.ascontiguousarray(np.stack([T0, T1, T2], axis=1)).astype(np.float32)
    c["gmask"] = np.ascontiguousarray(np.tile(tri, (1, 4))).astype(np.float32)
    slopes = np.exp2(-(np.arange(1, 9, dtype=np.float64))).astype(np.float32)
    iq = np.arange(128)[:, None]
    jk = np.arange(256)[None, :]
    dist = 128 + iq - jk
    valid = (dist >= 0) & (dist < 128)
    bias = np.zeros((128, 2, 4, 256), np.float32)
    for g in range(2):
        for jh in range(4):
            h = 4 * g + jh
            bias[:, g, jh, :] = np.where(valid, -slopes[h] * dist.astype(np.float32), NEG)
    c["swab"] = np.ascontiguousarray(bias.reshape(128, 2, 1024))
    return c


def _chunkcols(v):
    n = v.shape[0] // 128
    return np.ascontiguousarray(v.reshape(n, 128).T)


def _layout_params(L, g_mix, g_ffn, g_final, gla_norm_g, conv_w, ffn_conv_w, swa_sinks, gla_w_alpha, gla_b_alpha):
    gv = np.concatenate([_chunkcols(g_mix[l]) for l in range(L)] + [_chunkcols(g_ffn[l]) for l in range(L)]
                        + [_chunkcols(g_final)], axis=1).astype(np.float32)
    glag = np.concatenate([_chunkcols(gla_norm_g[l]) for l in range(L)], axis=1).astype(np.float32)
    cw = np.stack([np.stack([_chunkcols(conv_w[l, k]) for k in range(3)], axis=2) for l in range(L)], axis=1)
    fw = np.stack([np.stack([_chunkcols(ffn_conv_w[l, k]) for k in range(3)], axis=2) for l in range(L)], axis=1)
    sinks = np.ascontiguousarray(np.broadcast_to(swa_sinks[:L].reshape(1, L * 8), (128, L * 8))).astype(np.float32)
    wal = np.concatenate([gla_w_alpha[:L], gla_b_alpha[:L, None, :]], axis=1).astype(np.float32)
    wal = np.ascontiguousarray(np.transpose(wal, (1, 0, 2)))
    return {"gv": gv, "glag": glag, "convw": np.ascontiguousarray(cw.astype(np.float32)),
            "fconvw": np.ascontiguousarray(fw.astype(np.float32)), "sinks": sinks, "walpha": wal}


class _Stop(Exception):
    pass


_MARKS = {}


def build_nc(S, L, final_norm=True, dbg=None, stage=None):
    assert S % NT == 0
    NTILES = S // NT
    nc = bass.Bass("TRN2", target_bir_lowering=False)

    def dram(name, shape, dt, kind="ExternalInput"):
        return nc.dram_tensor(name, list(shape), dt, kind=kind).ap()

    x_d = dram("x", [S, D], F32)
    out_d = dram("out", [S, D], F32, kind="ExternalOutput")
    wspec = {"w_in": (D, DIN), "w_gla_o": (512, D), "w_conv_o": (512, D), "w_swa_o": (512, D),
             "w_o": (D, D), "w_up": (D, 2 * DFF), "w_down": (DFF, D)}
    w_d = {k: dram(k, [L, r, c], F32) for k, (r, c) in wspec.items()}
    wb_d = {k: dram(k + "_bf", [L, r, c], BF16, kind="Internal") for k, (r, c) in wspec.items()}
    gv_d = dram("gv", [128, (2 * L + 1) * 8], F32)
    glag_d = dram("glag", [128, L * 4], F32)
    convw_d = dram("convw", [128, L, 4, 3], F32)
    fconvw_d = dram("fconvw", [128, L, 44, 3], F32)
    sinks_d = dram("sinks", [128, L * 8], F32)
    walpha_d = dram("walpha", [17, L, 512], F32)
    identf_d = dram("identf", [128, 128], F32)
    identb_d = dram("identb", [128, 128], BF16)
    onesf_d = dram("onesf", [128, 128], F32)
    tri_d = dram("tri", [128, 3, 128], F32)
    gmask_d = dram("gmask", [128, 512], F32)
    swab_d = dram("swab", [128, 2, 1024], F32)
    dbg_d = {}
    if dbg:
        for name, (shape, dt) in dbg.items():
            dbg_d[name] = dram("dbg_" + name, shape, dt, kind="ExternalOutput")

    with ExitStack() as st:
        P = Prog(nc, st)

        def chk(n):
            P.mark(n)
            if stage is not None and stage == n:
                raise _Stop()

        def sb(name, shape, dt):
            return st.enter_context(nc.sbuf_tensor("s_" + name, list(shape), dt))

        def ps(name, shape, dt):
            return st.enter_context(nc.psum_tensor(name, list(shape), dt))

        def VT(name, shape, dt):
            t = sb(name, shape, dt)
            return V(t[:], [Buf(name)])

        def chunked(name, n, cols, dt):
            t = sb(name, [128, n, cols], dt)
            return t, [V(t[:, i, :], [Buf("%s%d" % (name, i))]) for i in range(n)]

        hT_t, hT = chunked("hT", 8, NT, F32)
        uT_t, uT = chunked("uT", 8, NT, BF16)
        ar_t, ar = chunked("arena", 22, NT, BF16)
        goT, cvT, oswT, mgT, gT = ar[0:4], ar[4:8], ar[8:12], ar[12:20], ar
        qs_t, qs = chunked("qs", 4, NT, BF16)
        sg_t, sigb = chunked("sigb", 24, NT, BF16)
        vt_t, vt = chunked("vt", 4, NT, BF16)
        kT = [[VT("kT%d_%d" % (l, g), [128, 640], BF16) for g in range(2)] for l in range(L)]
        Vr = []
        for l in range(L):
            t = sb("Vr%d" % l, [128, 5, 128], BF16)
            Vr.append([V(t[:, i, :], [Buf("Vr%d_%d" % (l, i))]) for i in range(5)])
        swab = VT("swab", [128, 2, 1024], F32)
        Sst = [[VT("Sst%d_%d" % (l, h), [128, 128], F32) for h in range(4)] for l in range(L)]
        Sbf = []
        for i in range(2):
            t = sb("Sbf%d" % i, [128, 9, 128], BF16)
            Sbf.append([V(t[:, c, :], [Buf("Sbf%d_%d" % (i, c))]) for c in range(9)])
        hc_t = sb("halo_c", [128, L, 4, 2], F32)
        halo_c = [[V(hc_t[:, l, c, :], [Buf("hc%d_%d" % (l, c))]) for c in range(4)] for l in range(L)]
        hf_t = sb("halo_f", [128, L, 44, 2], F32)
        halo_f = [[V(hf_t[:, l, c, :], [Buf("hf%d_%d" % (l, c))]) for c in range(44)] for l in range(L)]
        halo_all = [V(hc_t[:], [b for l in range(L) for c in range(4) for b in halo_c[l][c].bufs]),
                    V(hf_t[:], [b for l in range(L) for c in range(44) for b in halo_f[l][c].bufs])]
        identf = VT("identf", [128, 128], F32)
        identb = VT("identb", [128, 128], BF16)
        onesb = VT("onesb", [128, 128], BF16)
        sq_t = sb("sqpool", [128, 3, NT], BF16)
        sqpool = Ring([V(sq_t[:, i, :], [Buf("sq%d" % i)]) for i in range(3)])
        tri = VT("tri", [128, 3, 128], F32)
        gmask = VT("gmask", [128, 512], F32)
        gv = VT("gv", [128, (2 * L + 1) * 8], F32)
        glag = VT("glag", [128, L * 4], F32)
        convw = VT("convw", [128, L, 4, 3], F32)
        fconvw = VT("fconvw", [128, L, 44, 3], F32)
        sinks = VT("sinks", [128, L * 8], F32)
        walpha = VT("walpha", [17, L, 512], F32)
        gaaug = VT("gaaug", [32, 512], F32)
        epsb = VT("epsb", [128, 1], F32)
        NF = 9
        fp_t = sb("fpool", [128, NF, 514], F32)
        fpool = Ring([V(fp_t[:, i, :], [Buf("fp%d" % i)]) for i in range(NF)])
        NB = 20
        bp_t = sb("bpool", [128, NB, 512], BF16)
        bpool = Ring([V(bp_t[:, i, :], [Buf("bp%d" % i)]) for i in range(NB)])
        sc_t = sb("scpool", [128, 2, 1024], F32)
        scV = [V(sc_t[:, i, :], [Buf("sc%d" % i)]) for i in range(2)]
        scpool = Ring(scV)
        ltokb = [scV[tb // 2][:, (tb % 2) * 512:(tb % 2 + 1) * 512] for tb in range(4)]
        pp_t = sb("ppool", [128, 2, 1024], BF16)
        ppool = Ring([V(pp_t[:, i, :], [Buf("pp%d" % i)]) for i in range(2)])
        pt_t = sb("ptpool", [128, 2, 1024], BF16)
        ptpool = Ring([V(pt_t[:, i, :], [Buf("pt%d" % i)]) for i in range(2)])
        kt_t = sb("ktok", [128, 2, 4, 128], BF16)
        ktpool = Ring([(i, [V(kt_t[:, i, tb, :], [Buf("ktok%d_%d" % (i, tb))]) for tb in range(4)]) for i in range(2)])
        NS = 48
        sm_t = sb("small", [128, NS, 8], F32)
        small = Ring([V(sm_t[:, i, :], [Buf("sm%d" % i)]) for i in range(NS)])
        NW = 4
        wr_t = sb("wring", [128, NW, 8, 512], BF16)
        wring = Ring([V(wr_t[:, i, :, :], [Buf("wr%d" % i)]) for i in range(NW)])
        NPS = 7
        psb = [ps("ps%d" % i, [128, 512], F32) for i in range(NPS)]
        psV = [V(psb[i][:], [Buf("ps%d" % i)]) for i in range(NPS)]
        pspool = Ring(psV)
        mixring = Ring(psV[0:5])
        poring = Ring(psV[5:6])
        gatering = Ring(psV[6:7])
        pbt = ps("psbf", [128, 1024], BF16)
        PB = V(pbt[:], [Buf("psbf")])

        def _body():
            for dst, src in ((identf, identf_d), (identb, identb_d), (tri, tri_d), (gmask, gmask_d),
                             (swab, swab_d), (gv, gv_d), (glag, glag_d), (convw, convw_d), (fconvw, fconvw_d),
                             (sinks, sinks_d), (walpha, walpha_d)):
                P.dma(dst, V(src, []))
            P.memset("pool", epsb, EPS)
            P.memset("pool", onesb, 1.0)
            P.memset("pool", gaaug, 1.0)
            P.memset("pool", halo_all[0], 0.0)
            P.memset("pool", halo_all[1], 0.0)
            for l in range(L):
                for h in range(4):
                    P.memset("pool", Sst[l][h], 0.0)
                for g in range(2):
                    P.memset("pool", kT[l][g], 0.0)
                P.memset("pool", Vr[l][0], 0.0)

            chk(1)
            wbuf = {}
            order = ["w_in", "w_gla_o", "w_conv_o", "w_swa_o", "w_o", "w_up", "w_down"]
            for l in range(L):
                for k in order:
                    r, c = wspec[k]
                    b = Buf("%s_bf%d" % (k, l))
                    wbuf[(k, l)] = b
                    for r0 in range(0, r, 128):
                        P.dma(V(wb_d[k][l, r0:r0 + 128, :], [b]), V(w_d[k][l, r0:r0 + 128, :], []), queue="pool", nowaw=True)

            chk(2)
            def wsrc(k, l, r0, nk, c0, ncol):
                ap = wb_d[k][l, r0 * 128:(r0 + nk) * 128, c0:c0 + ncol].rearrange("(kc p) n -> p kc n", p=128)
                return V(ap, [wbuf[(k, l)]])

            def wload(k, l, r0, nk, c0, ncol):
                slot = wring()
                P.dma(slot[:, 0:nk, 0:ncol], wsrc(k, l, r0, nk, c0, ncol), final=(stage in (61, 62)))
                return slot

            def dump(name, v):
                if dbg and name in dbg_d:
                    P.dma(V(dbg_d[name], [Buf("dbg_" + name)]), v, final=True)

            def rms_stats(srcs, nparts_scale, ring=None):
                pst = (ring or pspool)()
                n = len(srcs)
                for i, s in enumerate(srcs):
                    sq = sqpool()
                    P.act(sq, s, AF.Square)
                    P.mm(pst, onesb, sq, start=(i == 0), stop=(i == n - 1))
                ln = fpool()
                P.act(ln[:, 0:NT], pst, AF.Ln, bias=epsb, scale=nparts_scale)
                r = fpool()
                P.act(r[:, 0:NT], ln[:, 0:NT], AF.Exp, scale=-0.5)
                return r

            def norm_to_uT(gcol0):
                r = rms_stats(hT, 1.0 / D)
                for c in range(8):
                    P.stt(uT[c], hT[c], gv[:, gcol0 + c:gcol0 + c + 1], r[:, 0:NT], ALU.mult, ALU.mult)

            def proj(wslot, col0, ncols_m, kcs=8, rhs_list=None, ring=None):
                rhs_list = rhs_list if rhs_list is not None else uT
                pt = (ring or pspool)()
                for kc in range(kcs):
                    P.mm(pt[0:ncols_m, :], wslot[:, kc, col0:col0 + ncols_m], rhs_list[kc], start=(kc == 0), stop=(kc == kcs - 1))
                return pt

            for t in range(NTILES):
                tok0 = t * NT
                for tb in range(4):
                    xs = scpool()
                    P.dma(xs, V(x_d[tok0 + tb * 128: tok0 + (tb + 1) * 128, :], []))
                    for half in range(2):
                        pt = pspool()
                        for cc in range(4):
                            c = half * 4 + cc
                            P.tr(pt[:, cc * 128:(cc + 1) * 128], xs[:, c * 128:(c + 1) * 128], identf)
                        dst = V(hT_t[:, half * 4:(half + 1) * 4, tb * 128:(tb + 1) * 128],
                                [hT[half * 4 + cc].bufs[0] for cc in range(4)])
                        src = V(pt.ap.rearrange("p (c n) -> p c n", c=4), pt.bufs)
                        P.copy("act" if half == 0 else "dve", dst, src)

                chk(3)
                for l in range(L):
                    first = (t == 0)
                    norm_to_uT(l * 8)
                    if t == 0 and l == 0:
                        dump("uT0", V(uT_t[:, 0, :], uT[0].bufs))

                    chk(4)
                    gring = mixring
                    wsm = wring()
                    P.dma(wsm[:, :, 0:16], wsrc("w_in", l, 0, 8, 2048, 16), nowaw=True)
                    for g in range(2):
                        for d2 in range(2):
                            P.dma(wsm[:, :, 16 + g * 128 + d2 * 64: 16 + g * 128 + (d2 + 1) * 64],
                                  wsrc("w_in", l, 0, 8, 4112 + g * 64, 64), nowaw=True)
                    P.dma(wsm[:, :, 272:400], wsrc("w_in", l, 0, 8, 4240, 128), nowaw=True)
                    pga = gring()
                    for kc in range(8):
                        P.mm(pga[0:16, :], wsm[:, kc, 0:16], uT[kc], start=(kc == 0), stop=(kc == 7))
                    P.copy("act", gaaug[0:16, :], pga[0:16, :])
                    ltok = []
                    for tb in range(4):
                        px = gring()
                        P.mm(px, gaaug[0:17, tb * 128:(tb + 1) * 128], walpha[0:17, l, :])
                        e = ltokb[tb]
                        P.act(e, px, AF.Exp, scale=-1.0)
                        ltok.append(e)
                    for tb in range(4):
                        P.act(ltok[tb], ltok[tb], AF.Ln, bias=1.0)
                    chk(5)
                    for g in range(2):
                        pk = gring()
                        for kc in range(8):
                            P.mm(pk, wsm[:, kc, 16 + g * 128:16 + (g + 1) * 128], uT[kc], start=(kc == 0), stop=(kc == 7))
                        P.copy("act", kT[l][g][:, 128:640], pk)
                    for tb in range(4):
                        pv = gring()
                        for kc in range(8):
                            P.mm(pv[:, 0:128], uT[kc][:, tb * 128:(tb + 1) * 128], wsm[:, kc, 272:400], start=(kc == 0), stop=(kc == 7))
                        P.copy("act", Vr[l][tb + 1], pv[:, 0:128])
                    w_sq = wload("w_in", l, 0, 8, 3600, 512)
                    for c4 in range(4):
                        pq = proj(w_sq, c4 * 128, 128, ring=gring)
                        P.act(qs[c4], pq, AF.Copy, scale=0.125)
                    chk(8)
                    w_cx = wload("w_in", l, 0, 8, 2064, 512)
                    w_cb = wload("w_in", l, 0, 8, 2576, 512)
                    w_cc = wload("w_in", l, 0, 8, 3088, 512)
                    for c4 in range(4):
                        pcx = proj(w_cx, c4 * 128, 128, ring=gring)
                        cxs = fpool()
                        P.copy("act", cxs[:, 0:NT], pcx)
                        pcc = proj(w_cc, c4 * 128, 128, ring=gring)
                        pbuf = fpool()
                        P.copy("pool", pbuf[:, 0:2], halo_c[l][c4])
                        P.tt("dve", pbuf[:, 2:2 + NT], pcc, cxs[:, 0:NT], ALU.mult)
                        P.copy("pool", halo_c[l][c4], pbuf[:, NT:NT + 2])
                        acc = fpool()
                        P.act(acc[:, 0:NT], pbuf[:, 0:NT], AF.Copy, scale=convw[:, l, c4, 0:1])
                        P.stt(acc[:, 0:NT], pbuf[:, 1:1 + NT], convw[:, l, c4, 1:2], acc[:, 0:NT], ALU.mult, ALU.add)
                        P.stt(acc[:, 0:NT], pbuf[:, 2:2 + NT], convw[:, l, c4, 2:3], acc[:, 0:NT], ALU.mult, ALU.add)
                        pcb = proj(w_cb, c4 * 128, 128, ring=gring)
                        P.tt("dve", cvT[c4], pcb, acc[:, 0:NT], ALU.mult)
                    chk(6)
                    w_gv = wload("w_in", l, 0, 8, 1024, 512)
                    for tb in range(4):
                        pv = gring()
                        for kc in range(8):
                            P.mm(pv, uT[kc][:, tb * 128:(tb + 1) * 128], w_gv[:, kc, :], start=(kc == 0), stop=(kc == 7))
                        P.copy("act", vt[tb], pv)
                    w_gq = wload("w_in", l, 0, 8, 0, 512)
                    w_gk = wload("w_in", l, 0, 8, 512, 512)
                    HS = [slice(hh * 128, (hh + 1) * 128) for hh in range(4)]
                    gl = [dict() for _ in range(4)]

                    def G1(hh):
                        hs = HS[hh]
                        d = gl[hh]
                        pA, pB, pb = gring(), gring(), gring()
                        for tb in range(4):
                            ts_ = slice(tb * 128, (tb + 1) * 128)
                            P.mm(pA[:, ts_], ltok[tb][:, hs], tri[:, 1, :])
                            P.mm(pB[:, ts_], ltok[tb][:, hs], tri[:, 2, :])
                            P.mm(pb[:, ts_], ltok[tb][:, hs], tri[:, 0, :])
                        E1, E2, E3, E4 = fpool(), fpool(), fpool(), fpool()
                        P.act(E1[:, 0:NT], pA, AF.Exp)
                        P.act(E2[:, 0:NT], pA, AF.Exp, scale=-1.0)
                        P.act(E3[:, 0:NT], pB, AF.Exp)
                        P.act(E4[:, 0:NT], pb, AF.Exp)
                        pq = proj(w_gq, hh * 128, 128, ring=gring)
                        d["qa"], d["qb"] = bpool(), bpool()
                        P.stt(d["qa"], pq, 128.0 ** -0.5, E1[:, 0:NT], ALU.mult, ALU.mult)
                        P.stt(d["qb"], pq, 128.0 ** -0.5, E4[:, 0:NT], ALU.mult, ALU.mult)
                        pk = proj(w_gk, hh * 128, 128, ring=gring)
                        d["ka"], d["kb"] = bpool(), bpool()
                        P.tt("dve", d["ka"], pk, E2[:, 0:NT], ALU.mult)
                        P.tt("dve", d["kb"], pk, E3[:, 0:NT], ALU.mult)
                        d["dec"] = small()
                        P.copy("pool", d["dec"], V(E4.ap[:, 0:NT].rearrange("p (c k) -> p c k", c=8)[:, :, 63], E4.bufs))
                        if t == 0 and l == 0 and hh == 0:
                            dump("E4", E4[:, 0:NT])

                    def G2(hh):
                        hs = HS[hh]
                        d = gl[hh]
                        ktok = ktpool()
                        for tb in range(4):
                            P.tr(PB[:, tb * 128:(tb + 1) * 128], d["kb"][:, tb * 128:(tb + 1) * 128], identb)
                        P.copy("act", V(kt_t[:, ktok[0], :, :], [b for tb in range(4) for b in ktok[1][tb].bufs]),
                               V(PB.ap[:, 0:512].rearrange("p (a b) -> p a b", a=4), PB.bufs))
                        ktok = ktok[1]
                        pat = gring()
                        for tb in range(4):
                            ts_ = slice(tb * 128, (tb + 1) * 128)
                            P.mm(pat[:, ts_], d["ka"][:, ts_], d["qa"][:, ts_])
                        d["am"] = bpool()
                        P.tt("dve", d["am"], pat, gmask, ALU.mult)
                        pkv = [gring(), gring()]
                        for c in range(8):
                            tb, r0 = c // 2, (c % 2) * 64
                            P.mm(pkv[c % 2][:, tb * 128:(tb + 1) * 128], ktok[tb][r0:r0 + 64, :], vt[tb][r0:r0 + 64, hs])
                        d["pkv"] = pkv

                    def G3(hh):
                        d = gl[hh]
                        sb_ = Sbf[hh % 2]
                        pkv = d["pkv"]
                        P.copy("dve", sb_[0], Sst[l][hh])
                        for c in range(8):
                            tb = c // 2
                            P.stt(Sst[l][hh], Sst[l][hh], d["dec"][:, c:c + 1], pkv[c % 2][:, tb * 128:(tb + 1) * 128],
                                  ALU.mult, ALU.add)
                            P.copy("dve", sb_[c + 1], Sst[l][hh])

                    def G4(hh):
                        hs = HS[hh]
                        d = gl[hh]
                        sb_ = Sbf[hh % 2]
                        po = poring()
                        for tb in range(4):
                            ts_ = slice(tb * 128, (tb + 1) * 128)
                            P.mm(po[:, ts_], vt[tb][:, hs], d["am"][:, ts_], start=True, stop=False)
                            for c2 in range(2):
                                c = tb * 2 + c2
                                cs = slice(c * 64, (c + 1) * 64)
                                P.mm(po[:, cs], sb_[c], d["qb"][:, cs], start=False, stop=(c2 == 1))
                        d["po"] = po

                    def G5(hh, w_gr):
                        d = gl[hh]
                        po = d["po"]
                        pg = proj(w_gr, hh * 128, 128, ring=gring)
                        sg = fpool()
                        P.act(sg[:, 0:NT], pg, AF.Exp, scale=-1.0)
                        P.act(sg[:, 0:NT], sg[:, 0:NT], AF.Ln, bias=1.0)
                        P.act(sg[:, 0:NT], sg[:, 0:NT], AF.Exp, scale=-1.0)
                        P.stt(sg[:, 0:NT], pg, 1.0, sg[:, 0:NT], ALU.mult, ALU.mult)
                        r = rms_stats([po], 1.0 / 128, ring=gring)
                        tmp = fpool()
                        P.stt(tmp[:, 0:NT], po, glag[:, l * 4 + hh:l * 4 + hh + 1], r[:, 0:NT], ALU.mult, ALU.mult)
                        P.tt("pool", goT[hh], tmp[:, 0:NT], sg[:, 0:NT], ALU.mult)
                        if t == 0 and l == 0 and hh == 0:
                            dump("tmp_o", tmp[:, 0:NT])

                    for hh in range(4):
                        G1(hh)
                    chk(7)
                    w_gr = wload("w_in", l, 0, 8, 1536, 512)
                    G2(0); G3(0)
                    G2(1); G3(1)
                    G4(0); G5(0, w_gr)
                    G2(2); G3(2)
                    G4(1); G5(1, w_gr)
                    G2(3); G3(3)
                    G4(2); G5(2, w_gr)
                    G4(3); G5(3, w_gr)
                    chk(9)
                    sw = {}

                    def SA(i):
                        tb, g = i // 2, i % 2
                        sA, sB = psV[(i % 2) * 2], psV[(i % 2) * 2 + 1]
                        for j in range(4):
                            h = 4 * g + j
                            ch, po_ = h // 2, (h % 2) * 64
                            bank = sA if j % 2 == 0 else sB
                            P.mm(bank[:, (j // 2) * 256:(j // 2 + 1) * 256], qs[ch][po_:po_ + 64, tb * 128:(tb + 1) * 128],
                                 kT[l][g][po_:po_ + 64, tb * 128:tb * 128 + 256])
                        sc = scpool()
                        sc4 = V(sc.ap.rearrange("p (j k) -> p j k", j=4), sc.bufs)
                        sw4 = V(swab.ap[:, g, :].rearrange("p (j k) -> p j k", j=4), swab.bufs)
                        P.tt("dve", sc4[:, 0::2, :], V(sA.ap.rearrange("p (j k) -> p j k", j=2), sA.bufs), sw4[:, 0::2, :], ALU.add)
                        P.tt("dve", sc4[:, 1::2, :], V(sB.ap.rearrange("p (j k) -> p j k", j=2), sB.bufs), sw4[:, 1::2, :], ALU.add)
                        if first and tb == 0:
                            P.ts("pool", sc4[:, :, 0:128], sc4[:, :, 0:128], NEG, ALU.add)
                        sm, sm2, sm3 = small(), small(), small()
                        mx, negm = sm[:, 0:4], sm[:, 4:8]
                        rsum, dd = sm2[:, 0:4], sm2[:, 4:8]
                        es, rinv = sm3[:, 0:4], sm3[:, 4:8]
                        sk_ = sinks[:, l * 8 + g * 4:l * 8 + g * 4 + 4]
                        P.reduce(mx, sc4, ALU.max)
                        P.tt("dve", mx, mx, sk_, ALU.max)
                        P.ts("dve", negm, mx, -1.0, ALU.mult)
                        pn = ppool()
                        for j in range(4):
                            P.act(pn[:, j * 256:(j + 1) * 256], sc[:, j * 256:(j + 1) * 256], AF.Exp,
                                  bias=negm[:, j:j + 1], accum=rsum[:, j:j + 1])
                        P.tt("dve", dd, sk_, mx, ALU.subtract)
                        P.act(es, dd, AF.Exp)
                        P.tt("dve", es, es, rsum, ALU.add)
                        P.recip(rinv, es)
                        pn4 = V(pn.ap.rearrange("p (j k) -> p j k", j=4), pn.bufs)
                        P.tt("dve", pn4, pn4, V(rinv.ap.unsqueeze(2).broadcast_to([128, 4, 256]), rinv.bufs), ALU.mult)
                        sw[i] = pn

                    def SB(i):
                        tb, g = i // 2, i % 2
                        pn = sw.pop(i)
                        posw = psV[4 + (tb % 2)]
                        for j in range(4):
                            for kb in range(2):
                                P.tr(PB[:, (kb * 4 + j) * 128:(kb * 4 + j + 1) * 128],
                                     pn[:, j * 256 + kb * 128: j * 256 + (kb + 1) * 128], identb)
                        ptt = ptpool()
                        P.copy("act", ptt, PB)
                        for j in range(4):
                            h = 4 * g + j
                            ch, po_ = h // 2, (h % 2) * 64
                            for kb in range(2):
                                P.mm(posw[po_:po_ + 64, ch * 128:(ch + 1) * 128], Vr[l][tb + kb][:, g * 64:(g + 1) * 64],
                                     ptt[:, (kb * 4 + j) * 128:(kb * 4 + j + 1) * 128], start=(kb == 0), stop=(kb == 1))
                        if g == 1:
                            dst = V(ar_t[:, 8:12, tb * 128:(tb + 1) * 128], [oswT[c4].bufs[0] for c4 in range(4)])
                            P.copy("act", dst, V(posw.ap.rearrange("p (c n) -> p c n", c=4), posw.bufs))

                    SA(0)
                    for i in range(8):
                        if i + 1 < 8:
                            SA(i + 1)
                        SB(i)
                    for g in range(2):
                        P.copy("pool", kT[l][g][:, 0:128], kT[l][g][:, 512:640])
                    P.copy("pool", Vr[l][0], Vr[l][4])
                    if t == 0 and l == 0:
                        dump("go0", V(ar_t[:, 0, :], goT[0].bufs))
                        dump("cv0", V(ar_t[:, 4, :], cvT[0].bufs))
                        dump("osw0", V(ar_t[:, 8, :], oswT[0].bufs))


                    bo_names = ["w_gla_o", "w_conv_o", "w_swa_o"]
                    brs = [goT, cvT, oswT]
                    for b in range(3):
                        for mgp in range(2):
                            wg = wload("w_in", l, 0, 8, 4368 + b * 1024 + mgp * 512, 512)
                            for m4 in range(4):
                                pgt = proj(wg, m4 * 128, 128, ring=gatering)
                                e = fpool()
                                P.act(e[:, 0:NT], pgt, AF.Exp, scale=-1.0)
                                P.act(e[:, 0:NT], e[:, 0:NT], AF.Ln, bias=1.0)
                                P.act(sigb[b * 8 + mgp * 4 + m4], e[:, 0:NT], AF.Exp, scale=-1.0)
                    for mgp in range(2):
                        wbo = [wload(bo_names[b], l, 0, 4, mgp * 512, 512) for b in range(3)]
                        for m4 in range(4):
                            m = mgp * 4 + m4
                            terms = []
                            for b in range(3):
                                py = proj(wbo[b], m4 * 128, 128, kcs=4, rhs_list=brs[b])
                                tm = fpool()
                                P.tt("dve", tm[:, 0:NT], py, sigb[b * 8 + m], ALU.mult)
                                terms.append(tm)
                            P.tt("pool", terms[0][:, 0:NT], terms[0][:, 0:NT], terms[1][:, 0:NT], ALU.add)
                            P.tt("pool", mgT[m], terms[0][:, 0:NT], terms[2][:, 0:NT], ALU.add)
                    chk(11)
                    for mgp in range(2):
                        wo = wload("w_o", l, 0, 8, mgp * 512, 512)
                        for m4 in range(4):
                            m = mgp * 4 + m4
                            pt = proj(wo, m4 * 128, 128, rhs_list=mgT)
                            P.tt("dve", hT[m], pt, hT[m], ALU.add)
                    if t == 0 and l == 0:
                        dump("h1", V(hT_t[:, 0, :], hT[0].bufs))

                    chk(12)
                    norm_to_uT((L + l) * 8)
                    for pg in range(6):
                        npair = 4 if pg < 5 else 2
                        wa = wload("w_up", l, 0, 8, pg * 512, npair * 128)
                        wb = wload("w_up", l, 0, 8, DFF + pg * 512, npair * 128)
                        for pi in range(npair):
                            c = pg * 4 + pi
                            accs = []
                            for (wsl, cidx) in ((wa, c), (wb, c + 22)):
                                ph = proj(wsl, pi * 128, 128)
                                hb = fpool()
                                P.copy("pool", hb[:, 0:2], halo_f[l][cidx])
                                P.copy("act", hb[:, 2:2 + NT], ph)
                                P.copy("pool", halo_f[l][cidx], hb[:, NT:NT + 2])
                                acc = fpool()
                                P.act(acc[:, 0:NT], ph, AF.Copy, scale=fconvw[:, l, cidx, 2:3])
                                P.stt(acc[:, 0:NT], hb[:, 0:NT], fconvw[:, l, cidx, 0:1], acc[:, 0:NT], ALU.mult, ALU.add)
                                P.stt(acc[:, 0:NT], hb[:, 1:1 + NT], fconvw[:, l, cidx, 1:2], acc[:, 0:NT], ALU.mult, ALU.add)
                                accs.append(acc)
                            sa = fpool()
                            P.act(sa[:, 0:NT], accs[0][:, 0:NT], AF.Silu)
                            P.tt("dve", gT[c], sa[:, 0:NT], accs[1][:, 0:NT], ALU.mult)
                    chk(13)
                    for mgp in range(2):
                        banks = [pspool() for _ in range(4)]
                        for (k0, nk) in ((0, 8), (8, 8), (16, 6)):
                            wd = wload("w_down", l, k0, nk, mgp * 512, 512)
                            for m4 in range(4):
                                for kk in range(nk):
                                    k = k0 + kk
                                    P.mm(banks[m4], wd[:, kk, m4 * 128:(m4 + 1) * 128], gT[k], start=(k == 0), stop=(k == 21))
                        for m4 in range(4):
                            m = mgp * 4 + m4
                            P.tt("dve", hT[m], banks[m4], hT[m], ALU.add)
                    if t == 0 and l == 0:
                        dump("h2", V(hT_t[:, 0, :], hT[0].bufs))

                chk(14)
                if final_norm:
                    r = rms_stats(hT, 1.0 / D)
                ofm = []
                for c in range(8):
                    o = fpool()
                    if final_norm:
                        P.stt(o[:, 0:NT], hT[c], gv[:, 2 * L * 8 + c:2 * L * 8 + c + 1], r[:, 0:NT], ALU.mult, ALU.mult)
                    else:
                        P.copy("pool", o[:, 0:NT], hT[c])
                    ofm.append(o)
                for tb in range(4):
                    xo = scpool()
                    for half in range(2):
                        pt = pspool()
                        for cc in range(4):
                            c = half * 4 + cc
                            P.tr(pt[:, cc * 128:(cc + 1) * 128], ofm[c][:, tb * 128:(tb + 1) * 128], identf)
                        P.copy("act" if half == 0 else "dve", xo[:, half * 512:(half + 1) * 512], pt)
                    P.dma(V(out_d[tok0 + tb * 128: tok0 + (tb + 1) * 128, :], [Buf("out%d_%d" % (t, tb))]), xo, final=True)

        try:
            _body()
        except _Stop:
            pass
        P.emit()
        nc_marks = P.marks
        nc_counts = {e: len(v) for e, v in P.streams.items()}
    _MARKS[id(nc)] = (nc_marks, nc_counts)
    return nc


_WNAMES = ["w_in", "w_gla_o", "w_conv_o", "w_swa_o", "w_o", "w_up", "w_down"]


def make_in_maps(x, params, L):
    consts = _consts()
    lay = _layout_params(L, params["g_mix"], params["g_ffn"], params["g_final"], params["gla_norm_g"], params["conv_w"],
                         params["ffn_conv_w"], params["swa_sinks"], params["gla_w_alpha"], params["gla_b_alpha"])
    shared = {}
    shared.update(consts)
    shared.update(lay)
    for k in _WNAMES:
        shared[k] = np.ascontiguousarray(params[k][:L], dtype=np.float32)
    maps = []
    for b in range(x.shape[0]):
        m = dict(shared)
        m["x"] = np.ascontiguousarray(x[b], dtype=np.float32)
        maps.append(m)
    return maps


_NC_CACHE = {}


def kernel(x, g_mix, w_in, gla_w_alpha, gla_b_alpha, gla_norm_g, conv_w, swa_sinks, w_gla_o, w_conv_o, w_swa_o, w_o,
           g_ffn, w_up, ffn_conv_w, w_down, g_final):
    x = np.asarray(x)
    B, S, _ = x.shape
    L = int(np.asarray(g_mix).shape[0])
    params = dict(g_mix=g_mix, w_in=w_in, gla_w_alpha=gla_w_alpha, gla_b_alpha=gla_b_alpha, gla_norm_g=gla_norm_g,
                  conv_w=conv_w, swa_sinks=swa_sinks, w_gla_o=w_gla_o, w_conv_o=w_conv_o, w_swa_o=w_swa_o, w_o=w_o,
                  g_ffn=g_ffn, w_up=w_up, ffn_conv_w=ffn_conv_w, w_down=w_down, g_final=g_final)
    params = {k: np.asarray(v, dtype=np.float32) for k, v in params.items()}
    key = (S, L)
    if key not in _NC_CACHE:
        _NC_CACHE[key] = build_nc(S, L)
    nc = _NC_CACHE[key]
    in_maps = make_in_maps(x, params, L)
    res = run_bass_kernel_spmd(nc, in_maps, core_ids=list(range(B)))
    out = np.stack([np.asarray(r["out"]) for r in res.results], axis=0)
    return out.astype(np.float32)
```

```python
from contextlib import ExitStack
import numpy as np
import ml_dtypes
import concourse.bass as bass
import concourse.mybir as mybir
from concourse.bass_utils import run_bass_kernel_spmd

F32 = mybir.dt.float32
BF16 = mybir.dt.bfloat16
AF = mybir.ActivationFunctionType
ALU = mybir.AluOpType
AX = mybir.AxisListType

ENGS = ("pe", "act", "dve", "pool", "sp")
SAME_ENGINE_SYNC = {"pe": False, "act": True, "dve": True, "pool": True, "sp": True}

D = 1024
DIN = 7440
DFF = 2816
NT = 512
EPS = 1e-6
NEG = -30000.0


class Buf:
    __slots__ = ("name", "last_writer", "readers", "sem", "sem_cnt")

    def __init__(self, name):
        self.name = name
        self.last_writer = None
        self.readers = []
        self.sem = None
        self.sem_cnt = 0


class V:
    __slots__ = ("ap", "bufs")

    def __init__(self, ap, bufs):
        self.ap = ap
        self.bufs = tuple(bufs)

    def __getitem__(self, k):
        return V(self.ap[k], self.bufs)


class Op:
    __slots__ = ("eng", "fn", "idx", "pidx", "preds", "succs", "waits", "dma_waits", "signal", "signo", "is_dma",
                 "sem", "sem_val", "dur", "tab", "nbytes", "npend", "ready", "finish", "tag", "start", "prio")

    def __init__(self, eng, fn, is_dma=False):
        self.eng = eng
        self.fn = fn
        self.idx = -1
        self.pidx = -1
        self.preds = []
        self.succs = []
        self.waits = []
        self.dma_waits = []
        self.signal = False
        self.signo = 0
        self.is_dma = is_dma
        self.sem = None
        self.sem_val = 0
        self.dur = 100.0
        self.tab = None
        self.nbytes = 0
        self.npend = 0
        self.ready = 0.0
        self.finish = 0.0
        self.tag = None
        self.start = 0.0
        self.prio = 0.0


_ACT_SETS = {}


def _act_set(func):
    if func in (AF.Exp, AF.Ln):
        return "explog"
    if func == AF.Silu:
        return "silu"
    if func == AF.Sigmoid:
        return "sigmoid"
    return None


def _fsize(ap):
    n = 1
    for d in ap.shape[1:]:
        n *= d
    return n


class Prog:
    REORDER = ("pe", "act", "dve")

    def __init__(self, nc, stack):
        self.nc = nc
        self.stack = stack
        self.all_ops = []
        self.streams = {e: [] for e in ENGS}
        self.esem = {}
        for e in ("pe", "act", "dve", "pool"):
            self.esem[e] = stack.enter_context(nc.semaphore("es_" + e))
        self.final_dma = []
        self.marks = []
        self.sched = True
        self.prio_mode = "prog"
        self.cur_tag = None

    def mark(self, label):
        self.marks.append((label, len(self.streams["pe"])))
        self.cur_tag = (label, len(self.marks))

    def new_sem(self, name):
        return self.stack.enter_context(self.nc.semaphore(name))

    def op(self, eng, fn, reads=(), writes=(), dma=False, sem_buf=None, nowaw=False, dur=100.0, tab=None, nbytes=0):
        X = Op(eng, fn, is_dma=dma)
        X.pidx = len(self.all_ops)
        X.dur = dur
        X.tab = tab
        X.nbytes = nbytes
        X.tag = self.cur_tag
        if dma:
            b = sem_buf
            if b.sem is None:
                b.sem = self.new_sem("ds_" + b.name)
            b.sem_cnt += 16
            X.sem = b.sem
            X.sem_val = b.sem_cnt
        deps = []
        for r in reads:
            deps.append(r.last_writer)
        for w in writes:
            if not nowaw:
                deps.append(w.last_writer)
            deps.extend(w.readers)
        seen = set()
        for Y in deps:
            if Y is None or Y is X or id(Y) in seen:
                continue
            seen.add(id(Y))
            X.preds.append(Y)
        for r in reads:
            r.readers.append(X)
        for w in writes:
            w.last_writer = X
            w.readers = []
        self.all_ops.append(X)
        self.streams[eng].append(X)
        return X

    def dma(self, out, in_, queue="sp", final=False, nowaw=False):
        nb = 1
        for d in out.ap.shape:
            nb *= d
        nb *= 2 if out.ap.dtype == BF16 else 4
        X = self.op(queue, lambda e: e.dma_start(out=out.ap, in_=in_.ap), reads=in_.bufs, writes=out.bufs,
                    dma=True, sem_buf=out.bufs[0], nowaw=nowaw, dur=60.0, nbytes=nb)
        if final:
            self.final_dma.append(X)
        return X

    def mm(self, out, lhsT, rhs, start=True, stop=True):
        n = _fsize(rhs.ap)
        passes = 4 if rhs.ap.dtype == F32 else 1
        return self.op("pe", lambda e: e.matmul(out.ap, lhsT.ap, rhs.ap, start=start, stop=stop),
                       reads=lhsT.bufs + rhs.bufs, writes=out.bufs, dur=passes * max(n, 64) / 2.4 + 12)

    def tr(self, out, in_, ident):
        return self.op("pe", lambda e: e.transpose(out.ap, in_.ap, ident.ap), reads=in_.bufs + ident.bufs, writes=out.bufs,
                       dur=110.0)

    def act(self, out, in_, func, bias=None, scale=None, accum=None):
        reads = list(in_.bufs)
        kw = {}
        if bias is not None:
            if isinstance(bias, V):
                reads += bias.bufs
                kw["bias"] = bias.ap
            else:
                kw["bias"] = bias
        if scale is not None:
            if isinstance(scale, V):
                reads += scale.bufs
                kw["scale"] = scale.ap
            else:
                kw["scale"] = scale
        writes = list(out.bufs)
        d = 180.0 + _fsize(in_.ap) / 1.2
        if accum is not None:
            writes += accum.bufs
            kw["accum_out"] = accum.ap
            d += 100
        return self.op("act", lambda e: e.activation(out=out.ap, in_=in_.ap, func=func, **kw), reads=reads, writes=writes,
                       dur=d, tab=_act_set(func))

    def copy(self, eng, out, in_):
        n = _fsize(in_.ap)
        if eng == "act":
            return self.op("act", lambda e: e.copy(out.ap, in_.ap), reads=in_.bufs, writes=out.bufs, dur=180.0 + n / 1.2)
        d = (100.0 + 1.15 * n) if eng == "dve" else (350.0 + 1.0 * n)
        return self.op(eng, lambda e: e.tensor_copy(out.ap, in_.ap), reads=in_.bufs, writes=out.bufs, dur=d)

    def tt(self, eng, out, in0, in1, op):
        n = _fsize(in0.ap)
        d = (100.0 + 1.15 * n) if eng == "dve" else (150.0 + 2.2 * n)
        return self.op(eng, lambda e: e.tensor_tensor(out=out.ap, in0=in0.ap, in1=in1.ap, op=op),
                       reads=in0.bufs + in1.bufs, writes=out.bufs, dur=d)

    def ts(self, eng, out, in0, s1, op0, s2=None, op1=None):
        reads = list(in0.bufs)
        a1 = s1
        if isinstance(s1, V):
            reads += s1.bufs
            a1 = s1.ap
        a2 = s2
        if isinstance(s2, V):
            reads += s2.bufs
            a2 = s2.ap
        n = _fsize(in0.ap)
        d = (100.0 + 1.15 * n) if eng == "dve" else (300.0 + 11.0 * n)
        if op1 is None:
            return self.op(eng, lambda e: e.tensor_scalar(out=out.ap, in0=in0.ap, scalar1=a1, scalar2=None, op0=op0),
                           reads=reads, writes=out.bufs, dur=d)
        return self.op(eng, lambda e: e.tensor_scalar(out=out.ap, in0=in0.ap, scalar1=a1, scalar2=a2, op0=op0, op1=op1),
                       reads=reads, writes=out.bufs, dur=d)

    def stt(self, out, in0, scalar, in1, op0, op1):
        reads = list(in0.bufs) + list(in1.bufs)
        sc = scalar
        if isinstance(scalar, V):
            reads += scalar.bufs
            sc = scalar.ap
        return self.op("dve", lambda e: e.scalar_tensor_tensor(out=out.ap, in0=in0.ap, scalar=sc, in1=in1.ap, op0=op0, op1=op1),
                       reads=reads, writes=out.bufs, dur=120.0 + 1.2 * _fsize(in0.ap))

    def reduce(self, out, in_, op, axis=AX.X):
        return self.op("dve", lambda e: e.tensor_reduce(out=out.ap, in_=in_.ap, axis=axis, op=op), reads=in_.bufs, writes=out.bufs,
                       dur=100.0 + 1.1 * _fsize(in_.ap))

    def recip(self, out, in_):
        return self.op("dve", lambda e: e.reciprocal(out.ap, in_.ap), reads=in_.bufs, writes=out.bufs,
                       dur=100.0 + 8.4 * _fsize(in_.ap))

    def memset(self, eng, out, val):
        return self.op(eng, lambda e: e.memset(out.ap, val), writes=out.bufs, dur=200.0)

    def schedule(self):
        import heapq
        ops = self.all_ops
        for X in ops:
            X.succs = []
        for X in ops:
            for Y in X.preds:
                Y.succs.append(X)
        fixed_prev = {}
        extra = {}
        for X in ops:
            if X.eng not in self.REORDER:
                pv = fixed_prev.get(X.eng)
                if pv is not None:
                    extra[id(X)] = pv
                    pv.succs.append(X)
                fixed_prev[X.eng] = X
        for X in ops:
            X.npend = len(X.preds) + (1 if id(X) in extra else 0)
            X.ready = 0.0
        if self.prio_mode == "bl":
            bl = {}
            for X in reversed(ops):
                m = 0.0
                for Z in X.succs:
                    v = bl[id(Z)] + 200.0
                    if v > m:
                        m = v
                bl[id(X)] = m + (X.dur if not X.is_dma else X.nbytes / 260.0 + 2000.0)
            for X in ops:
                X.prio = -bl[id(X)]
        else:
            for X in ops:
                X.prio = float(X.pidx)
        HOP = 200.0
        free_at = {e: 0.0 for e in ENGS}
        pending = {e: [] for e in ENGS}
        avail = {e: [] for e in ENGS}
        last_tab = {"act": None}
        dma_free = [0.0]
        for X in ops:
            if X.npend == 0:
                heapq.heappush(pending[X.eng], (0.0, X.pidx, X))
        order = {e: [] for e in ENGS}
        nleft = len(ops)
        while nleft:
            best = None
            for e in ENGS:
                T = free_at[e]
                pq, av = pending[e], avail[e]
                while pq and pq[0][0] <= T:
                    r, pi, X = heapq.heappop(pq)
                    heapq.heappush(av, (X.prio, pi, X))
                if av:
                    st = T
                elif pq:
                    st = pq[0][0]
                else:
                    continue
                if best is None or st < best[0]:
                    best = (st, e)
            st, e = best
            if avail[e]:
                pr, pi, X = heapq.heappop(avail[e])
            else:
                r, pi, X = heapq.heappop(pending[e])
            d = X.dur
            if e == "act" and X.tab is not None and X.tab != last_tab["act"]:
                d += 1300.0
                last_tab["act"] = X.tab
            if X.is_dma:
                free_at[e] = st + d
                t0 = max(dma_free[0], st + d)
                t1 = t0 + X.nbytes / 260.0
                dma_free[0] = t1
                X.finish = t1 + 2000.0
            else:
                X.finish = st + d
                free_at[e] = X.finish
            X.start = st
            order[e].append(X)
            nleft -= 1
            for Z in X.succs:
                Z.npend -= 1
                rt = X.finish + (HOP if Z.eng != X.eng or X.is_dma else 60.0)
                if rt > Z.ready:
                    Z.ready = rt
                if Z.npend == 0:
                    heapq.heappush(pending[Z.eng], (Z.ready, Z.pidx, Z))
        self.streams = order
        self.est_ns = max(free_at.values())

    def resolve(self):
        for e in ENGS:
            for i, X in enumerate(self.streams[e]):
                X.idx = i
        waited = {e: {f: -1 for f in ENGS} for e in ENGS}
        waited_dma = {e: {} for e in ENGS}
        for e in ENGS:
            for X in self.streams[e]:
                best = {}
                for Y in X.preds:
                    if Y.is_dma:
                        cur = waited_dma[e].get(Y.sem, 0)
                        if cur < Y.sem_val:
                            waited_dma[e][Y.sem] = Y.sem_val
                            X.dma_waits.append((Y.sem, Y.sem_val))
                        continue
                    if Y.eng == e:
                        assert Y.idx < X.idx, "same-engine order violated"
                        if not SAME_ENGINE_SYNC[e]:
                            continue
                    cur = best.get(Y.eng)
                    if cur is None or Y.idx > cur.idx:
                        best[Y.eng] = Y
                for f, Y in best.items():
                    if waited[e][f] >= Y.idx:
                        continue
                    waited[e][f] = Y.idx
                    Y.signal = True
                    X.waits.append(Y)

    def emit(self):
        nc = self.nc
        if self.sched:
            self.schedule()
        self.resolve()
        for e in ("pe", "act", "dve", "pool"):
            n = 0
            for X in self.streams[e]:
                if X.signal:
                    n += 1
                    X.signo = n
        with nc.Block() as block:
            def make(e):
                def body(eng):
                    for X in self.streams[e]:
                        for Y in X.waits:
                            eng.wait_ge(self.esem[Y.eng], Y.signo)
                        for (s, v) in X.dma_waits:
                            eng.wait_ge(s, v)
                        ins = X.fn(eng)
                        if X.is_dma:
                            ins.then_inc(X.sem, 16)
                        elif X.signal:
                            ins.then_inc(self.esem[e], 1)
                    if e == "sp":
                        for X in self.final_dma:
                            eng.wait_ge(X.sem, X.sem_val)
                return body
            block.tensor(make("pe"))
            block.scalar(make("act"))
            block.vector(make("dve"))
            block.gpsimd(make("pool"))
            block.sync(make("sp"))


class Ring:
    def __init__(self, items):
        self.items = items
        self.i = 0

    def __call__(self):
        v = self.items[self.i % len(self.items)]
        self.i += 1
        return v


def _consts():
    c = {}
    c["identf"] = np.eye(128, dtype=np.float32)
    c["identb"] = np.eye(128, dtype=np.float32).astype(ml_dtypes.bfloat16)
    c["onesf"] = np.ones((128, 128), np.float32)
    j = np.arange(128)[:, None]
    i = np.arange(128)[None, :]
    same = (j // 64) == (i // 64)
    tri = (same & (j <= i)).astype(np.float32)
    ref = (i // 64) * 64 + 31
    last = (i // 64) * 64 + 63
    t_ref = (same & (j <= ref)).astype(np.float32)
    t_last = (same & (j <= last)).astype(np.float32)
    sc = -1.0 / 16.0
    T0 = sc * tri
    T1 = sc * (tri - t_ref)
    T2 = sc * (t_last - tri)
    c["tri"] = np.ascontiguousarray(np.stack([T0, T1, T2], axis=1)).astype(np.float32)
    c["gmask"] = np.ascontiguousarray(np.tile(tri, (1, 4))).astype(np.float32)
    slopes = np.exp2(-(np.arange(1, 9, dtype=np.float64))).astype(np.float32)
    iq = np.arange(128)[:, None]
    jk = np.arange(256)[None, :]
    dist = 128 + iq - jk
    valid = (dist >= 0) & (dist < 128)
    bias = np.zeros((128, 2, 4, 256), np.float32)
    for g in range(2):
        for jh in range(4):
            h = 4 * g + jh
            bias[:, g, jh, :] = np.where(valid, -slopes[h] * dist.astype(np.float32), NEG)
    c["swab"] = np.ascontiguousarray(bias.reshape(128, 2, 1024))
    return c


def _chunkcols(v):
    n = v.shape[0] // 128
    return np.ascontiguousarray(v.reshape(n, 128).T)


def _layout_params(L, g_mix, g_ffn, g_final, gla_norm_g, conv_w, ffn_conv_w, swa_sinks, gla_w_alpha, gla_b_alpha):
    gv = np.concatenate([_chunkcols(g_mix[l]) for l in range(L)] + [_chunkcols(g_ffn[l]) for l in range(L)]
                        + [_chunkcols(g_final)], axis=1).astype(np.float32)
    glag = np.concatenate([_chunkcols(gla_norm_g[l]) for l in range(L)], axis=1).astype(np.float32)
    cw = np.stack([np.stack([_chunkcols(conv_w[l, k]) for k in range(3)], axis=2) for l in range(L)], axis=1)
    fw = np.stack([np.stack([_chunkcols(ffn_conv_w[l, k]) for k in range(3)], axis=2) for l in range(L)], axis=1)
    sinks = np.ascontiguousarray(np.broadcast_to(swa_sinks[:L].reshape(1, L * 8), (128, L * 8))).astype(np.float32)
    wal = np.concatenate([gla_w_alpha[:L], gla_b_alpha[:L, None, :]], axis=1).astype(np.float32)
    wal = np.ascontiguousarray(np.transpose(wal, (1, 0, 2)))
    return {"gv": gv, "glag": glag, "convw": np.ascontiguousarray(cw.astype(np.float32)),
            "fconvw": np.ascontiguousarray(fw.astype(np.float32)), "sinks": sinks, "walpha": wal}


class _Stop(Exception):
    pass


_MARKS = {}


def build_nc(S, L, final_norm=True, dbg=None, stage=None):
    assert S % NT == 0
    NTILES = S // NT
    nc = bass.Bass("TRN2", target_bir_lowering=False)

    def dram(name, shape, dt, kind="ExternalInput"):
        return nc.dram_tensor(name, list(shape), dt, kind=kind).ap()

    x_d = dram("x", [S, D], F32)
    out_d = dram("out", [S, D], F32, kind="ExternalOutput")
    wspec = {"w_in": (D, DIN), "w_gla_o": (512, D), "w_conv_o": (512, D), "w_swa_o": (512, D),
             "w_o": (D, D), "w_up": (D, 2 * DFF), "w_down": (DFF, D)}
    w_d = {k: dram(k, [L, r, c], F32) for k, (r, c) in wspec.items()}
    wb_d = {k: dram(k + "_bf", [L, r, c], BF16, kind="Internal") for k, (r, c) in wspec.items()}
    gv_d = dram("gv", [128, (2 * L + 1) * 8], F32)
    glag_d = dram("glag", [128, L * 4], F32)
    convw_d = dram("convw", [128, L, 4, 3], F32)
    fconvw_d = dram("fconvw", [128, L, 44, 3], F32)
    sinks_d = dram("sinks", [128, L * 8], F32)
    walpha_d = dram("walpha", [17, L, 512], F32)
    identf_d = dram("identf", [128, 128], F32)
    identb_d = dram("identb", [128, 128], BF16)
    onesf_d = dram("onesf", [128, 128], F32)
    tri_d = dram("tri", [128, 3, 128], F32)
    gmask_d = dram("gmask", [128, 512], F32)
    swab_d = dram("swab", [128, 2, 1024], F32)
    dbg_d = {}
    if dbg:
        for name, (shape, dt) in dbg.items():
            dbg_d[name] = dram("dbg_" + name, shape, dt, kind="ExternalOutput")

    with ExitStack() as st:
        P = Prog(nc, st)

        def chk(n):
            P.mark(n)
            if stage is not None and stage == n:
                raise _Stop()

        def sb(name, shape, dt):
            return st.enter_context(nc.sbuf_tensor("s_" + name, list(shape), dt))

        def ps(name, shape, dt):
            return st.enter_context(nc.psum_tensor(name, list(shape), dt))

        def VT(name, shape, dt):
            t = sb(name, shape, dt)
            return V(t[:], [Buf(name)])

        def chunked(name, n, cols, dt):
            t = sb(name, [128, n, cols], dt)
            return t, [V(t[:, i, :], [Buf("%s%d" % (name, i))]) for i in range(n)]

        hT_t, hT = chunked("hT", 8, NT, F32)
        uT_t, uT = chunked("uT", 8, NT, BF16)
        ar_t, ar = chunked("arena", 22, NT, BF16)
        goT, cvT, oswT, mgT, gT = ar[0:4], ar[4:8], ar[8:12], ar[12:20], ar
        qs_t, qs = chunked("qs", 4, NT, BF16)
        sg_t, sigb = chunked("sigb", 24, NT, BF16)
        vt_t, vt = chunked("vt", 4, NT, BF16)
        kT = [[VT("kT%d_%d" % (l, g), [128, 640], BF16) for g in range(2)] for l in range(L)]
        Vr = []
        for l in range(L):
            t = sb("Vr%d" % l, [128, 5, 128], BF16)
            Vr.append([V(t[:, i, :], [Buf("Vr%d_%d" % (l, i))]) for i in range(5)])
        swab = VT("swab", [128, 2, 1024], F32)
        Sst = [[VT("Sst%d_%d" % (l, h), [128, 128], F32) for h in range(4)] for l in range(L)]
        Sbf = []
        for i in range(2):
            t = sb("Sbf%d" % i, [128, 9, 128], BF16)
            Sbf.append([V(t[:, c, :], [Buf("Sbf%d_%d" % (i, c))]) for c in range(9)])
        hc_t = sb("halo_c", [128, L, 4, 2], F32)
        halo_c = [[V(hc_t[:, l, c, :], [Buf("hc%d_%d" % (l, c))]) for c in range(4)] for l in range(L)]
        hf_t = sb("halo_f", [128, L, 44, 2], F32)
        halo_f = [[V(hf_t[:, l, c, :], [Buf("hf%d_%d" % (l, c))]) for c in range(44)] for l in range(L)]
        halo_all = [V(hc_t[:], [b for l in range(L) for c in range(4) for b in halo_c[l][c].bufs]),
                    V(hf_t[:], [b for l in range(L) for c in range(44) for b in halo_f[l][c].bufs])]
        identf = VT("identf", [128, 128], F32)
        identb = VT("identb", [128, 128], BF16)
        onesb = VT("onesb", [128, 128], BF16)
        sq_t = sb("sqpool", [128, 3, NT], BF16)
        sqpool = Ring([V(sq_t[:, i, :], [Buf("sq%d" % i)]) for i in range(3)])
        tri = VT("tri", [128, 3, 128], F32)
        gmask = VT("gmask", [128, 512], F32)
        gv = VT("gv", [128, (2 * L + 1) * 8], F32)
        glag = VT("glag", [128, L * 4], F32)
        convw = VT("convw", [128, L, 4, 3], F32)
        fconvw = VT("fconvw", [128, L, 44, 3], F32)
        sinks = VT("sinks", [128, L * 8], F32)
        walpha = VT("walpha", [17, L, 512], F32)
        gaaug = VT("gaaug", [32, 512], F32)
        epsb = VT("epsb", [128, 1], F32)
        NF = 9
        fp_t = sb("fpool", [128, NF, 514], F32)
        fpool = Ring([V(fp_t[:, i, :], [Buf("fp%d" % i)]) for i in range(NF)])
        NB = 20
        bp_t = sb("bpool", [128, NB, 512], BF16)
        bpool = Ring([V(bp_t[:, i, :], [Buf("bp%d" % i)]) for i in range(NB)])
        sc_t = sb("scpool", [128, 2, 1024], F32)
        scV = [V(sc_t[:, i, :], [Buf("sc%d" % i)]) for i in range(2)]
        scpool = Ring(scV)
        ltokb = [scV[tb // 2][:, (tb % 2) * 512:(tb % 2 + 1) * 512] for tb in range(4)]
        ltokh = [V(sc_t[:, tb // 2, :].bitcast(BF16)[:, (tb % 2) * 1024:(tb % 2) * 1024 + 512], scV[tb // 2].bufs)
                 for tb in range(4)]
        trib = VT("trib", [128, 3, 128], BF16)
        pp_t = sb("ppool", [128, 2, 1024], BF16)
        ppool = Ring([V(pp_t[:, i, :], [Buf("pp%d" % i)]) for i in range(2)])
        pt_t = sb("ptpool", [128, 2, 1024], BF16)
        ptpool = Ring([V(pt_t[:, i, :], [Buf("pt%d" % i)]) for i in range(2)])
        kt_t = sb("ktok", [128, 2, 4, 128], BF16)
        ktpool = Ring([(i, [V(kt_t[:, i, tb, :], [Buf("ktok%d_%d" % (i, tb))]) for tb in range(4)]) for i in range(2)])
        NS = 24
        sm_t = sb("small", [128, NS, 8], F32)
        small = Ring([V(sm_t[:, i, :], [Buf("sm%d" % i)]) for i in range(NS)])
        NW = 4
        wr_t = sb("wring", [128, NW, 8, 512], BF16)
        wring = Ring([V(wr_t[:, i, :, :], [Buf("wr%d" % i)]) for i in range(NW)])
        NPS = 7
        psb = [ps("ps%d" % i, [128, 512], F32) for i in range(NPS)]
        psV = [V(psb[i][:], [Buf("ps%d" % i)]) for i in range(NPS)]
        pspool = Ring(psV)
        mixring = Ring(psV[0:5])
        poring = Ring(psV[5:6])
        gatering = Ring(psV[6:7])
        pbt = ps("psbf", [128, 1024], BF16)
        PB = V(pbt[:], [Buf("psbf")])

        def _body():
            for dst, src in ((identf, identf_d), (identb, identb_d), (tri, tri_d), (gmask, gmask_d),
                             (swab, swab_d), (gv, gv_d), (glag, glag_d), (convw, convw_d), (fconvw, fconvw_d),
                             (sinks, sinks_d), (walpha, walpha_d)):
                P.dma(dst, V(src, []))
            P.memset("pool", epsb, EPS)
            P.copy("pool", trib, tri)
            P.memset("pool", onesb, 1.0)
            P.memset("pool", gaaug, 1.0)
            P.memset("pool", halo_all[0], 0.0)
            P.memset("pool", halo_all[1], 0.0)
            for l in range(L):
                for h in range(4):
                    P.memset("pool", Sst[l][h], 0.0)
                for g in range(2):
                    P.memset("pool", kT[l][g], 0.0)
                P.memset("pool", Vr[l][0], 0.0)

            chk(1)
            wbuf = {}
            order = ["w_in", "w_gla_o", "w_conv_o", "w_swa_o", "w_o", "w_up", "w_down"]
            for l in range(L):
                for k in order:
                    r, c = wspec[k]
                    b = Buf("%s_bf%d" % (k, l))
                    wbuf[(k, l)] = b
                    for r0 in range(0, r, 128):
                        P.dma(V(wb_d[k][l, r0:r0 + 128, :], [b]), V(w_d[k][l, r0:r0 + 128, :], []), queue="pool", nowaw=True)

            chk(2)
            def wsrc(k, l, r0, nk, c0, ncol):
                ap = wb_d[k][l, r0 * 128:(r0 + nk) * 128, c0:c0 + ncol].rearrange("(kc p) n -> p kc n", p=128)
                return V(ap, [wbuf[(k, l)]])

            def wload(k, l, r0, nk, c0, ncol):
                slot = wring()
                P.dma(slot[:, 0:nk, 0:ncol], wsrc(k, l, r0, nk, c0, ncol), final=(stage in (61, 62)))
                return slot

            def dump(name, v):
                if dbg and name in dbg_d:
                    P.dma(V(dbg_d[name], [Buf("dbg_" + name)]), v, final=True)

            def rms_stats(srcs, nparts_scale, ring=None):
                pst = (ring or pspool)()
                n = len(srcs)
                for i, s in enumerate(srcs):
                    sq = sqpool()
                    P.act(sq, s, AF.Square)
                    P.mm(pst, onesb, sq, start=(i == 0), stop=(i == n - 1))
                ln = fpool()
                P.act(ln[:, 0:NT], pst, AF.Ln, bias=epsb, scale=nparts_scale)
                r = fpool()
                P.act(r[:, 0:NT], ln[:, 0:NT], AF.Exp, scale=-0.5)
                return r

            def norm_to_uT(gcol0):
                r = rms_stats(hT, 1.0 / D)
                for c in range(8):
                    P.stt(uT[c], hT[c], gv[:, gcol0 + c:gcol0 + c + 1], r[:, 0:NT], ALU.mult, ALU.mult)

            def proj(wslot, col0, ncols_m, kcs=8, rhs_list=None, ring=None):
                rhs_list = rhs_list if rhs_list is not None else uT
                pt = (ring or pspool)()
                for kc in range(kcs):
                    P.mm(pt[0:ncols_m, :], wslot[:, kc, col0:col0 + ncols_m], rhs_list[kc], start=(kc == 0), stop=(kc == kcs - 1))
                return pt

            for t in range(NTILES):
                tok0 = t * NT
                for tb in range(4):
                    xs = scpool()
                    P.dma(xs, V(x_d[tok0 + tb * 128: tok0 + (tb + 1) * 128, :], []))
                    for half in range(2):
                        pt = pspool()
                        for cc in range(4):
                            c = half * 4 + cc
                            P.tr(pt[:, cc * 128:(cc + 1) * 128], xs[:, c * 128:(c + 1) * 128], identf)
                        dst = V(hT_t[:, half * 4:(half + 1) * 4, tb * 128:(tb + 1) * 128],
                                [hT[half * 4 + cc].bufs[0] for cc in range(4)])
                        src = V(pt.ap.rearrange("p (c n) -> p c n", c=4), pt.bufs)
                        P.copy("act" if half == 0 else "dve", dst, src)

                chk(3)
                for l in range(L):
                    first = (t == 0)
                    norm_to_uT(l * 8)
                    if t == 0 and l == 0:
                        dump("uT0", V(uT_t[:, 0, :], uT[0].bufs))

                    chk(4)
                    gring = mixring
                    wsm = wring()
                    P.dma(wsm[:, :, 0:16], wsrc("w_in", l, 0, 8, 2048, 16), nowaw=True)
                    for g in range(2):
                        for d2 in range(2):
                            P.dma(wsm[:, :, 16 + g * 128 + d2 * 64: 16 + g * 128 + (d2 + 1) * 64],
                                  wsrc("w_in", l, 0, 8, 4112 + g * 64, 64), nowaw=True)
                    P.dma(wsm[:, :, 272:400], wsrc("w_in", l, 0, 8, 4240, 128), nowaw=True)
                    pga = gring()
                    for kc in range(8):
                        P.mm(pga[0:16, :], wsm[:, kc, 0:16], uT[kc], start=(kc == 0), stop=(kc == 7))
                    P.copy("act", gaaug[0:16, :], pga[0:16, :])
                    ltok = []
                    for tb in range(4):
                        px = gring()
                        P.mm(px, gaaug[0:17, tb * 128:(tb + 1) * 128], walpha[0:17, l, :])
                        e = ltokb[tb]
                        P.act(e, px, AF.Exp, scale=-1.0)
                        ltok.append(e)
                    for tb in range(4):
                        P.act(ltokh[tb], ltok[tb], AF.Ln, bias=1.0)
                    chk(5)
                    for g in range(2):
                        pk = gring()
                        for kc in range(8):
                            P.mm(pk, wsm[:, kc, 16 + g * 128:16 + (g + 1) * 128], uT[kc], start=(kc == 0), stop=(kc == 7))
                        P.copy("act", kT[l][g][:, 128:640], pk)
                    for tb in range(4):
                        pv = gring()
                        for kc in range(8):
                            P.mm(pv[:, 0:128], uT[kc][:, tb * 128:(tb + 1) * 128], wsm[:, kc, 272:400], start=(kc == 0), stop=(kc == 7))
                        P.copy("act", Vr[l][tb + 1], pv[:, 0:128])
                    w_sq = wload("w_in", l, 0, 8, 3600, 512)
                    for c4 in range(4):
                        pq = proj(w_sq, c4 * 128, 128, ring=gring)
                        P.act(qs[c4], pq, AF.Copy, scale=0.125)
                    chk(8)
                    w_cx = wload("w_in", l, 0, 8, 2064, 512)
                    w_cb = wload("w_in", l, 0, 8, 2576, 512)
                    w_cc = wload("w_in", l, 0, 8, 3088, 512)
                    for c4 in range(4):
                        pcx = proj(w_cx, c4 * 128, 128, ring=gring)
                        cxs = fpool()
                        P.copy("act", cxs[:, 0:NT], pcx)
                        pcc = proj(w_cc, c4 * 128, 128, ring=gring)
                        pbuf = fpool()
                        P.copy("pool", pbuf[:, 0:2], halo_c[l][c4])
                        P.tt("dve", pbuf[:, 2:2 + NT], pcc, cxs[:, 0:NT], ALU.mult)
                        P.copy("pool", halo_c[l][c4], pbuf[:, NT:NT + 2])
                        acc = fpool()
                        P.act(acc[:, 0:NT], pbuf[:, 0:NT], AF.Copy, scale=convw[:, l, c4, 0:1])
                        P.stt(acc[:, 0:NT], pbuf[:, 1:1 + NT], convw[:, l, c4, 1:2], acc[:, 0:NT], ALU.mult, ALU.add)
                        P.stt(acc[:, 0:NT], pbuf[:, 2:2 + NT], convw[:, l, c4, 2:3], acc[:, 0:NT], ALU.mult, ALU.add)
                        pcb = proj(w_cb, c4 * 128, 128, ring=gring)
                        P.tt("dve", cvT[c4], pcb, acc[:, 0:NT], ALU.mult)
                    chk(6)
                    w_gv = wload("w_in", l, 0, 8, 1024, 512)
                    for tb in range(4):
                        pv = gring()
                        for kc in range(8):
                            P.mm(pv, uT[kc][:, tb * 128:(tb + 1) * 128], w_gv[:, kc, :], start=(kc == 0), stop=(kc == 7))
                        P.copy("act", vt[tb], pv)
                    w_gq = wload("w_in", l, 0, 8, 0, 512)
                    w_gk = wload("w_in", l, 0, 8, 512, 512)
                    HS = [slice(hh * 128, (hh + 1) * 128) for hh in range(4)]
                    gl = [dict() for _ in range(4)]

                    def G1(hh):
                        hs = HS[hh]
                        d = gl[hh]
                        pA, pB, pb = gring(), gring(), gring()
                        for tb in range(4):
                            ts_ = slice(tb * 128, (tb + 1) * 128)
                            P.mm(pA[:, ts_], ltokh[tb][:, hs], trib[:, 1, :])
                            P.mm(pB[:, ts_], ltokh[tb][:, hs], trib[:, 2, :])
                            P.mm(pb[:, ts_], ltokh[tb][:, hs], trib[:, 0, :])
                        E1, E2, E3, E4 = fpool(), fpool(), fpool(), fpool()
                        P.act(E1[:, 0:NT], pA, AF.Exp)
                        P.act(E2[:, 0:NT], pA, AF.Exp, scale=-1.0)
                        P.act(E3[:, 0:NT], pB, AF.Exp)
                        P.act(E4[:, 0:NT], pb, AF.Exp)
                        pq = proj(w_gq, hh * 128, 128, ring=gring)
                        d["qa"], d["qb"] = bpool(), bpool()
                        P.stt(d["qa"], pq, 128.0 ** -0.5, E1[:, 0:NT], ALU.mult, ALU.mult)
                        P.stt(d["qb"], pq, 128.0 ** -0.5, E4[:, 0:NT], ALU.mult, ALU.mult)
                        pk = proj(w_gk, hh * 128, 128, ring=gring)
                        d["ka"], d["kb"] = bpool(), bpool()
                        P.tt("dve", d["ka"], pk, E2[:, 0:NT], ALU.mult)
                        P.tt("dve", d["kb"], pk, E3[:, 0:NT], ALU.mult)
                        d["dec"] = small()
                        P.copy("pool", d["dec"], V(E4.ap[:, 0:NT].rearrange("p (c k) -> p c k", c=8)[:, :, 63], E4.bufs))
                        if t == 0 and l == 0 and hh == 0:
                            dump("E4", E4[:, 0:NT])

                    def G2(hh):
                        hs = HS[hh]
                        d = gl[hh]
                        ktok = ktpool()
                        for tb in range(4):
                            P.tr(PB[:, tb * 128:(tb + 1) * 128], d["kb"][:, tb * 128:(tb + 1) * 128], identb)
                        P.copy("act", V(kt_t[:, ktok[0], :, :], [b for tb in range(4) for b in ktok[1][tb].bufs]),
                               V(PB.ap[:, 0:512].rearrange("p (a b) -> p a b", a=4), PB.bufs))
                        ktok = ktok[1]
                        pat = gring()
                        for tb in range(4):
                            ts_ = slice(tb * 128, (tb + 1) * 128)
                            P.mm(pat[:, ts_], d["ka"][:, ts_], d["qa"][:, ts_])
                        d["am"] = bpool()
                        P.tt("dve", d["am"], pat, gmask, ALU.mult)
                        pkv = [gring(), gring()]
                        for c in range(8):
                            tb, r0 = c // 2, (c % 2) * 64
                            P.mm(pkv[c % 2][:, tb * 128:(tb + 1) * 128], ktok[tb][r0:r0 + 64, :], vt[tb][r0:r0 + 64, hs])
                        d["pkv"] = pkv

                    def G3(hh):
                        d = gl[hh]
                        sb_ = Sbf[hh % 2]
                        pkv = d["pkv"]
                        P.copy("dve", sb_[0], Sst[l][hh])
                        for c in range(8):
                            tb = c // 2
                            P.stt(Sst[l][hh], Sst[l][hh], d["dec"][:, c:c + 1], pkv[c % 2][:, tb * 128:(tb + 1) * 128],
                                  ALU.mult, ALU.add)
                            P.copy("dve", sb_[c + 1], Sst[l][hh])

                    def G4(hh):
                        hs = HS[hh]
                        d = gl[hh]
                        sb_ = Sbf[hh % 2]
                        po = poring()
                        for tb in range(4):
                            ts_ = slice(tb * 128, (tb + 1) * 128)
                            P.mm(po[:, ts_], vt[tb][:, hs], d["am"][:, ts_], start=True, stop=False)
                            for c2 in range(2):
                                c = tb * 2 + c2
                                cs = slice(c * 64, (c + 1) * 64)
                                P.mm(po[:, cs], sb_[c], d["qb"][:, cs], start=False, stop=(c2 == 1))
                        d["po"] = po

                    def G5(hh, w_gr):
                        d = gl[hh]
                        po = d["po"]
                        pg = proj(w_gr, hh * 128, 128, ring=gring)
                        sg = fpool()
                        P.act(sg[:, 0:NT], pg, AF.Exp, scale=-1.0)
                        P.act(sg[:, 0:NT], sg[:, 0:NT], AF.Ln, bias=1.0)
                        P.act(sg[:, 0:NT], sg[:, 0:NT], AF.Exp, scale=-1.0)
                        P.stt(sg[:, 0:NT], pg, 1.0, sg[:, 0:NT], ALU.mult, ALU.mult)
                        r = rms_stats([po], 1.0 / 128, ring=gring)
                        tmp = fpool()
                        P.stt(tmp[:, 0:NT], po, glag[:, l * 4 + hh:l * 4 + hh + 1], r[:, 0:NT], ALU.mult, ALU.mult)
                        P.tt("pool", goT[hh], tmp[:, 0:NT], sg[:, 0:NT], ALU.mult)
                        if t == 0 and l == 0 and hh == 0:
                            dump("tmp_o", tmp[:, 0:NT])

                    for hh in range(4):
                        G1(hh)
                    chk(7)
                    w_gr = wload("w_in", l, 0, 8, 1536, 512)
                    G2(0); G3(0)
                    G2(1); G3(1)
                    G4(0); G5(0, w_gr)
                    G2(2); G3(2)
                    G4(1); G5(1, w_gr)
                    G2(3); G3(3)
                    G4(2); G5(2, w_gr)
                    G4(3); G5(3, w_gr)
                    chk(9)
                    sw = {}

                    def SA(i):
                        tb, g = i // 2, i % 2
                        sA, sB = psV[(i % 2) * 2], psV[(i % 2) * 2 + 1]
                        for j in range(4):
                            h = 4 * g + j
                            ch, po_ = h // 2, (h % 2) * 64
                            bank = sA if j % 2 == 0 else sB
                            P.mm(bank[:, (j // 2) * 256:(j // 2 + 1) * 256], qs[ch][po_:po_ + 64, tb * 128:(tb + 1) * 128],
                                 kT[l][g][po_:po_ + 64, tb * 128:tb * 128 + 256])
                        sc = scpool()
                        sc4 = V(sc.ap.rearrange("p (j k) -> p j k", j=4), sc.bufs)
                        sw4 = V(swab.ap[:, g, :].rearrange("p (j k) -> p j k", j=4), swab.bufs)
                        P.tt("dve", sc4[:, 0::2, :], V(sA.ap.rearrange("p (j k) -> p j k", j=2), sA.bufs), sw4[:, 0::2, :], ALU.add)
                        P.tt("dve", sc4[:, 1::2, :], V(sB.ap.rearrange("p (j k) -> p j k", j=2), sB.bufs), sw4[:, 1::2, :], ALU.add)
                        if first and tb == 0:
                            P.ts("pool", sc4[:, :, 0:128], sc4[:, :, 0:128], NEG, ALU.add)
                        sm, sm2, sm3 = small(), small(), small()
                        mx, negm = sm[:, 0:4], sm[:, 4:8]
                        rsum, dd = sm2[:, 0:4], sm2[:, 4:8]
                        es, rinv = sm3[:, 0:4], sm3[:, 4:8]
                        sk_ = sinks[:, l * 8 + g * 4:l * 8 + g * 4 + 4]
                        P.reduce(mx, sc4, ALU.max)
                        P.tt("dve", mx, mx, sk_, ALU.max)
                        P.ts("dve", negm, mx, -1.0, ALU.mult)
                        pn = ppool()
                        for j in range(4):
                            P.act(pn[:, j * 256:(j + 1) * 256], sc[:, j * 256:(j + 1) * 256], AF.Exp,
                                  bias=negm[:, j:j + 1], accum=rsum[:, j:j + 1])
                        P.tt("dve", dd, sk_, mx, ALU.subtract)
                        P.act(es, dd, AF.Exp)
                        P.tt("dve", es, es, rsum, ALU.add)
                        P.recip(rinv, es)
                        pn4 = V(pn.ap.rearrange("p (j k) -> p j k", j=4), pn.bufs)
                        P.tt("dve", pn4, pn4, V(rinv.ap.unsqueeze(2).broadcast_to([128, 4, 256]), rinv.bufs), ALU.mult)
                        sw[i] = pn

                    def SB(i):
                        tb, g = i // 2, i % 2
                        pn = sw.pop(i)
                        posw = psV[4 + (tb % 2)]
                        for j in range(4):
                            for kb in range(2):
                                P.tr(PB[:, (kb * 4 + j) * 128:(kb * 4 + j + 1) * 128],
                                     pn[:, j * 256 + kb * 128: j * 256 + (kb + 1) * 128], identb)
                        ptt = ptpool()
                        P.copy("act", ptt, PB)
                        for j in range(4):
                            h = 4 * g + j
                            ch, po_ = h // 2, (h % 2) * 64
                            for kb in range(2):
                                P.mm(posw[po_:po_ + 64, ch * 128:(ch + 1) * 128], Vr[l][tb + kb][:, g * 64:(g + 1) * 64],
                                     ptt[:, (kb * 4 + j) * 128:(kb * 4 + j + 1) * 128], start=(kb == 0), stop=(kb == 1))
                        if g == 1:
                            dst = V(ar_t[:, 8:12, tb * 128:(tb + 1) * 128], [oswT[c4].bufs[0] for c4 in range(4)])
                            P.copy("act", dst, V(posw.ap.rearrange("p (c n) -> p c n", c=4), posw.bufs))

                    SA(0)
                    for i in range(8):
                        if i + 1 < 8:
                            SA(i + 1)
                        SB(i)
                    for g in range(2):
                        P.copy("pool", kT[l][g][:, 0:128], kT[l][g][:, 512:640])
                    P.copy("pool", Vr[l][0], Vr[l][4])
                    if t == 0 and l == 0:
                        dump("go0", V(ar_t[:, 0, :], goT[0].bufs))
                        dump("cv0", V(ar_t[:, 4, :], cvT[0].bufs))
                        dump("osw0", V(ar_t[:, 8, :], oswT[0].bufs))


                    bo_names = ["w_gla_o", "w_conv_o", "w_swa_o"]
                    brs = [goT, cvT, oswT]
                    for b in range(3):
                        for mgp in range(2):
                            wg = wload("w_in", l, 0, 8, 4368 + b * 1024 + mgp * 512, 512)
                            for m4 in range(4):
                                pgt = proj(wg, m4 * 128, 128, ring=gatering)
                                e = fpool()
                                P.act(e[:, 0:NT], pgt, AF.Exp, scale=-1.0)
                                P.act(e[:, 0:NT], e[:, 0:NT], AF.Ln, bias=1.0)
                                P.act(sigb[b * 8 + mgp * 4 + m4], e[:, 0:NT], AF.Exp, scale=-1.0)
                    for mgp in range(2):
                        wbo = [wload(bo_names[b], l, 0, 4, mgp * 512, 512) for b in range(3)]
                        for m4 in range(4):
                            m = mgp * 4 + m4
                            terms = []
                            for b in range(3):
                                py = proj(wbo[b], m4 * 128, 128, kcs=4, rhs_list=brs[b])
                                tm = fpool()
                                P.tt("dve", tm[:, 0:NT], py, sigb[b * 8 + m], ALU.mult)
                                terms.append(tm)
                            P.tt("pool", terms[0][:, 0:NT], terms[0][:, 0:NT], terms[1][:, 0:NT], ALU.add)
                            P.tt("pool", mgT[m], terms[0][:, 0:NT], terms[2][:, 0:NT], ALU.add)
                    chk(11)
                    for mgp in range(2):
                        wo = wload("w_o", l, 0, 8, mgp * 512, 512)
                        for m4 in range(4):
                            m = mgp * 4 + m4
                            pt = proj(wo, m4 * 128, 128, rhs_list=mgT)
                            P.tt("dve", hT[m], pt, hT[m], ALU.add)
                    if t == 0 and l == 0:
                        dump("h1", V(hT_t[:, 0, :], hT[0].bufs))

                    chk(12)
                    norm_to_uT((L + l) * 8)
                    for pg in range(6):
                        npair = 4 if pg < 5 else 2
                        wa = wload("w_up", l, 0, 8, pg * 512, npair * 128)
                        wb = wload("w_up", l, 0, 8, DFF + pg * 512, npair * 128)
                        for pi in range(npair):
                            c = pg * 4 + pi
                            accs = []
                            for (wsl, cidx) in ((wa, c), (wb, c + 22)):
                                ph = proj(wsl, pi * 128, 128)
                                hb = fpool()
                                P.copy("pool", hb[:, 0:2], halo_f[l][cidx])
                                P.copy("act", hb[:, 2:2 + NT], ph)
                                P.copy("pool", halo_f[l][cidx], hb[:, NT:NT + 2])
                                acc = fpool()
                                P.act(acc[:, 0:NT], ph, AF.Copy, scale=fconvw[:, l, cidx, 2:3])
                                P.stt(acc[:, 0:NT], hb[:, 0:NT], fconvw[:, l, cidx, 0:1], acc[:, 0:NT], ALU.mult, ALU.add)
                                P.stt(acc[:, 0:NT], hb[:, 1:1 + NT], fconvw[:, l, cidx, 1:2], acc[:, 0:NT], ALU.mult, ALU.add)
                                accs.append(acc)
                            sa = fpool()
                            P.act(sa[:, 0:NT], accs[0][:, 0:NT], AF.Silu)
                            P.tt("dve", gT[c], sa[:, 0:NT], accs[1][:, 0:NT], ALU.mult)
                    chk(13)
                    for mgp in range(2):
                        banks = [pspool() for _ in range(4)]
                        for (k0, nk) in ((0, 8), (8, 8), (16, 6)):
                            wd = wload("w_down", l, k0, nk, mgp * 512, 512)
                            for m4 in range(4):
                                for kk in range(nk):
                                    k = k0 + kk
                                    P.mm(banks[m4], wd[:, kk, m4 * 128:(m4 + 1) * 128], gT[k], start=(k == 0), stop=(k == 21))
                        for m4 in range(4):
                            m = mgp * 4 + m4
                            P.tt("dve", hT[m], banks[m4], hT[m], ALU.add)
                    if t == 0 and l == 0:
                        dump("h2", V(hT_t[:, 0, :], hT[0].bufs))

                chk(14)
                if final_norm:
                    r = rms_stats(hT, 1.0 / D)
                ofm = []
                for c in range(8):
                    o = fpool()
                    if final_norm:
                        P.stt(o[:, 0:NT], hT[c], gv[:, 2 * L * 8 + c:2 * L * 8 + c + 1], r[:, 0:NT], ALU.mult, ALU.mult)
                    else:
                        P.copy("pool", o[:, 0:NT], hT[c])
                    ofm.append(o)
                for tb in range(4):
                    xo = scpool()
                    for half in range(2):
                        pt = pspool()
                        for cc in range(4):
                            c = half * 4 + cc
                            P.tr(pt[:, cc * 128:(cc + 1) * 128], ofm[c][:, tb * 128:(tb + 1) * 128], identf)
                        P.copy("act" if half == 0 else "dve", xo[:, half * 512:(half + 1) * 512], pt)
                    P.dma(V(out_d[tok0 + tb * 128: tok0 + (tb + 1) * 128, :], [Buf("out%d_%d" % (t, tb))]), xo, final=True)

        try:
            _body()
        except _Stop:
            pass
        P.emit()
        nc_marks = P.marks
        nc_counts = {e: len(v) for e, v in P.streams.items()}
    _MARKS[id(nc)] = (nc_marks, nc_counts)
    return nc


_WNAMES = ["w_in", "w_gla_o", "w_conv_o", "w_swa_o", "w_o", "w_up", "w_down"]


def make_in_maps(x, params, L):
    consts = _consts()
    lay = _layout_params(L, params["g_mix"], params["g_ffn"], params["g_final"], params["gla_norm_g"], params["conv_w"],
                         params["ffn_conv_w"], params["swa_sinks"], params["gla_w_alpha"], params["gla_b_alpha"])
    shared = {}
    shared.update(consts)
    shared.update(lay)
    for k in _WNAMES:
        shared[k] = np.ascontiguousarray(params[k][:L], dtype=np.float32)
    maps = []
    for b in range(x.shape[0]):
        m = dict(shared)
        m["x"] = np.ascontiguousarray(x[b], dtype=np.float32)
        maps.append(m)
    return maps


_NC_CACHE = {}


def kernel(x, g_mix, w_in, gla_w_alpha, gla_b_alpha, gla_norm_g, conv_w, swa_sinks, w_gla_o, w_conv_o, w_swa_o, w_o,
           g_ffn, w_up, ffn_conv_w, w_down, g_final):
    x = np.asarray(x)
    B, S, _ = x.shape
    L = int(np.asarray(g_mix).shape[0])
    params = dict(g_mix=g_mix, w_in=w_in, gla_w_alpha=gla_w_alpha, gla_b_alpha=gla_b_alpha, gla_norm_g=gla_norm_g,
                  conv_w=conv_w, swa_sinks=swa_sinks, w_gla_o=w_gla_o, w_conv_o=w_conv_o, w_swa_o=w_swa_o, w_o=w_o,
                  g_ffn=g_ffn, w_up=w_up, ffn_conv_w=ffn_conv_w, w_down=w_down, g_final=g_final)
    params = {k: np.asarray(v, dtype=np.float32) for k, v in params.items()}
    key = (S, L)
    if key not in _NC_CACHE:
        _NC_CACHE[key] = build_nc(S, L)
    nc = _NC_CACHE[key]
    in_maps = make_in_maps(x, params, L)
    res = run_bass_kernel_spmd(nc, in_maps, core_ids=list(range(B)))
    out = np.stack([np.asarray(r["out"]) for r in res.results], axis=0)
    return out.astype(np.float32)
```

```python
from contextlib import ExitStack
import numpy as np
import ml_dtypes
import concourse.bass as bass
import concourse.mybir as mybir
from concourse.bass_utils import run_bass_kernel_spmd

F32 = mybir.dt.float32
BF16 = mybir.dt.bfloat16
AF = mybir.ActivationFunctionType
ALU = mybir.AluOpType
AX = mybir.AxisListType

ENGS = ("pe", "act", "dve", "pool", "sp")
SAME_ENGINE_SYNC = {"pe": False, "act": True, "dve": True, "pool": True, "sp": True}

D = 1024
DIN = 7440
DFF = 2816
NT = 512
EPS = 1e-6
NEG = -30000.0


class Buf:
    __slots__ = ("name", "last_writer", "readers", "sem", "sem_cnt")

    def __init__(self, name):
        self.name = name
        self.last_writer = None
        self.readers = []
        self.sem = None
        self.sem_cnt = 0


class V:
    __slots__ = ("ap", "bufs")

    def __init__(self, ap, bufs):
        self.ap = ap
        self.bufs = tuple(bufs)

    def __getitem__(self, k):
        return V(self.ap[k], self.bufs)


class Op:
    __slots__ = ("eng", "fn", "idx", "pidx", "preds", "succs", "waits", "dma_waits", "signal", "signo", "is_dma",
                 "sem", "sem_val", "dur", "tab", "nbytes", "npend", "ready", "finish", "tag", "start", "prio")

    def __init__(self, eng, fn, is_dma=False):
        self.eng = eng
        self.fn = fn
        self.idx = -1
        self.pidx = -1
        self.preds = []
        self.succs = []
        self.waits = []
        self.dma_waits = []
        self.signal = False
        self.signo = 0
        self.is_dma = is_dma
        self.sem = None
        self.sem_val = 0
        self.dur = 100.0
        self.tab = None
        self.nbytes = 0
        self.npend = 0
        self.ready = 0.0
        self.finish = 0.0
        self.tag = None
        self.start = 0.0
        self.prio = 0.0


_ACT_SETS = {}


def _act_set(func):
    if func in (AF.Exp, AF.Ln):
        return "explog"
    if func == AF.Silu:
        return "silu"
    if func == AF.Sigmoid:
        return "sigmoid"
    return None


def _fsize(ap):
    n = 1
    for d in ap.shape[1:]:
        n *= d
    return n


class Prog:
    REORDER = ("pe", "act", "dve")

    def __init__(self, nc, stack):
        self.nc = nc
        self.stack = stack
        self.all_ops = []
        self.streams = {e: [] for e in ENGS}
        self.esem = {}
        for e in ("pe", "act", "dve", "pool"):
            self.esem[e] = stack.enter_context(nc.semaphore("es_" + e))
        self.final_dma = []
        self.marks = []
        self.sched = True
        self.prio_mode = "bl"
        self.cur_tag = None

    def mark(self, label):
        self.marks.append((label, len(self.streams["pe"])))
        self.cur_tag = (label, len(self.marks))

    def new_sem(self, name):
        return self.stack.enter_context(self.nc.semaphore(name))

    def op(self, eng, fn, reads=(), writes=(), dma=False, sem_buf=None, nowaw=False, dur=100.0, tab=None, nbytes=0):
        X = Op(eng, fn, is_dma=dma)
        X.pidx = len(self.all_ops)
        X.dur = dur
        X.tab = tab
        X.nbytes = nbytes
        X.tag = self.cur_tag
        if dma:
            b = sem_buf
            if b.sem is None:
                b.sem = self.new_sem("ds_" + b.name)
            b.sem_cnt += 16
            X.sem = b.sem
            X.sem_val = b.sem_cnt
        deps = []
        for r in reads:
            deps.append(r.last_writer)
        for w in writes:
            if not nowaw:
                deps.append(w.last_writer)
            deps.extend(w.readers)
        seen = set()
        for Y in deps:
            if Y is None or Y is X or id(Y) in seen:
                continue
            seen.add(id(Y))
            X.preds.append(Y)
        for r in reads:
            r.readers.append(X)
        for w in writes:
            w.last_writer = X
            w.readers = []
        self.all_ops.append(X)
        self.streams[eng].append(X)
        return X

    def dma(self, out, in_, queue="sp", final=False, nowaw=False):
        nb = 1
        for d in out.ap.shape:
            nb *= d
        nb *= 2 if out.ap.dtype == BF16 else 4
        X = self.op(queue, lambda e: e.dma_start(out=out.ap, in_=in_.ap), reads=in_.bufs, writes=out.bufs,
                    dma=True, sem_buf=out.bufs[0], nowaw=nowaw, dur=60.0, nbytes=nb)
        if final:
            self.final_dma.append(X)
        return X

    def mm(self, out, lhsT, rhs, start=True, stop=True):
        n = _fsize(rhs.ap)
        passes = 4 if rhs.ap.dtype == F32 else 1
        return self.op("pe", lambda e: e.matmul(out.ap, lhsT.ap, rhs.ap, start=start, stop=stop),
                       reads=lhsT.bufs + rhs.bufs, writes=out.bufs, dur=passes * max(n, 64) / 2.4 + 12)

    def tr(self, out, in_, ident):
        return self.op("pe", lambda e: e.transpose(out.ap, in_.ap, ident.ap), reads=in_.bufs + ident.bufs, writes=out.bufs,
                       dur=110.0)

    def act(self, out, in_, func, bias=None, scale=None, accum=None):
        reads = list(in_.bufs)
        kw = {}
        if bias is not None:
            if isinstance(bias, V):
                reads += bias.bufs
                kw["bias"] = bias.ap
            else:
                kw["bias"] = bias
        if scale is not None:
            if isinstance(scale, V):
                reads += scale.bufs
                kw["scale"] = scale.ap
            else:
                kw["scale"] = scale
        writes = list(out.bufs)
        d = 180.0 + _fsize(in_.ap) / 1.2
        if accum is not None:
            writes += accum.bufs
            kw["accum_out"] = accum.ap
            d += 100
        return self.op("act", lambda e: e.activation(out=out.ap, in_=in_.ap, func=func, **kw), reads=reads, writes=writes,
                       dur=d, tab=_act_set(func))

    def copy(self, eng, out, in_):
        n = _fsize(in_.ap)
        if eng == "act":
            return self.op("act", lambda e: e.copy(out.ap, in_.ap), reads=in_.bufs, writes=out.bufs, dur=180.0 + n / 1.2)
        d = (100.0 + 1.15 * n) if eng == "dve" else (350.0 + 1.0 * n)
        return self.op(eng, lambda e: e.tensor_copy(out.ap, in_.ap), reads=in_.bufs, writes=out.bufs, dur=d)

    def tt(self, eng, out, in0, in1, op):
        n = _fsize(in0.ap)
        d = (100.0 + 1.15 * n) if eng == "dve" else (150.0 + 2.2 * n)
        return self.op(eng, lambda e: e.tensor_tensor(out=out.ap, in0=in0.ap, in1=in1.ap, op=op),
                       reads=in0.bufs + in1.bufs, writes=out.bufs, dur=d)

    def ts(self, eng, out, in0, s1, op0, s2=None, op1=None):
        reads = list(in0.bufs)
        a1 = s1
        if isinstance(s1, V):
            reads += s1.bufs
            a1 = s1.ap
        a2 = s2
        if isinstance(s2, V):
            reads += s2.bufs
            a2 = s2.ap
        n = _fsize(in0.ap)
        d = (100.0 + 1.15 * n) if eng == "dve" else (300.0 + 11.0 * n)
        if op1 is None:
            return self.op(eng, lambda e: e.tensor_scalar(out=out.ap, in0=in0.ap, scalar1=a1, scalar2=None, op0=op0),
                           reads=reads, writes=out.bufs, dur=d)
        return self.op(eng, lambda e: e.tensor_scalar(out=out.ap, in0=in0.ap, scalar1=a1, scalar2=a2, op0=op0, op1=op1),
                       reads=reads, writes=out.bufs, dur=d)

    def stt(self, out, in0, scalar, in1, op0, op1):
        reads = list(in0.bufs) + list(in1.bufs)
        sc = scalar
        if isinstance(scalar, V):
            reads += scalar.bufs
            sc = scalar.ap
        return self.op("dve", lambda e: e.scalar_tensor_tensor(out=out.ap, in0=in0.ap, scalar=sc, in1=in1.ap, op0=op0, op1=op1),
                       reads=reads, writes=out.bufs, dur=120.0 + 1.2 * _fsize(in0.ap))

    def reduce(self, out, in_, op, axis=AX.X):
        return self.op("dve", lambda e: e.tensor_reduce(out=out.ap, in_=in_.ap, axis=axis, op=op), reads=in_.bufs, writes=out.bufs,
                       dur=100.0 + 1.1 * _fsize(in_.ap))

    def recip(self, out, in_):
        return self.op("dve", lambda e: e.reciprocal(out.ap, in_.ap), reads=in_.bufs, writes=out.bufs,
                       dur=100.0 + 8.4 * _fsize(in_.ap))

    def memset(self, eng, out, val):
        return self.op(eng, lambda e: e.memset(out.ap, val), writes=out.bufs, dur=200.0)

    def schedule(self):
        import heapq
        ops = self.all_ops
        for X in ops:
            X.succs = []
        for X in ops:
            for Y in X.preds:
                Y.succs.append(X)
        fixed_prev = {}
        extra = {}
        for X in ops:
            if X.eng not in self.REORDER:
                pv = fixed_prev.get(X.eng)
                if pv is not None:
                    extra[id(X)] = pv
                    pv.succs.append(X)
                fixed_prev[X.eng] = X
        for X in ops:
            X.npend = len(X.preds) + (1 if id(X) in extra else 0)
            X.ready = 0.0
        if self.prio_mode == "bl":
            bl = {}
            for X in reversed(ops):
                m = 0.0
                for Z in X.succs:
                    v = bl[id(Z)] + 200.0
                    if v > m:
                        m = v
                bl[id(X)] = m + (X.dur if not X.is_dma else X.nbytes / 260.0 + 2000.0)
            for X in ops:
                X.prio = -bl[id(X)]
        else:
            for X in ops:
                X.prio = float(X.pidx)
        HOP = 200.0
        free_at = {e: 0.0 for e in ENGS}
        pending = {e: [] for e in ENGS}
        avail = {e: [] for e in ENGS}
        last_tab = {"act": None}
        dma_free = [0.0]
        for X in ops:
            if X.npend == 0:
                heapq.heappush(pending[X.eng], (0.0, X.pidx, X))
        order = {e: [] for e in ENGS}
        nleft = len(ops)
        while nleft:
            best = None
            for e in ENGS:
                T = free_at[e]
                pq, av = pending[e], avail[e]
                while pq and pq[0][0] <= T:
                    r, pi, X = heapq.heappop(pq)
                    heapq.heappush(av, (X.prio, pi, X))
                if av:
                    st = T
                elif pq:
                    st = pq[0][0]
                else:
                    continue
                if best is None or st < best[0]:
                    best = (st, e)
            st, e = best
            if avail[e]:
                pr, pi, X = heapq.heappop(avail[e])
            else:
                r, pi, X = heapq.heappop(pending[e])
            d = X.dur
            if e == "act" and X.tab is not None and X.tab != last_tab["act"]:
                d += 1300.0
                last_tab["act"] = X.tab
            if X.is_dma:
                free_at[e] = st + d
                t0 = max(dma_free[0], st + d)
                t1 = t0 + X.nbytes / 260.0
                dma_free[0] = t1
                X.finish = t1 + 2000.0
            else:
                X.finish = st + d
                free_at[e] = X.finish
            X.start = st
            order[e].append(X)
            nleft -= 1
            for Z in X.succs:
                Z.npend -= 1
                rt = X.finish + (HOP if Z.eng != X.eng or X.is_dma else 60.0)
                if rt > Z.ready:
                    Z.ready = rt
                if Z.npend == 0:
                    heapq.heappush(pending[Z.eng], (Z.ready, Z.pidx, Z))
        self.streams = order
        self.est_ns = max(free_at.values())

    def resolve(self):
        for e in ENGS:
            for i, X in enumerate(self.streams[e]):
                X.idx = i
        waited = {e: {f: -1 for f in ENGS} for e in ENGS}
        waited_dma = {e: {} for e in ENGS}
        for e in ENGS:
            for X in self.streams[e]:
                best = {}
                for Y in X.preds:
                    if Y.is_dma:
                        cur = waited_dma[e].get(Y.sem, 0)
                        if cur < Y.sem_val:
                            waited_dma[e][Y.sem] = Y.sem_val
                            X.dma_waits.append((Y.sem, Y.sem_val))
                        continue
                    if Y.eng == e:
                        assert Y.idx < X.idx, "same-engine order violated"
                        if not SAME_ENGINE_SYNC[e]:
                            continue
                    cur = best.get(Y.eng)
                    if cur is None or Y.idx > cur.idx:
                        best[Y.eng] = Y
                for f, Y in best.items():
                    if waited[e][f] >= Y.idx:
                        continue
                    waited[e][f] = Y.idx
                    Y.signal = True
                    X.waits.append(Y)

    def emit(self):
        nc = self.nc
        if self.sched:
            self.schedule()
        self.resolve()
        for e in ("pe", "act", "dve", "pool"):
            n = 0
            for X in self.streams[e]:
                if X.signal:
                    n += 1
                    X.signo = n
        with nc.Block() as block:
            def make(e):
                def body(eng):
                    for X in self.streams[e]:
                        for Y in X.waits:
                            eng.wait_ge(self.esem[Y.eng], Y.signo)
                        for (s, v) in X.dma_waits:
                            eng.wait_ge(s, v)
                        ins = X.fn(eng)
                        if X.is_dma:
                            ins.then_inc(X.sem, 16)
                        elif X.signal:
                            ins.then_inc(self.esem[e], 1)
                    if e == "sp":
                        for X in self.final_dma:
                            eng.wait_ge(X.sem, X.sem_val)
                return body
            block.tensor(make("pe"))
            block.scalar(make("act"))
            block.vector(make("dve"))
            block.gpsimd(make("pool"))
            block.sync(make("sp"))


class Ring:
    def __init__(self, items):
        self.items = items
        self.i = 0

    def __call__(self):
        v = self.items[self.i % len(self.items)]
        self.i += 1
        return v


def _consts():
    c = {}
    c["identf"] = np.eye(128, dtype=np.float32)
    c["identb"] = np.eye(128, dtype=np.float32).astype(ml_dtypes.bfloat16)
    c["onesf"] = np.ones((128, 128), np.float32)
    j = np.arange(128)[:, None]
    i = np.arange(128)[None, :]
    same = (j // 64) == (i // 64)
    tri = (same & (j <= i)).astype(np.float32)
    ref = (i // 64) * 64 + 31
    last = (i // 64) * 64 + 63
    t_ref = (same & (j <= ref)).astype(np.float32)
    t_last = (same & (j <= last)).astype(np.float32)
    sc = -1.0 / 16.0
    T0 = sc * tri
    T1 = sc * (tri - t_ref)
    T2 = sc * (t_last - tri)
    c["tri"] = np.ascontiguousarray(np.stack([T0, T1, T2], axis=1)).astype(np.float32)
    c["gmask"] = np.ascontiguousarray(np.tile(tri, (1, 4))).astype(np.float32)
    slopes = np.exp2(-(np.arange(1, 9, dtype=np.float64))).astype(np.float32)
    iq = np.arange(128)[:, None]
    jk = np.arange(256)[None, :]
    dist = 128 + iq - jk
    valid = (dist >= 0) & (dist < 128)
    bias = np.zeros((128, 2, 4, 256), np.float32)
    for g in range(2):
        for jh in range(4):
            h = 4 * g + jh
            bias[:, g, jh, :] = np.where(valid, -slopes[h] * dist.astype(np.float32), NEG)
    c["swab"] = np.ascontiguousarray(bias.reshape(128, 2, 1024))
    return c


def _chunkcols(v):
    n = v.shape[0] // 128
    return np.ascontiguousarray(v.reshape(n, 128).T)


def _layout_params(L, g_mix, g_ffn, g_final, gla_norm_g, conv_w, ffn_conv_w, swa_sinks, gla_w_alpha, gla_b_alpha):
    gv = np.concatenate([_chunkcols(g_mix[l]) for l in range(L)] + [_chunkcols(g_ffn[l]) for l in range(L)]
                        + [_chunkcols(g_final)], axis=1).astype(np.float32)
    glag = np.concatenate([_chunkcols(gla_norm_g[l]) for l in range(L)], axis=1).astype(np.float32)
    cw = np.stack([np.stack([_chunkcols(conv_w[l, k]) for k in range(3)], axis=2) for l in range(L)], axis=1)
    fw = np.stack([np.stack([_chunkcols(ffn_conv_w[l, k]) for k in range(3)], axis=2) for l in range(L)], axis=1)
    sinks = np.ascontiguousarray(np.broadcast_to(swa_sinks[:L].reshape(1, L * 8), (128, L * 8))).astype(np.float32)
    wal = np.concatenate([gla_w_alpha[:L], gla_b_alpha[:L, None, :]], axis=1).astype(np.float32)
    wal = np.ascontiguousarray(np.transpose(wal, (1, 0, 2)))
    return {"gv": gv, "glag": glag, "convw": np.ascontiguousarray(cw.astype(np.float32)),
            "fconvw": np.ascontiguousarray(fw.astype(np.float32)), "sinks": sinks, "walpha": wal}


class _Stop(Exception):
    pass


_MARKS = {}


def build_nc(S, L, final_norm=True, dbg=None, stage=None):
    assert S % NT == 0
    NTILES = S // NT
    nc = bass.Bass("TRN2", target_bir_lowering=False)

    def dram(name, shape, dt, kind="ExternalInput"):
        return nc.dram_tensor(name, list(shape), dt, kind=kind).ap()

    x_d = dram("x", [S, D], F32)
    out_d = dram("out", [S, D], F32, kind="ExternalOutput")
    wspec = {"w_in": (D, DIN), "w_gla_o": (512, D), "w_conv_o": (512, D), "w_swa_o": (512, D),
             "w_o": (D, D), "w_up": (D, 2 * DFF), "w_down": (DFF, D)}
    w_d = {k: dram(k, [L, r, c], F32) for k, (r, c) in wspec.items()}
    wb_d = {k: dram(k + "_bf", [L, r, c], BF16, kind="Internal") for k, (r, c) in wspec.items()}
    gv_d = dram("gv", [128, (2 * L + 1) * 8], F32)
    glag_d = dram("glag", [128, L * 4], F32)
    convw_d = dram("convw", [128, L, 4, 3], F32)
    fconvw_d = dram("fconvw", [128, L, 44, 3], F32)
    sinks_d = dram("sinks", [128, L * 8], F32)
    walpha_d = dram("walpha", [17, L, 512], F32)
    identf_d = dram("identf", [128, 128], F32)
    identb_d = dram("identb", [128, 128], BF16)
    onesf_d = dram("onesf", [128, 128], F32)
    tri_d = dram("tri", [128, 3, 128], F32)
    gmask_d = dram("gmask", [128, 512], F32)
    swab_d = dram("swab", [128, 2, 1024], F32)
    dbg_d = {}
    if dbg:
        for name, (shape, dt) in dbg.items():
            dbg_d[name] = dram("dbg_" + name, shape, dt, kind="ExternalOutput")

    with ExitStack() as st:
        P = Prog(nc, st)

        def chk(n):
            P.mark(n)
            if stage is not None and stage == n:
                raise _Stop()

        def sb(name, shape, dt):
            return st.enter_context(nc.sbuf_tensor("s_" + name, list(shape), dt))

        def ps(name, shape, dt):
            return st.enter_context(nc.psum_tensor(name, list(shape), dt))

        def VT(name, shape, dt):
            t = sb(name, shape, dt)
            return V(t[:], [Buf(name)])

        def chunked(name, n, cols, dt):
            t = sb(name, [128, n, cols], dt)
            return t, [V(t[:, i, :], [Buf("%s%d" % (name, i))]) for i in range(n)]

        hT_t, hT = chunked("hT", 8, NT, F32)
        uT_t, uT = chunked("uT", 8, NT, BF16)
        ar_t, ar = chunked("arena", 22, NT, BF16)
        goT, cvT, oswT, mgT, gT = ar[0:4], ar[4:8], ar[8:12], ar[12:20], ar
        qs_t, qs = chunked("qs", 4, NT, BF16)
        sg_t, sigb = chunked("sigb", 24, NT, BF16)
        vt_t, vt = chunked("vt", 4, NT, BF16)
        kT = [[VT("kT%d_%d" % (l, g), [128, 640], BF16) for g in range(2)] for l in range(L)]
        Vr = []
        for l in range(L):
            t = sb("Vr%d" % l, [128, 5, 128], BF16)
            Vr.append([V(t[:, i, :], [Buf("Vr%d_%d" % (l, i))]) for i in range(5)])
        swab = VT("swab", [128, 2, 1024], F32)
        Sst = [[VT("Sst%d_%d" % (l, h), [128, 128], F32) for h in range(4)] for l in range(L)]
        Sbf = []
        for i in range(2):
            t = sb("Sbf%d" % i, [128, 9, 128], BF16)
            Sbf.append([V(t[:, c, :], [Buf("Sbf%d_%d" % (i, c))]) for c in range(9)])
        hc_t = sb("halo_c", [128, L, 4, 2], F32)
        halo_c = [[V(hc_t[:, l, c, :], [Buf("hc%d_%d" % (l, c))]) for c in range(4)] for l in range(L)]
        hf_t = sb("halo_f", [128, L, 44, 2], F32)
        halo_f = [[V(hf_t[:, l, c, :], [Buf("hf%d_%d" % (l, c))]) for c in range(44)] for l in range(L)]
        halo_all = [V(hc_t[:], [b for l in range(L) for c in range(4) for b in halo_c[l][c].bufs]),
                    V(hf_t[:], [b for l in range(L) for c in range(44) for b in halo_f[l][c].bufs])]
        identf = VT("identf", [128, 128], F32)
        identb = VT("identb", [128, 128], BF16)
        onesb = VT("onesb", [128, 128], BF16)
        sq_t = sb("sqpool", [128, 3, NT], BF16)
        sqpool = Ring([V(sq_t[:, i, :], [Buf("sq%d" % i)]) for i in range(3)])
        tri = VT("tri", [128, 3, 128], F32)
        gmask = VT("gmask", [128, 512], F32)
        gv = VT("gv", [128, (2 * L + 1) * 8], F32)
        glag = VT("glag", [128, L * 4], F32)
        convw = VT("convw", [128, L, 4, 3], F32)
        fconvw = VT("fconvw", [128, L, 44, 3], F32)
        sinks = VT("sinks", [128, L * 8], F32)
        walpha = VT("walpha", [17, L, 512], F32)
        gaaug = VT("gaaug", [32, 512], F32)
        epsb = VT("epsb", [128, 1], F32)
        NF = 9
        fp_t = sb("fpool", [128, NF, 514], F32)
        fpool = Ring([V(fp_t[:, i, :], [Buf("fp%d" % i)]) for i in range(NF)])
        NB = 20
        bp_t = sb("bpool", [128, NB, 512], BF16)
        bpool = Ring([V(bp_t[:, i, :], [Buf("bp%d" % i)]) for i in range(NB)])
        sc_t = sb("scpool", [128, 2, 1024], F32)
        scV = [V(sc_t[:, i, :], [Buf("sc%d" % i)]) for i in range(2)]
        scpool = Ring(scV)
        ltokb = [scV[tb // 2][:, (tb % 2) * 512:(tb % 2 + 1) * 512] for tb in range(4)]
        ltokh = [V(sc_t[:, tb // 2, :].bitcast(BF16)[:, (tb % 2) * 1024:(tb % 2) * 1024 + 512], scV[tb // 2].bufs)
                 for tb in range(4)]
        trib = VT("trib", [128, 3, 128], BF16)
        pp_t = sb("ppool", [128, 2, 1024], BF16)
        ppool = Ring([V(pp_t[:, i, :], [Buf("pp%d" % i)]) for i in range(2)])
        pt_t = sb("ptpool", [128, 2, 1024], BF16)
        ptpool = Ring([V(pt_t[:, i, :], [Buf("pt%d" % i)]) for i in range(2)])
        kt_t = sb("ktok", [128, 2, 4, 128], BF16)
        ktpool = Ring([(i, [V(kt_t[:, i, tb, :], [Buf("ktok%d_%d" % (i, tb))]) for tb in range(4)]) for i in range(2)])
        NS = 24
        sm_t = sb("small", [128, NS, 8], F32)
        small = Ring([V(sm_t[:, i, :], [Buf("sm%d" % i)]) for i in range(NS)])
        NW = 4
        wr_t = sb("wring", [128, NW, 8, 512], BF16)
        wring = Ring([V(wr_t[:, i, :, :], [Buf("wr%d" % i)]) for i in range(NW)])
        NPS = 7
        psb = [ps("ps%d" % i, [128, 512], F32) for i in range(NPS)]
        psV = [V(psb[i][:], [Buf("ps%d" % i)]) for i in range(NPS)]
        pspool = Ring(psV)
        mixring = Ring(psV[0:5])
        poring = Ring(psV[5:6])
        gatering = Ring(psV[6:7])
        pbt = ps("psbf", [128, 1024], BF16)
        PB = V(pbt[:], [Buf("psbf")])

        def _body():
            for dst, src in ((identf, identf_d), (identb, identb_d), (tri, tri_d), (gmask, gmask_d),
                             (swab, swab_d), (gv, gv_d), (glag, glag_d), (convw, convw_d), (fconvw, fconvw_d),
                             (sinks, sinks_d), (walpha, walpha_d)):
                P.dma(dst, V(src, []))
            P.memset("pool", epsb, EPS)
            P.copy("pool", trib, tri)
            P.memset("pool", onesb, 1.0)
            P.memset("pool", gaaug, 1.0)
            P.memset("pool", halo_all[0], 0.0)
            P.memset("pool", halo_all[1], 0.0)
            for l in range(L):
                for h in range(4):
                    P.memset("pool", Sst[l][h], 0.0)
                for g in range(2):
                    P.memset("pool", kT[l][g], 0.0)
                P.memset("pool", Vr[l][0], 0.0)

            chk(1)
            wbuf = {}
            order = ["w_in", "w_gla_o", "w_conv_o", "w_swa_o", "w_o", "w_up", "w_down"]
            for l in range(L):
                for k in order:
                    r, c = wspec[k]
                    b = Buf("%s_bf%d" % (k, l))
                    wbuf[(k, l)] = b
                    for r0 in range(0, r, 128):
                        P.dma(V(wb_d[k][l, r0:r0 + 128, :], [b]), V(w_d[k][l, r0:r0 + 128, :], []), queue="pool", nowaw=True)

            chk(2)
            def wsrc(k, l, r0, nk, c0, ncol):
                ap = wb_d[k][l, r0 * 128:(r0 + nk) * 128, c0:c0 + ncol].rearrange("(kc p) n -> p kc n", p=128)
                return V(ap, [wbuf[(k, l)]])

            def wload(k, l, r0, nk, c0, ncol):
                slot = wring()
                P.dma(slot[:, 0:nk, 0:ncol], wsrc(k, l, r0, nk, c0, ncol), final=(stage in (61, 62)))
                return slot

            def dump(name, v):
                if dbg and name in dbg_d:
                    P.dma(V(dbg_d[name], [Buf("dbg_" + name)]), v, final=True)

            def rms_stats(srcs, nparts_scale, ring=None):
                pst = (ring or pspool)()
                n = len(srcs)
                for i, s in enumerate(srcs):
                    sq = sqpool()
                    P.act(sq, s, AF.Square)
                    P.mm(pst, onesb, sq, start=(i == 0), stop=(i == n - 1))
                ln = fpool()
                P.act(ln[:, 0:NT], pst, AF.Ln, bias=epsb, scale=nparts_scale)
                r = fpool()
                P.act(r[:, 0:NT], ln[:, 0:NT], AF.Exp, scale=-0.5)
                return r

            def norm_to_uT(gcol0):
                r = rms_stats(hT, 1.0 / D)
                for c in range(8):
                    P.stt(uT[c], hT[c], gv[:, gcol0 + c:gcol0 + c + 1], r[:, 0:NT], ALU.mult, ALU.mult)

            def proj(wslot, col0, ncols_m, kcs=8, rhs_list=None, ring=None):
                rhs_list = rhs_list if rhs_list is not None else uT
                pt = (ring or pspool)()
                for kc in range(kcs):
                    P.mm(pt[0:ncols_m, :], wslot[:, kc, col0:col0 + ncols_m], rhs_list[kc], start=(kc == 0), stop=(kc == kcs - 1))
                return pt

            for t in range(NTILES):
                tok0 = t * NT
                for tb in range(4):
                    xs = scpool()
                    P.dma(xs, V(x_d[tok0 + tb * 128: tok0 + (tb + 1) * 128, :], []))
                    for half in range(2):
                        pt = pspool()
                        for cc in range(4):
                            c = half * 4 + cc
                            P.tr(pt[:, cc * 128:(cc + 1) * 128], xs[:, c * 128:(c + 1) * 128], identf)
                        dst = V(hT_t[:, half * 4:(half + 1) * 4, tb * 128:(tb + 1) * 128],
                                [hT[half * 4 + cc].bufs[0] for cc in range(4)])
                        src = V(pt.ap.rearrange("p (c n) -> p c n", c=4), pt.bufs)
                        P.copy("act" if half == 0 else "dve", dst, src)

                chk(3)
                for l in range(L):
                    first = (t == 0)
                    norm_to_uT(l * 8)
                    if t == 0 and l == 0:
                        dump("uT0", V(uT_t[:, 0, :], uT[0].bufs))

                    chk(4)
                    gring = mixring
                    wsm = wring()
                    P.dma(wsm[:, :, 0:16], wsrc("w_in", l, 0, 8, 2048, 16), nowaw=True)
                    for g in range(2):
                        for d2 in range(2):
                            P.dma(wsm[:, :, 16 + g * 128 + d2 * 64: 16 + g * 128 + (d2 + 1) * 64],
                                  wsrc("w_in", l, 0, 8, 4112 + g * 64, 64), nowaw=True)
                    P.dma(wsm[:, :, 272:400], wsrc("w_in", l, 0, 8, 4240, 128), nowaw=True)
                    pga = gring()
                    for kc in range(8):
                        P.mm(pga[0:16, :], wsm[:, kc, 0:16], uT[kc], start=(kc == 0), stop=(kc == 7))
                    P.copy("act", gaaug[0:16, :], pga[0:16, :])
                    ltok = []
                    for tb in range(4):
                        px = gring()
                        P.mm(px, gaaug[0:17, tb * 128:(tb + 1) * 128], walpha[0:17, l, :])
                        e = ltokb[tb]
                        P.act(e, px, AF.Exp, scale=-1.0)
                        ltok.append(e)
                    for tb in range(4):
                        P.act(ltokh[tb], ltok[tb], AF.Ln, bias=1.0)
                    chk(5)
                    for g in range(2):
                        pk = gring()
                        for kc in range(8):
                            P.mm(pk, wsm[:, kc, 16 + g * 128:16 + (g + 1) * 128], uT[kc], start=(kc == 0), stop=(kc == 7))
                        P.copy("act", kT[l][g][:, 128:640], pk)
                    for tb in range(4):
                        pv = gring()
                        for kc in range(8):
                            P.mm(pv[:, 0:128], uT[kc][:, tb * 128:(tb + 1) * 128], wsm[:, kc, 272:400], start=(kc == 0), stop=(kc == 7))
                        P.copy("act", Vr[l][tb + 1], pv[:, 0:128])
                    w_sq = wload("w_in", l, 0, 8, 3600, 512)
                    for c4 in range(4):
                        pq = proj(w_sq, c4 * 128, 128, ring=gring)
                        P.act(qs[c4], pq, AF.Copy, scale=0.125)
                    chk(8)
                    w_cx = wload("w_in", l, 0, 8, 2064, 512)
                    w_cb = wload("w_in", l, 0, 8, 2576, 512)
                    w_cc = wload("w_in", l, 0, 8, 3088, 512)
                    for c4 in range(4):
                        pcx = proj(w_cx, c4 * 128, 128, ring=gring)
                        cxs = fpool()
                        P.copy("act", cxs[:, 0:NT], pcx)
                        pcc = proj(w_cc, c4 * 128, 128, ring=gring)
                        pbuf = fpool()
                        P.copy("pool", pbuf[:, 0:2], halo_c[l][c4])
                        P.tt("dve", pbuf[:, 2:2 + NT], pcc, cxs[:, 0:NT], ALU.mult)
                        P.copy("pool", halo_c[l][c4], pbuf[:, NT:NT + 2])
                        acc = fpool()
                        P.act(acc[:, 0:NT], pbuf[:, 0:NT], AF.Copy, scale=convw[:, l, c4, 0:1])
                        P.stt(acc[:, 0:NT], pbuf[:, 1:1 + NT], convw[:, l, c4, 1:2], acc[:, 0:NT], ALU.mult, ALU.add)
                        P.stt(acc[:, 0:NT], pbuf[:, 2:2 + NT], convw[:, l, c4, 2:3], acc[:, 0:NT], ALU.mult, ALU.add)
                        pcb = proj(w_cb, c4 * 128, 128, ring=gring)
                        P.tt("dve", cvT[c4], pcb, acc[:, 0:NT], ALU.mult)
                    chk(6)
                    w_gv = wload("w_in", l, 0, 8, 1024, 512)
                    for tb in range(4):
                        pv = gring()
                        for kc in range(8):
                            P.mm(pv, uT[kc][:, tb * 128:(tb + 1) * 128], w_gv[:, kc, :], start=(kc == 0), stop=(kc == 7))
                        P.copy("act", vt[tb], pv)
                    w_gq = wload("w_in", l, 0, 8, 0, 512)
                    w_gk = wload("w_in", l, 0, 8, 512, 512)
                    HS = [slice(hh * 128, (hh + 1) * 128) for hh in range(4)]
                    gl = [dict() for _ in range(4)]

                    def G1(hh):
                        hs = HS[hh]
                        d = gl[hh]
                        pA, pB, pb = gring(), gring(), gring()
                        for tb in range(4):
                            ts_ = slice(tb * 128, (tb + 1) * 128)
                            P.mm(pA[:, ts_], ltokh[tb][:, hs], trib[:, 1, :])
                            P.mm(pB[:, ts_], ltokh[tb][:, hs], trib[:, 2, :])
                            P.mm(pb[:, ts_], ltokh[tb][:, hs], trib[:, 0, :])
                        E1, E2, E3, E4 = fpool(), fpool(), fpool(), fpool()
                        P.act(E1[:, 0:NT], pA, AF.Exp)
                        P.act(E2[:, 0:NT], pA, AF.Exp, scale=-1.0)
                        P.act(E3[:, 0:NT], pB, AF.Exp)
                        P.act(E4[:, 0:NT], pb, AF.Exp)
                        pq = proj(w_gq, hh * 128, 128, ring=gring)
                        d["qa"], d["qb"] = bpool(), bpool()
                        P.stt(d["qa"], pq, 128.0 ** -0.5, E1[:, 0:NT], ALU.mult, ALU.mult)
                        P.stt(d["qb"], pq, 128.0 ** -0.5, E4[:, 0:NT], ALU.mult, ALU.mult)
                        pk = proj(w_gk, hh * 128, 128, ring=gring)
                        d["ka"], d["kb"] = bpool(), bpool()
                        P.tt("dve", d["ka"], pk, E2[:, 0:NT], ALU.mult)
                        P.tt("dve", d["kb"], pk, E3[:, 0:NT], ALU.mult)
                        d["dec"] = small()
                        P.copy("pool", d["dec"], V(E4.ap[:, 0:NT].rearrange("p (c k) -> p c k", c=8)[:, :, 63], E4.bufs))
                        if t == 0 and l == 0 and hh == 0:
                            dump("E4", E4[:, 0:NT])

                    def G2(hh):
                        hs = HS[hh]
                        d = gl[hh]
                        ktok = ktpool()
                        for tb in range(4):
                            P.tr(PB[:, tb * 128:(tb + 1) * 128], d["kb"][:, tb * 128:(tb + 1) * 128], identb)
                        P.copy("act", V(kt_t[:, ktok[0], :, :], [b for tb in range(4) for b in ktok[1][tb].bufs]),
                               V(PB.ap[:, 0:512].rearrange("p (a b) -> p a b", a=4), PB.bufs))
                        ktok = ktok[1]
                        pat = gring()
                        for tb in range(4):
                            ts_ = slice(tb * 128, (tb + 1) * 128)
                            P.mm(pat[:, ts_], d["ka"][:, ts_], d["qa"][:, ts_])
                        d["am"] = bpool()
                        P.tt("dve", d["am"], pat, gmask, ALU.mult)
                        pkv = [gring(), gring()]
                        for c in range(8):
                            tb, r0 = c // 2, (c % 2) * 64
                            P.mm(pkv[c % 2][:, tb * 128:(tb + 1) * 128], ktok[tb][r0:r0 + 64, :], vt[tb][r0:r0 + 64, hs])
                        d["pkv"] = pkv

                    def G3(hh):
                        d = gl[hh]
                        sb_ = Sbf[hh % 2]
                        pkv = d["pkv"]
                        P.copy("dve", sb_[0], Sst[l][hh])
                        for c in range(8):
                            tb = c // 2
                            P.stt(Sst[l][hh], Sst[l][hh], d["dec"][:, c:c + 1], pkv[c % 2][:, tb * 128:(tb + 1) * 128],
                                  ALU.mult, ALU.add)
                            P.copy("dve", sb_[c + 1], Sst[l][hh])

                    def G4(hh):
                        hs = HS[hh]
                        d = gl[hh]
                        sb_ = Sbf[hh % 2]
                        po = poring()
                        for tb in range(4):
                            ts_ = slice(tb * 128, (tb + 1) * 128)
                            P.mm(po[:, ts_], vt[tb][:, hs], d["am"][:, ts_], start=True, stop=False)
                            for c2 in range(2):
                                c = tb * 2 + c2
                                cs = slice(c * 64, (c + 1) * 64)
                                P.mm(po[:, cs], sb_[c], d["qb"][:, cs], start=False, stop=(c2 == 1))
                        d["po"] = po

                    def G5(hh, w_gr):
                        d = gl[hh]
                        po = d["po"]
                        pg = proj(w_gr, hh * 128, 128, ring=gring)
                        sg = fpool()
                        P.act(sg[:, 0:NT], pg, AF.Exp, scale=-1.0)
                        P.act(sg[:, 0:NT], sg[:, 0:NT], AF.Ln, bias=1.0)
                        P.act(sg[:, 0:NT], sg[:, 0:NT], AF.Exp, scale=-1.0)
                        P.stt(sg[:, 0:NT], pg, 1.0, sg[:, 0:NT], ALU.mult, ALU.mult)
                        r = rms_stats([po], 1.0 / 128, ring=gring)
                        tmp = fpool()
                        P.stt(tmp[:, 0:NT], po, glag[:, l * 4 + hh:l * 4 + hh + 1], r[:, 0:NT], ALU.mult, ALU.mult)
                        P.tt("pool", goT[hh], tmp[:, 0:NT], sg[:, 0:NT], ALU.mult)
                        if t == 0 and l == 0 and hh == 0:
                            dump("tmp_o", tmp[:, 0:NT])

                    for hh in range(4):
                        G1(hh)
                    chk(7)
                    w_gr = wload("w_in", l, 0, 8, 1536, 512)
                    G2(0); G3(0)
                    G2(1); G3(1)
                    G4(0); G5(0, w_gr)
                    G2(2); G3(2)
                    G4(1); G5(1, w_gr)
                    G2(3); G3(3)
                    G4(2); G5(2, w_gr)
                    G4(3); G5(3, w_gr)
                    chk(9)
                    sw = {}

                    def SA(i):
                        tb, g = i // 2, i % 2
                        sA, sB = psV[(i % 2) * 2], psV[(i % 2) * 2 + 1]
                        for j in range(4):
                            h = 4 * g + j
                            ch, po_ = h // 2, (h % 2) * 64
                            bank = sA if j % 2 == 0 else sB
                            P.mm(bank[:, (j // 2) * 256:(j // 2 + 1) * 256], qs[ch][po_:po_ + 64, tb * 128:(tb + 1) * 128],
                                 kT[l][g][po_:po_ + 64, tb * 128:tb * 128 + 256])
                        sc = scpool()
                        sc4 = V(sc.ap.rearrange("p (j k) -> p j k", j=4), sc.bufs)
                        sw4 = V(swab.ap[:, g, :].rearrange("p (j k) -> p j k", j=4), swab.bufs)
                        P.tt("dve", sc4[:, 0::2, :], V(sA.ap.rearrange("p (j k) -> p j k", j=2), sA.bufs), sw4[:, 0::2, :], ALU.add)
                        P.tt("dve", sc4[:, 1::2, :], V(sB.ap.rearrange("p (j k) -> p j k", j=2), sB.bufs), sw4[:, 1::2, :], ALU.add)
                        if first and tb == 0:
                            P.ts("pool", sc4[:, :, 0:128], sc4[:, :, 0:128], NEG, ALU.add)
                        sm, sm2, sm3 = small(), small(), small()
                        mx, negm = sm[:, 0:4], sm[:, 4:8]
                        rsum, dd = sm2[:, 0:4], sm2[:, 4:8]
                        es, rinv = sm3[:, 0:4], sm3[:, 4:8]
                        sk_ = sinks[:, l * 8 + g * 4:l * 8 + g * 4 + 4]
                        P.reduce(mx, sc4, ALU.max)
                        P.tt("dve", mx, mx, sk_, ALU.max)
                        P.ts("dve", negm, mx, -1.0, ALU.mult)
                        pn = ppool()
                        for j in range(4):
                            P.act(pn[:, j * 256:(j + 1) * 256], sc[:, j * 256:(j + 1) * 256], AF.Exp,
                                  bias=negm[:, j:j + 1], accum=rsum[:, j:j + 1])
                        P.tt("dve", dd, sk_, mx, ALU.subtract)
                        P.act(es, dd, AF.Exp)
                        P.tt("dve", es, es, rsum, ALU.add)
                        P.recip(rinv, es)
                        pn4 = V(pn.ap.rearrange("p (j k) -> p j k", j=4), pn.bufs)
                        P.tt("dve", pn4, pn4, V(rinv.ap.unsqueeze(2).broadcast_to([128, 4, 256]), rinv.bufs), ALU.mult)
                        sw[i] = pn

                    def SB(i):
                        tb, g = i // 2, i % 2
                        pn = sw.pop(i)
                        posw = psV[4 + (tb % 2)]
                        for j in range(4):
                            for kb in range(2):
                                P.tr(PB[:, (kb * 4 + j) * 128:(kb * 4 + j + 1) * 128],
                                     pn[:, j * 256 + kb * 128: j * 256 + (kb + 1) * 128], identb)
                        ptt = ptpool()
                        P.copy("act", ptt, PB)
                        for j in range(4):
                            h = 4 * g + j
                            ch, po_ = h // 2, (h % 2) * 64
                            for kb in range(2):
                                P.mm(posw[po_:po_ + 64, ch * 128:(ch + 1) * 128], Vr[l][tb + kb][:, g * 64:(g + 1) * 64],
                                     ptt[:, (kb * 4 + j) * 128:(kb * 4 + j + 1) * 128], start=(kb == 0), stop=(kb == 1))
                        if g == 1:
                            dst = V(ar_t[:, 8:12, tb * 128:(tb + 1) * 128], [oswT[c4].bufs[0] for c4 in range(4)])
                            P.copy("act", dst, V(posw.ap.rearrange("p (c n) -> p c n", c=4), posw.bufs))

                    SA(0)
                    for i in range(8):
                        if i + 1 < 8:
                            SA(i + 1)
                        SB(i)
                    for g in range(2):
                        P.copy("pool", kT[l][g][:, 0:128], kT[l][g][:, 512:640])
                    P.copy("pool", Vr[l][0], Vr[l][4])
                    if t == 0 and l == 0:
                        dump("go0", V(ar_t[:, 0, :], goT[0].bufs))
                        dump("cv0", V(ar_t[:, 4, :], cvT[0].bufs))
                        dump("osw0", V(ar_t[:, 8, :], oswT[0].bufs))


                    bo_names = ["w_gla_o", "w_conv_o", "w_swa_o"]
                    brs = [goT, cvT, oswT]
                    for b in range(3):
                        for mgp in range(2):
                            wg = wload("w_in", l, 0, 8, 4368 + b * 1024 + mgp * 512, 512)
                            for m4 in range(4):
                                pgt = proj(wg, m4 * 128, 128, ring=gatering)
                                e = fpool()
                                P.act(e[:, 0:NT], pgt, AF.Exp, scale=-1.0)
                                P.act(e[:, 0:NT], e[:, 0:NT], AF.Ln, bias=1.0)
                                P.act(sigb[b * 8 + mgp * 4 + m4], e[:, 0:NT], AF.Exp, scale=-1.0)
                    for mgp in range(2):
                        wbo = [wload(bo_names[b], l, 0, 4, mgp * 512, 512) for b in range(3)]
                        for m4 in range(4):
                            m = mgp * 4 + m4
                            terms = []
                            for b in range(3):
                                py = proj(wbo[b], m4 * 128, 128, kcs=4, rhs_list=brs[b])
                                tm = fpool()
                                P.tt("dve", tm[:, 0:NT], py, sigb[b * 8 + m], ALU.mult)
                                terms.append(tm)
                            P.tt("pool", terms[0][:, 0:NT], terms[0][:, 0:NT], terms[1][:, 0:NT], ALU.add)
                            P.tt("pool", mgT[m], terms[0][:, 0:NT], terms[2][:, 0:NT], ALU.add)
                    chk(11)
                    for mgp in range(2):
                        wo = wload("w_o", l, 0, 8, mgp * 512, 512)
                        for m4 in range(4):
                            m = mgp * 4 + m4
                            pt = proj(wo, m4 * 128, 128, rhs_list=mgT)
                            P.tt("dve", hT[m], pt, hT[m], ALU.add)
                    if t == 0 and l == 0:
                        dump("h1", V(hT_t[:, 0, :], hT[0].bufs))

                    chk(12)
                    norm_to_uT((L + l) * 8)
                    for pg in range(6):
                        npair = 4 if pg < 5 else 2
                        wa = wload("w_up", l, 0, 8, pg * 512, npair * 128)
                        wb = wload("w_up", l, 0, 8, DFF + pg * 512, npair * 128)
                        for pi in range(npair):
                            c = pg * 4 + pi
                            accs = []
                            for (wsl, cidx) in ((wa, c), (wb, c + 22)):
                                ph = proj(wsl, pi * 128, 128)
                                hb = fpool()
                                P.copy("pool", hb[:, 0:2], halo_f[l][cidx])
                                P.copy("act", hb[:, 2:2 + NT], ph)
                                P.copy("pool", halo_f[l][cidx], hb[:, NT:NT + 2])
                                acc = fpool()
                                P.act(acc[:, 0:NT], ph, AF.Copy, scale=fconvw[:, l, cidx, 2:3])
                                P.stt(acc[:, 0:NT], hb[:, 0:NT], fconvw[:, l, cidx, 0:1], acc[:, 0:NT], ALU.mult, ALU.add)
                                P.stt(acc[:, 0:NT], hb[:, 1:1 + NT], fconvw[:, l, cidx, 1:2], acc[:, 0:NT], ALU.mult, ALU.add)
                                accs.append(acc)
                            sa = fpool()
                            P.act(sa[:, 0:NT], accs[0][:, 0:NT], AF.Silu)
                            P.tt("dve", gT[c], sa[:, 0:NT], accs[1][:, 0:NT], ALU.mult)
                    chk(13)
                    for mgp in range(2):
                        banks = [pspool() for _ in range(4)]
                        for (k0, nk) in ((0, 8), (8, 8), (16, 6)):
                            wd = wload("w_down", l, k0, nk, mgp * 512, 512)
                            for m4 in range(4):
                                for kk in range(nk):
                                    k = k0 + kk
                                    P.mm(banks[m4], wd[:, kk, m4 * 128:(m4 + 1) * 128], gT[k], start=(k == 0), stop=(k == 21))
                        for m4 in range(4):
                            m = mgp * 4 + m4
                            P.tt("dve", hT[m], banks[m4], hT[m], ALU.add)
                    if t == 0 and l == 0:
                        dump("h2", V(hT_t[:, 0, :], hT[0].bufs))

                chk(14)
                if final_norm:
                    r = rms_stats(hT, 1.0 / D)
                ofm = []
                for c in range(8):
                    o = fpool()
                    if final_norm:
                        P.stt(o[:, 0:NT], hT[c], gv[:, 2 * L * 8 + c:2 * L * 8 + c + 1], r[:, 0:NT], ALU.mult, ALU.mult)
                    else:
                        P.copy("pool", o[:, 0:NT], hT[c])
                    ofm.append(o)
                for tb in range(4):
                    xo = scpool()
                    for half in range(2):
                        pt = pspool()
                        for cc in range(4):
                            c = half * 4 + cc
                            P.tr(pt[:, cc * 128:(cc + 1) * 128], ofm[c][:, tb * 128:(tb + 1) * 128], identf)
                        P.copy("act" if half == 0 else "dve", xo[:, half * 512:(half + 1) * 512], pt)
                    P.dma(V(out_d[tok0 + tb * 128: tok0 + (tb + 1) * 128, :], [Buf("out%d_%d" % (t, tb))]), xo, final=True)

        try:
            _body()
        except _Stop:
            pass
        P.emit()
        nc_marks = P.marks
        nc_counts = {e: len(v) for e, v in P.streams.items()}
    _MARKS[id(nc)] = (nc_marks, nc_counts)
    return nc


_WNAMES = ["w_in", "w_gla_o", "w_conv_o", "w_swa_o", "w_o", "w_up", "w_down"]


def make_in_maps(x, params, L):
    consts = _consts()
    lay = _layout_params(L, params["g_mix"], params["g_ffn"], params["g_final"], params["gla_norm_g"], params["conv_w"],
                         params["ffn_conv_w"], params["swa_sinks"], params["gla_w_alpha"], params["gla_b_alpha"])
    shared = {}
    shared.update(consts)
    shared.update(lay)
    for k in _WNAMES:
        shared[k] = np.ascontiguousarray(params[k][:L], dtype=np.float32)
    maps = []
    for b in range(x.shape[0]):
        m = dict(shared)
        m["x"] = np.ascontiguousarray(x[b], dtype=np.float32)
        maps.append(m)
    return maps


_NC_CACHE = {}


def kernel(x, g_mix, w_in, gla_w_alpha, gla_b_alpha, gla_norm_g, conv_w, swa_sinks, w_gla_o, w_conv_o, w_swa_o, w_o,
           g_ffn, w_up, ffn_conv_w, w_down, g_final):
    x = np.asarray(x)
    B, S, _ = x.shape
    L = int(np.asarray(g_mix).shape[0])
    params = dict(g_mix=g_mix, w_in=w_in, gla_w_alpha=gla_w_alpha, gla_b_alpha=gla_b_alpha, gla_norm_g=gla_norm_g,
                  conv_w=conv_w, swa_sinks=swa_sinks, w_gla_o=w_gla_o, w_conv_o=w_conv_o, w_swa_o=w_swa_o, w_o=w_o,
                  g_ffn=g_ffn, w_up=w_up, ffn_conv_w=ffn_conv_w, w_down=w_down, g_final=g_final)
    params = {k: np.asarray(v, dtype=np.float32) for k, v in params.items()}
    key = (S, L)
    if key not in _NC_CACHE:
        _NC_CACHE[key] = build_nc(S, L)
    nc = _NC_CACHE[key]
    in_maps = make_in_maps(x, params, L)
    res = run_bass_kernel_spmd(nc, in_maps, core_ids=list(range(B)))
    out = np.stack([np.asarray(r["out"]) for r in res.results], axis=0)
    return out.astype(np.float32)
```

```python
from contextlib import ExitStack
import numpy as np
import ml_dtypes
import concourse.bass as bass
import concourse.mybir as mybir
from concourse.bass_utils import run_bass_kernel_spmd

F32 = mybir.dt.float32
BF16 = mybir.dt.bfloat16
AF = mybir.ActivationFunctionType
ALU = mybir.AluOpType
AX = mybir.AxisListType

ENGS = ("pe", "act", "dve", "pool", "sp")
SAME_ENGINE_SYNC = {"pe": False, "act": True, "dve": True, "pool": True, "sp": True}

D = 1024
DIN = 7440
DFF = 2816
NT = 512
EPS = 1e-6
NEG = -30000.0


class Buf:
    __slots__ = ("name", "last_writer", "readers", "sem", "sem_cnt")

    def __init__(self, name):
        self.name = name
        self.last_writer = None
        self.readers = []
        self.sem = None
        self.sem_cnt = 0


class V:
    __slots__ = ("ap", "bufs")

    def __init__(self, ap, bufs):
        self.ap = ap
        self.bufs = tuple(bufs)

    def __getitem__(self, k):
        return V(self.ap[k], self.bufs)


class Op:
    __slots__ = ("eng", "fn", "idx", "pidx", "preds", "succs", "waits", "dma_waits", "signal", "signo", "is_dma",
                 "sem", "sem_val", "dur", "tab", "nbytes", "npend", "ready", "finish", "tag", "start", "prio")

    def __init__(self, eng, fn, is_dma=False):
        self.eng = eng
        self.fn = fn
        self.idx = -1
        self.pidx = -1
        self.preds = []
        self.succs = []
        self.waits = []
        self.dma_waits = []
        self.signal = False
        self.signo = 0
        self.is_dma = is_dma
        self.sem = None
        self.sem_val = 0
        self.dur = 100.0
        self.tab = None
        self.nbytes = 0
        self.npend = 0
        self.ready = 0.0
        self.finish = 0.0
        self.tag = None
        self.start = 0.0
        self.prio = 0.0


_ACT_SETS = {}


def _act_set(func):
    if func in (AF.Exp, AF.Ln):
        return "explog"
    if func == AF.Silu:
        return "silu"
    if func == AF.Sigmoid:
        return "sigmoid"
    return None


def _fsize(ap):
    n = 1
    for d in ap.shape[1:]:
        n *= d
    return n


class Prog:
    REORDER = ("pe", "act", "dve")

    def __init__(self, nc, stack):
        self.nc = nc
        self.stack = stack
        self.all_ops = []
        self.streams = {e: [] for e in ENGS}
        self.esem = {}
        for e in ("pe", "act", "dve", "pool"):
            self.esem[e] = stack.enter_context(nc.semaphore("es_" + e))
        self.final_dma = []
        self.marks = []
        self.sched = True
        self.prio_mode = "bl"
        self.cur_tag = None

    def mark(self, label):
        self.marks.append((label, len(self.streams["pe"])))
        self.cur_tag = (label, len(self.marks))

    def new_sem(self, name):
        return self.stack.enter_context(self.nc.semaphore(name))

    def op(self, eng, fn, reads=(), writes=(), dma=False, sem_buf=None, nowaw=False, dur=100.0, tab=None, nbytes=0):
        X = Op(eng, fn, is_dma=dma)
        X.pidx = len(self.all_ops)
        X.dur = dur
        X.tab = tab
        X.nbytes = nbytes
        X.tag = self.cur_tag
        if dma:
            b = sem_buf
            if b.sem is None:
                b.sem = self.new_sem("ds_" + b.name)
            b.sem_cnt += 16
            X.sem = b.sem
            X.sem_val = b.sem_cnt
        deps = []
        for r in reads:
            deps.append(r.last_writer)
        for w in writes:
            if not nowaw:
                deps.append(w.last_writer)
            deps.extend(w.readers)
        seen = set()
        for Y in deps:
            if Y is None or Y is X or id(Y) in seen:
                continue
            seen.add(id(Y))
            X.preds.append(Y)
        for r in reads:
            r.readers.append(X)
        for w in writes:
            w.last_writer = X
            w.readers = []
        self.all_ops.append(X)
        self.streams[eng].append(X)
        return X

    def dma(self, out, in_, queue="sp", final=False, nowaw=False):
        nb = 1
        for d in out.ap.shape:
            nb *= d
        nb *= 2 if out.ap.dtype == BF16 else 4
        X = self.op(queue, lambda e: e.dma_start(out=out.ap, in_=in_.ap), reads=in_.bufs, writes=out.bufs,
                    dma=True, sem_buf=out.bufs[0], nowaw=nowaw, dur=60.0, nbytes=nb)
        if final:
            self.final_dma.append(X)
        return X

    def mm(self, out, lhsT, rhs, start=True, stop=True):
        n = _fsize(rhs.ap)
        passes = 4 if rhs.ap.dtype == F32 else 1
        return self.op("pe", lambda e: e.matmul(out.ap, lhsT.ap, rhs.ap, start=start, stop=stop),
                       reads=lhsT.bufs + rhs.bufs, writes=out.bufs, dur=passes * max(n, 64) / 2.4 + 12)

    def tr(self, out, in_, ident):
        return self.op("pe", lambda e: e.transpose(out.ap, in_.ap, ident.ap), reads=in_.bufs + ident.bufs, writes=out.bufs,
                       dur=110.0)

    def act(self, out, in_, func, bias=None, scale=None, accum=None):
        reads = list(in_.bufs)
        kw = {}
        if bias is not None:
            if isinstance(bias, V):
                reads += bias.bufs
                kw["bias"] = bias.ap
            else:
                kw["bias"] = bias
        if scale is not None:
            if isinstance(scale, V):
                reads += scale.bufs
                kw["scale"] = scale.ap
            else:
                kw["scale"] = scale
        writes = list(out.bufs)
        d = 180.0 + _fsize(in_.ap) / 1.2
        if accum is not None:
            writes += accum.bufs
            kw["accum_out"] = accum.ap
            d += 100
        return self.op("act", lambda e: e.activation(out=out.ap, in_=in_.ap, func=func, **kw), reads=reads, writes=writes,
                       dur=d, tab=_act_set(func))

    def copy(self, eng, out, in_):
        n = _fsize(in_.ap)
        if eng == "act":
            return self.op("act", lambda e: e.copy(out.ap, in_.ap), reads=in_.bufs, writes=out.bufs, dur=180.0 + n / 1.2)
        d = (100.0 + 1.15 * n) if eng == "dve" else (350.0 + 1.0 * n)
        return self.op(eng, lambda e: e.tensor_copy(out.ap, in_.ap), reads=in_.bufs, writes=out.bufs, dur=d)

    def tt(self, eng, out, in0, in1, op):
        n = _fsize(in0.ap)
        d = (100.0 + 1.15 * n) if eng == "dve" else (150.0 + 2.2 * n)
        return self.op(eng, lambda e: e.tensor_tensor(out=out.ap, in0=in0.ap, in1=in1.ap, op=op),
                       reads=in0.bufs + in1.bufs, writes=out.bufs, dur=d)

    def ts(self, eng, out, in0, s1, op0, s2=None, op1=None):
        reads = list(in0.bufs)
        a1 = s1
        if isinstance(s1, V):
            reads += s1.bufs
            a1 = s1.ap
        a2 = s2
        if isinstance(s2, V):
            reads += s2.bufs
            a2 = s2.ap
        n = _fsize(in0.ap)
        d = (100.0 + 1.15 * n) if eng == "dve" else (300.0 + 11.0 * n)
        if op1 is None:
            return self.op(eng, lambda e: e.tensor_scalar(out=out.ap, in0=in0.ap, scalar1=a1, scalar2=None, op0=op0),
                           reads=reads, writes=out.bufs, dur=d)
        return self.op(eng, lambda e: e.tensor_scalar(out=out.ap, in0=in0.ap, scalar1=a1, scalar2=a2, op0=op0, op1=op1),
                       reads=reads, writes=out.bufs, dur=d)

    def stt(self, out, in0, scalar, in1, op0, op1):
        reads = list(in0.bufs) + list(in1.bufs)
        sc = scalar
        if isinstance(scalar, V):
            reads += scalar.bufs
            sc = scalar.ap
        return self.op("dve", lambda e: e.scalar_tensor_tensor(out=out.ap, in0=in0.ap, scalar=sc, in1=in1.ap, op0=op0, op1=op1),
                       reads=reads, writes=out.bufs, dur=120.0 + 1.2 * _fsize(in0.ap))

    def reduce(self, out, in_, op, axis=AX.X):
        return self.op("dve", lambda e: e.tensor_reduce(out=out.ap, in_=in_.ap, axis=axis, op=op), reads=in_.bufs, writes=out.bufs,
                       dur=100.0 + 1.1 * _fsize(in_.ap))

    def recip(self, out, in_):
        return self.op("dve", lambda e: e.reciprocal(out.ap, in_.ap), reads=in_.bufs, writes=out.bufs,
                       dur=100.0 + 8.4 * _fsize(in_.ap))

    def memset(self, eng, out, val):
        return self.op(eng, lambda e: e.memset(out.ap, val), writes=out.bufs, dur=200.0)

    def schedule(self):
        import heapq
        ops = self.all_ops
        for X in ops:
            X.succs = []
        for X in ops:
            for Y in X.preds:
                Y.succs.append(X)
        fixed_prev = {}
        extra = {}
        for X in ops:
            if X.eng not in self.REORDER:
                pv = fixed_prev.get(X.eng)
                if pv is not None:
                    extra[id(X)] = pv
                    pv.succs.append(X)
                fixed_prev[X.eng] = X
        for X in ops:
            X.npend = len(X.preds) + (1 if id(X) in extra else 0)
            X.ready = 0.0
        if self.prio_mode == "bl":
            bl = {}
            for X in reversed(ops):
                m = 0.0
                for Z in X.succs:
                    v = bl[id(Z)] + 200.0
                    if v > m:
                        m = v
                bl[id(X)] = m + (X.dur if not X.is_dma else X.nbytes / 260.0 + 2000.0)
            for X in ops:
                X.prio = -bl[id(X)]
        else:
            for X in ops:
                X.prio = float(X.pidx)
        HOP = 200.0
        free_at = {e: 0.0 for e in ENGS}
        pending = {e: [] for e in ENGS}
        avail = {e: [] for e in ENGS}
        last_tab = {"act": None}
        dma_free = [0.0]
        for X in ops:
            if X.npend == 0:
                heapq.heappush(pending[X.eng], (0.0, X.pidx, X))
        order = {e: [] for e in ENGS}
        nleft = len(ops)
        while nleft:
            best = None
            for e in ENGS:
                T = free_at[e]
                pq, av = pending[e], avail[e]
                while pq and pq[0][0] <= T:
                    r, pi, X = heapq.heappop(pq)
                    heapq.heappush(av, (X.prio, pi, X))
                if av:
                    st = T
                elif pq:
                    st = pq[0][0]
                else:
                    continue
                if best is None or st < best[0]:
                    best = (st, e)
            st, e = best
            if avail[e]:
                pr, pi, X = heapq.heappop(avail[e])
            else:
                r, pi, X = heapq.heappop(pending[e])
            d = X.dur
            if e == "act" and X.tab is not None and X.tab != last_tab["act"]:
                d += 1300.0
                last_tab["act"] = X.tab
            if X.is_dma:
                free_at[e] = st + d
                t0 = max(dma_free[0], st + d)
                t1 = t0 + X.nbytes / 260.0
                dma_free[0] = t1
                X.finish = t1 + 2000.0
            else:
                X.finish = st + d
                free_at[e] = X.finish
            X.start = st
            order[e].append(X)
            nleft -= 1
            for Z in X.succs:
                Z.npend -= 1
                rt = X.finish + (HOP if Z.eng != X.eng or X.is_dma else 60.0)
                if rt > Z.ready:
                    Z.ready = rt
                if Z.npend == 0:
                    heapq.heappush(pending[Z.eng], (Z.ready, Z.pidx, Z))
        self.streams = order
        self.est_ns = max(free_at.values())

    def resolve(self):
        for e in ENGS:
            for i, X in enumerate(self.streams[e]):
                X.idx = i
        waited = {e: {f: -1 for f in ENGS} for e in ENGS}
        waited_dma = {e: {} for e in ENGS}
        for e in ENGS:
            for X in self.streams[e]:
                best = {}
                for Y in X.preds:
                    if Y.is_dma:
                        cur = waited_dma[e].get(Y.sem, 0)
                        if cur < Y.sem_val:
                            waited_dma[e][Y.sem] = Y.sem_val
                            X.dma_waits.append((Y.sem, Y.sem_val))
                        continue
                    if Y.eng == e:
                        assert Y.idx < X.idx, "same-engine order violated"
                        if not SAME_ENGINE_SYNC[e]:
                            continue
                    cur = best.get(Y.eng)
                    if cur is None or Y.idx > cur.idx:
                        best[Y.eng] = Y
                for f, Y in best.items():
                    if waited[e][f] >= Y.idx:
                        continue
                    waited[e][f] = Y.idx
                    Y.signal = True
                    X.waits.append(Y)

    def emit(self):
        nc = self.nc
        if self.sched:
            self.schedule()
        self.resolve()
        for e in ("pe", "act", "dve", "pool"):
            n = 0
            for X in self.streams[e]:
                if X.signal:
                    n += 1
                    X.signo = n
        with nc.Block() as block:
            def make(e):
                def body(eng):
                    for X in self.streams[e]:
                        for Y in X.waits:
                            eng.wait_ge(self.esem[Y.eng], Y.signo)
                        for (s, v) in X.dma_waits:
                            eng.wait_ge(s, v)
                        ins = X.fn(eng)
                        if X.is_dma:
                            ins.then_inc(X.sem, 16)
                        elif X.signal:
                            ins.then_inc(self.esem[e], 1)
                    if e == "sp":
                        for X in self.final_dma:
                            eng.wait_ge(X.sem, X.sem_val)
                return body
            block.tensor(make("pe"))
            block.scalar(make("act"))
            block.vector(make("dve"))
            block.gpsimd(make("pool"))
            block.sync(make("sp"))


class Ring:
    def __init__(self, items):
        self.items = items
        self.i = 0

    def __call__(self):
        v = self.items[self.i % len(self.items)]
        self.i += 1
        return v


def _consts():
    c = {}
    c["identf"] = np.eye(128, dtype=np.float32)
    c["identb"] = np.eye(128, dtype=np.float32).astype(ml_dtypes.bfloat16)
    c["onesf"] = np.ones((128, 128), np.float32)
    j = np.arange(128)[:, None]
    i = np.arange(128)[None, :]
    same = (j // 64) == (i // 64)
    tri = (same & (j <= i)).astype(np.float32)
    ref = (i // 64) * 64 + 31
    last = (i // 64) * 64 + 63
    t_ref = (same & (j <= ref)).astype(np.float32)
    t_last = (same & (j <= last)).astype(np.float32)
    sc = -1.0 / 16.0
    T0 = sc * tri
    T1 = sc * (tri - t_ref)
    T2 = sc * (t_last - tri)
    c["tri"] = np.ascontiguousarray(np.stack([T0, T1, T2], axis=1)).astype(np.float32)
    c["gmask"] = np.ascontiguousarray(np.tile(tri, (1, 4))).astype(np.float32)
    slopes = np.exp2(-(np.arange(1, 9, dtype=np.float64))).astype(np.float32)
    iq = np.arange(128)[:, None]
    jk = np.arange(256)[None, :]
    dist = 128 + iq - jk
    valid = (dist >= 0) & (dist < 128)
    bias = np.zeros((128, 2, 4, 256), np.float32)
    for g in range(2):
        for jh in range(4):
            h = 4 * g + jh
            bias[:, g, jh, :] = np.where(valid, -slopes[h] * dist.astype(np.float32), NEG)
    c["swab"] = np.ascontiguousarray(bias.reshape(128, 2, 1024))
    return c


def _chunkcols(v):
    n = v.shape[0] // 128
    return np.ascontiguousarray(v.reshape(n, 128).T)


def _layout_params(L, g_mix, g_ffn, g_final, gla_norm_g, conv_w, ffn_conv_w, swa_sinks, gla_w_alpha, gla_b_alpha):
    gv = np.concatenate([_chunkcols(g_mix[l]) for l in range(L)] + [_chunkcols(g_ffn[l]) for l in range(L)]
                        + [_chunkcols(g_final)], axis=1).astype(np.float32)
    glag = np.concatenate([_chunkcols(gla_norm_g[l]) for l in range(L)], axis=1).astype(np.float32)
    cw = np.stack([np.stack([_chunkcols(conv_w[l, k]) for k in range(3)], axis=2) for l in range(L)], axis=1)
    fw = np.stack([np.stack([_chunkcols(ffn_conv_w[l, k]) for k in range(3)], axis=2) for l in range(L)], axis=1)
    sinks = np.ascontiguousarray(np.broadcast_to(swa_sinks[:L].reshape(1, L * 8), (128, L * 8))).astype(np.float32)
    wal = np.concatenate([gla_w_alpha[:L], gla_b_alpha[:L, None, :]], axis=1).astype(np.float32)
    wal = np.ascontiguousarray(np.transpose(wal, (1, 0, 2)))
    return {"gv": gv, "glag": glag, "convw": np.ascontiguousarray(cw.astype(np.float32)),
            "fconvw": np.ascontiguousarray(fw.astype(np.float32)), "sinks": sinks, "walpha": wal}


class _Stop(Exception):
    pass


_MARKS = {}


def build_nc(S, L, final_norm=True, dbg=None, stage=None):
    assert S % NT == 0
    NTILES = S // NT
    nc = bass.Bass("TRN2", target_bir_lowering=False)

    def dram(name, shape, dt, kind="ExternalInput"):
        return nc.dram_tensor(name, list(shape), dt, kind=kind).ap()

    x_d = dram("x", [S, D], F32)
    out_d = dram("out", [S, D], F32, kind="ExternalOutput")
    wspec = {"w_in": (D, DIN), "w_gla_o": (512, D), "w_conv_o": (512, D), "w_swa_o": (512, D),
             "w_o": (D, D), "w_up": (D, 2 * DFF), "w_down": (DFF, D)}
    w_d = {k: dram(k, [L, r, c], F32) for k, (r, c) in wspec.items()}
    wb_d = {k: dram(k + "_bf", [L, r, c], BF16, kind="Internal") for k, (r, c) in wspec.items()}
    gv_d = dram("gv", [128, (2 * L + 1) * 8], F32)
    glag_d = dram("glag", [128, L * 4], F32)
    convw_d = dram("convw", [128, L, 4, 3], F32)
    fconvw_d = dram("fconvw", [128, L, 44, 3], F32)
    sinks_d = dram("sinks", [128, L * 8], F32)
    walpha_d = dram("walpha", [17, L, 512], F32)
    identf_d = dram("identf", [128, 128], F32)
    identb_d = dram("identb", [128, 128], BF16)
    onesf_d = dram("onesf", [128, 128], F32)
    tri_d = dram("tri", [128, 3, 128], F32)
    gmask_d = dram("gmask", [128, 512], F32)
    swab_d = dram("swab", [128, 2, 1024], F32)
    dbg_d = {}
    if dbg:
        for name, (shape, dt) in dbg.items():
            dbg_d[name] = dram("dbg_" + name, shape, dt, kind="ExternalOutput")

    with ExitStack() as st:
        P = Prog(nc, st)

        def chk(n):
            P.mark(n)
            if stage is not None and stage == n:
                raise _Stop()

        def sb(name, shape, dt):
            return st.enter_context(nc.sbuf_tensor("s_" + name, list(shape), dt))

        def ps(name, shape, dt):
            return st.enter_context(nc.psum_tensor(name, list(shape), dt))

        def VT(name, shape, dt):
            t = sb(name, shape, dt)
            return V(t[:], [Buf(name)])

        def chunked(name, n, cols, dt):
            t = sb(name, [128, n, cols], dt)
            return t, [V(t[:, i, :], [Buf("%s%d" % (name, i))]) for i in range(n)]

        hT_t, hT = chunked("hT", 8, NT, F32)
        uT_t, uT = chunked("uT", 8, NT, BF16)
        ar_t, ar = chunked("arena", 22, NT, BF16)
        goT, cvT, oswT, mgT, gT = ar[0:4], ar[4:8], ar[8:12], ar[12:20], ar
        qs_t, qs = chunked("qs", 4, NT, BF16)
        sg_t, sigb = chunked("sigb", 24, NT, BF16)
        vt_t, vt = chunked("vt", 4, NT, BF16)
        kT = [[VT("kT%d_%d" % (l, g), [128, 640], BF16) for g in range(2)] for l in range(L)]
        Vr = []
        for l in range(L):
            t = sb("Vr%d" % l, [128, 5, 128], BF16)
            Vr.append([V(t[:, i, :], [Buf("Vr%d_%d" % (l, i))]) for i in range(5)])
        swab = VT("swab", [128, 2, 1024], F32)
        Sst = [[VT("Sst%d_%d" % (l, h), [128, 128], F32) for h in range(4)] for l in range(L)]
        Sbf = []
        for i in range(2):
            t = sb("Sbf%d" % i, [128, 9, 128], BF16)
            Sbf.append([V(t[:, c, :], [Buf("Sbf%d_%d" % (i, c))]) for c in range(9)])
        hc_t = sb("halo_c", [128, L, 4, 2], F32)
        halo_c = [[V(hc_t[:, l, c, :], [Buf("hc%d_%d" % (l, c))]) for c in range(4)] for l in range(L)]
        hf_t = sb("halo_f", [128, L, 44, 2], F32)
        halo_f = [[V(hf_t[:, l, c, :], [Buf("hf%d_%d" % (l, c))]) for c in range(44)] for l in range(L)]
        halo_all = [V(hc_t[:], [b for l in range(L) for c in range(4) for b in halo_c[l][c].bufs]),
                    V(hf_t[:], [b for l in range(L) for c in range(44) for b in halo_f[l][c].bufs])]
        identf = VT("identf", [128, 128], F32)
        identb = VT("identb", [128, 128], BF16)
        onesb = VT("onesb", [128, 128], BF16)
        sq_t = sb("sqpool", [128, 3, NT], BF16)
        sqpool = Ring([V(sq_t[:, i, :], [Buf("sq%d" % i)]) for i in range(3)])
        tri = VT("tri", [128, 3, 128], F32)
        gmask = VT("gmask", [128, 512], F32)
        gv = VT("gv", [128, (2 * L + 1) * 8], F32)
        glag = VT("glag", [128, L * 4], F32)
        convw = VT("convw", [128, L, 4, 3], F32)
        fconvw = VT("fconvw", [128, L, 44, 3], F32)
        sinks = VT("sinks", [128, L * 8], F32)
        walpha = VT("walpha", [17, L, 512], BF16)
        gaaug = VT("gaaug", [32, 512], BF16)
        epsb = VT("epsb", [128, 1], F32)
        NF = 9
        fp_t = sb("fpool", [128, NF, 514], F32)
        fpool = Ring([V(fp_t[:, i, :], [Buf("fp%d" % i)]) for i in range(NF)])
        NB = 20
        bp_t = sb("bpool", [128, NB, 512], BF16)
        bpool = Ring([V(bp_t[:, i, :], [Buf("bp%d" % i)]) for i in range(NB)])
        sc_t = sb("scpool", [128, 2, 1024], F32)
        scV = [V(sc_t[:, i, :], [Buf("sc%d" % i)]) for i in range(2)]
        scpool = Ring(scV)
        ltokb = [scV[tb // 2][:, (tb % 2) * 512:(tb % 2 + 1) * 512] for tb in range(4)]
        ltokh = [V(sc_t[:, tb // 2, :].bitcast(BF16)[:, (tb % 2) * 1024:(tb % 2) * 1024 + 512], scV[tb // 2].bufs)
                 for tb in range(4)]
        trib = VT("trib", [128, 3, 128], BF16)
        pp_t = sb("ppool", [128, 2, 1024], BF16)
        ppool = Ring([V(pp_t[:, i, :], [Buf("pp%d" % i)]) for i in range(2)])
        pt_t = sb("ptpool", [128, 2, 1024], BF16)
        ptpool = Ring([V(pt_t[:, i, :], [Buf("pt%d" % i)]) for i in range(2)])
        kt_t = sb("ktok", [128, 2, 4, 128], BF16)
        ktpool = Ring([(i, [V(kt_t[:, i, tb, :], [Buf("ktok%d_%d" % (i, tb))]) for tb in range(4)]) for i in range(2)])
        NS = 24
        sm_t = sb("small", [128, NS, 8], F32)
        small = Ring([V(sm_t[:, i, :], [Buf("sm%d" % i)]) for i in range(NS)])
        NW = 4
        wr_t = sb("wring", [128, NW, 8, 512], BF16)
        wring = Ring([V(wr_t[:, i, :, :], [Buf("wr%d" % i)]) for i in range(NW)])
        NPS = 7
        psb = [ps("ps%d" % i, [128, 512], F32) for i in range(NPS)]
        psV = [V(psb[i][:], [Buf("ps%d" % i)]) for i in range(NPS)]
        pspool = Ring(psV)
        mixring = Ring(psV[0:5])
        poring = Ring(psV[5:6])
        gatering = Ring(psV[6:7])
        pbt = ps("psbf", [128, 1024], BF16)
        PB = V(pbt[:], [Buf("psbf")])

        def _body():
            for dst, src in ((identf, identf_d), (identb, identb_d), (tri, tri_d), (gmask, gmask_d),
                             (swab, swab_d), (gv, gv_d), (glag, glag_d), (convw, convw_d), (fconvw, fconvw_d),
                             (sinks, sinks_d)):
                P.dma(dst, V(src, []))
            P.dma(walpha, V(walpha_d, []), queue="pool")
            P.memset("pool", epsb, EPS)
            P.copy("pool", trib, tri)
            P.memset("pool", onesb, 1.0)
            P.memset("pool", gaaug, 1.0)
            P.memset("pool", halo_all[0], 0.0)
            P.memset("pool", halo_all[1], 0.0)
            for l in range(L):
                for h in range(4):
                    P.memset("pool", Sst[l][h], 0.0)
                for g in range(2):
                    P.memset("pool", kT[l][g], 0.0)
                P.memset("pool", Vr[l][0], 0.0)

            chk(1)
            wbuf = {}
            order = ["w_in", "w_gla_o", "w_conv_o", "w_swa_o", "w_o", "w_up", "w_down"]
            for l in range(L):
                for k in order:
                    r, c = wspec[k]
                    b = Buf("%s_bf%d" % (k, l))
                    wbuf[(k, l)] = b
                    for r0 in range(0, r, 128):
                        P.dma(V(wb_d[k][l, r0:r0 + 128, :], [b]), V(w_d[k][l, r0:r0 + 128, :], []), queue="pool", nowaw=True)

            chk(2)
            def wsrc(k, l, r0, nk, c0, ncol):
                ap = wb_d[k][l, r0 * 128:(r0 + nk) * 128, c0:c0 + ncol].rearrange("(kc p) n -> p kc n", p=128)
                return V(ap, [wbuf[(k, l)]])

            def wload(k, l, r0, nk, c0, ncol):
                slot = wring()
                P.dma(slot[:, 0:nk, 0:ncol], wsrc(k, l, r0, nk, c0, ncol), final=(stage in (61, 62)))
                return slot

            def dump(name, v):
                if dbg and name in dbg_d:
                    P.dma(V(dbg_d[name], [Buf("dbg_" + name)]), v, final=True)

            def rms_stats(srcs, nparts_scale, ring=None):
                pst = (ring or pspool)()
                n = len(srcs)
                for i, s in enumerate(srcs):
                    sq = sqpool()
                    P.act(sq, s, AF.Square)
                    P.mm(pst, onesb, sq, start=(i == 0), stop=(i == n - 1))
                ln = fpool()
                P.act(ln[:, 0:NT], pst, AF.Ln, bias=epsb, scale=nparts_scale)
                r = fpool()
                P.act(r[:, 0:NT], ln[:, 0:NT], AF.Exp, scale=-0.5)
                return r

            def norm_to_uT(gcol0):
                r = rms_stats(hT, 1.0 / D)
                for c in range(8):
                    P.stt(uT[c], hT[c], gv[:, gcol0 + c:gcol0 + c + 1], r[:, 0:NT], ALU.mult, ALU.mult)

            def proj(wslot, col0, ncols_m, kcs=8, rhs_list=None, ring=None):
                rhs_list = rhs_list if rhs_list is not None else uT
                pt = (ring or pspool)()
                for kc in range(kcs):
                    P.mm(pt[0:ncols_m, :], wslot[:, kc, col0:col0 + ncols_m], rhs_list[kc], start=(kc == 0), stop=(kc == kcs - 1))
                return pt

            for t in range(NTILES):
                tok0 = t * NT
                for tb in range(4):
                    xs = scpool()
                    P.dma(xs, V(x_d[tok0 + tb * 128: tok0 + (tb + 1) * 128, :], []))
                    for half in range(2):
                        pt = pspool()
                        for cc in range(4):
                            c = half * 4 + cc
                            P.tr(pt[:, cc * 128:(cc + 1) * 128], xs[:, c * 128:(c + 1) * 128], identf)
                        dst = V(hT_t[:, half * 4:(half + 1) * 4, tb * 128:(tb + 1) * 128],
                                [hT[half * 4 + cc].bufs[0] for cc in range(4)])
                        src = V(pt.ap.rearrange("p (c n) -> p c n", c=4), pt.bufs)
                        P.copy("act" if half == 0 else "dve", dst, src)

                chk(3)
                for l in range(L):
                    first = (t == 0)
                    norm_to_uT(l * 8)
                    if t == 0 and l == 0:
                        dump("uT0", V(uT_t[:, 0, :], uT[0].bufs))

                    chk(4)
                    gring = mixring
                    wsm = wring()
                    P.dma(wsm[:, :, 0:16], wsrc("w_in", l, 0, 8, 2048, 16), nowaw=True)
                    for g in range(2):
                        for d2 in range(2):
                            P.dma(wsm[:, :, 16 + g * 128 + d2 * 64: 16 + g * 128 + (d2 + 1) * 64],
                                  wsrc("w_in", l, 0, 8, 4112 + g * 64, 64), nowaw=True)
                    P.dma(wsm[:, :, 272:400], wsrc("w_in", l, 0, 8, 4240, 128), nowaw=True)
                    pga = gring()
                    for kc in range(8):
                        P.mm(pga[0:16, :], wsm[:, kc, 0:16], uT[kc], start=(kc == 0), stop=(kc == 7))
                    P.copy("act", gaaug[0:16, :], pga[0:16, :])
                    ltok = []
                    for tb in range(4):
                        px = gring()
                        P.mm(px, gaaug[0:17, tb * 128:(tb + 1) * 128], walpha[0:17, l, :])
                        e = ltokb[tb]
                        P.act(e, px, AF.Exp, scale=-1.0)
                        ltok.append(e)
                    for tb in range(4):
                        P.act(ltokh[tb], ltok[tb], AF.Ln, bias=1.0)
                    chk(5)
                    for g in range(2):
                        pk = gring()
                        for kc in range(8):
                            P.mm(pk, wsm[:, kc, 16 + g * 128:16 + (g + 1) * 128], uT[kc], start=(kc == 0), stop=(kc == 7))
                        P.copy("act", kT[l][g][:, 128:640], pk)
                    for tb in range(4):
                        pv = gring()
                        for kc in range(8):
                            P.mm(pv[:, 0:128], uT[kc][:, tb * 128:(tb + 1) * 128], wsm[:, kc, 272:400], start=(kc == 0), stop=(kc == 7))
                        P.copy("act", Vr[l][tb + 1], pv[:, 0:128])
                    w_sq = wload("w_in", l, 0, 8, 3600, 512)
                    for c4 in range(4):
                        pq = proj(w_sq, c4 * 128, 128, ring=gring)
                        P.act(qs[c4], pq, AF.Copy, scale=0.125)
                    chk(8)
                    w_cx = wload("w_in", l, 0, 8, 2064, 512)
                    w_cb = wload("w_in", l, 0, 8, 2576, 512)
                    w_cc = wload("w_in", l, 0, 8, 3088, 512)
                    for c4 in range(4):
                        pcx = proj(w_cx, c4 * 128, 128, ring=gring)
                        cxs = fpool()
                        P.copy("act", cxs[:, 0:NT], pcx)
                        pcc = proj(w_cc, c4 * 128, 128, ring=gring)
                        pbuf = fpool()
                        P.copy("pool", pbuf[:, 0:2], halo_c[l][c4])
                        P.tt("dve", pbuf[:, 2:2 + NT], pcc, cxs[:, 0:NT], ALU.mult)
                        P.copy("pool", halo_c[l][c4], pbuf[:, NT:NT + 2])
                        acc = fpool()
                        P.act(acc[:, 0:NT], pbuf[:, 0:NT], AF.Copy, scale=convw[:, l, c4, 0:1])
                        P.stt(acc[:, 0:NT], pbuf[:, 1:1 + NT], convw[:, l, c4, 1:2], acc[:, 0:NT], ALU.mult, ALU.add)
                        P.stt(acc[:, 0:NT], pbuf[:, 2:2 + NT], convw[:, l, c4, 2:3], acc[:, 0:NT], ALU.mult, ALU.add)
                        pcb = proj(w_cb, c4 * 128, 128, ring=gring)
                        P.tt("dve", cvT[c4], pcb, acc[:, 0:NT], ALU.mult)
                    chk(6)
                    w_gv = wload("w_in", l, 0, 8, 1024, 512)
                    for tb in range(4):
                        pv = gring()
                        for kc in range(8):
                            P.mm(pv, uT[kc][:, tb * 128:(tb + 1) * 128], w_gv[:, kc, :], start=(kc == 0), stop=(kc == 7))
                        P.copy("act", vt[tb], pv)
                    w_gq = wload("w_in", l, 0, 8, 0, 512)
                    w_gk = wload("w_in", l, 0, 8, 512, 512)
                    HS = [slice(hh * 128, (hh + 1) * 128) for hh in range(4)]
                    gl = [dict() for _ in range(4)]

                    def G1(hh):
                        hs = HS[hh]
                        d = gl[hh]
                        pA, pB, pb = gring(), gring(), gring()
                        for tb in range(4):
                            ts_ = slice(tb * 128, (tb + 1) * 128)
                            P.mm(pA[:, ts_], ltokh[tb][:, hs], trib[:, 1, :])
                            P.mm(pB[:, ts_], ltokh[tb][:, hs], trib[:, 2, :])
                            P.mm(pb[:, ts_], ltokh[tb][:, hs], trib[:, 0, :])
                        E1, E2, E3, E4 = fpool(), fpool(), fpool(), fpool()
                        P.act(E1[:, 0:NT], pA, AF.Exp)
                        P.act(E2[:, 0:NT], pA, AF.Exp, scale=-1.0)
                        P.act(E3[:, 0:NT], pB, AF.Exp)
                        P.act(E4[:, 0:NT], pb, AF.Exp)
                        pq = proj(w_gq, hh * 128, 128, ring=gring)
                        d["qa"], d["qb"] = bpool(), bpool()
                        P.stt(d["qa"], pq, 128.0 ** -0.5, E1[:, 0:NT], ALU.mult, ALU.mult)
                        P.stt(d["qb"], pq, 128.0 ** -0.5, E4[:, 0:NT], ALU.mult, ALU.mult)
                        pk = proj(w_gk, hh * 128, 128, ring=gring)
                        d["ka"], d["kb"] = bpool(), bpool()
                        P.tt("dve", d["ka"], pk, E2[:, 0:NT], ALU.mult)
                        P.tt("dve", d["kb"], pk, E3[:, 0:NT], ALU.mult)
                        d["dec"] = small()
                        P.copy("pool", d["dec"], V(E4.ap[:, 0:NT].rearrange("p (c k) -> p c k", c=8)[:, :, 63], E4.bufs))
                        if t == 0 and l == 0 and hh == 0:
                            dump("E4", E4[:, 0:NT])

                    def G2(hh):
                        hs = HS[hh]
                        d = gl[hh]
                        ktok = ktpool()
                        for tb in range(4):
                            P.tr(PB[:, tb * 128:(tb + 1) * 128], d["kb"][:, tb * 128:(tb + 1) * 128], identb)
                        P.copy("act", V(kt_t[:, ktok[0], :, :], [b for tb in range(4) for b in ktok[1][tb].bufs]),
                               V(PB.ap[:, 0:512].rearrange("p (a b) -> p a b", a=4), PB.bufs))
                        ktok = ktok[1]
                        pat = gring()
                        for tb in range(4):
                            ts_ = slice(tb * 128, (tb + 1) * 128)
                            P.mm(pat[:, ts_], d["ka"][:, ts_], d["qa"][:, ts_])
                        d["am"] = bpool()
                        P.tt("dve", d["am"], pat, gmask, ALU.mult)
                        pkv = [gring(), gring()]
                        for c in range(8):
                            tb, r0 = c // 2, (c % 2) * 64
                            P.mm(pkv[c % 2][:, tb * 128:(tb + 1) * 128], ktok[tb][r0:r0 + 64, :], vt[tb][r0:r0 + 64, hs])
                        d["pkv"] = pkv

                    def G3(hh):
                        d = gl[hh]
                        sb_ = Sbf[hh % 2]
                        pkv = d["pkv"]
                        P.copy("dve", sb_[0], Sst[l][hh])
                        for c in range(8):
                            tb = c // 2
                            P.stt(Sst[l][hh], Sst[l][hh], d["dec"][:, c:c + 1], pkv[c % 2][:, tb * 128:(tb + 1) * 128],
                                  ALU.mult, ALU.add)
                            P.copy("dve", sb_[c + 1], Sst[l][hh])

                    def G4(hh):
                        hs = HS[hh]
                        d = gl[hh]
                        sb_ = Sbf[hh % 2]
                        po = poring()
                        for tb in range(4):
                            ts_ = slice(tb * 128, (tb + 1) * 128)
                            P.mm(po[:, ts_], vt[tb][:, hs], d["am"][:, ts_], start=True, stop=False)
                            for c2 in range(2):
                                c = tb * 2 + c2
                                cs = slice(c * 64, (c + 1) * 64)
                                P.mm(po[:, cs], sb_[c], d["qb"][:, cs], start=False, stop=(c2 == 1))
                        d["po"] = po

                    def G5(hh, w_gr):
                        d = gl[hh]
                        po = d["po"]
                        pg = proj(w_gr, hh * 128, 128, ring=gring)
                        sg = fpool()
                        P.act(sg[:, 0:NT], pg, AF.Exp, scale=-1.0)
                        P.act(sg[:, 0:NT], sg[:, 0:NT], AF.Ln, bias=1.0)
                        P.act(sg[:, 0:NT], sg[:, 0:NT], AF.Exp, scale=-1.0)
                        P.stt(sg[:, 0:NT], pg, 1.0, sg[:, 0:NT], ALU.mult, ALU.mult)
                        r = rms_stats([po], 1.0 / 128, ring=gring)
                        tmp = fpool()
                        P.stt(tmp[:, 0:NT], po, glag[:, l * 4 + hh:l * 4 + hh + 1], r[:, 0:NT], ALU.mult, ALU.mult)
                        P.tt("pool", goT[hh], tmp[:, 0:NT], sg[:, 0:NT], ALU.mult)
                        if t == 0 and l == 0 and hh == 0:
                            dump("tmp_o", tmp[:, 0:NT])

                    for hh in range(4):
                        G1(hh)
                    chk(7)
                    w_gr = wload("w_in", l, 0, 8, 1536, 512)
                    G2(0); G3(0)
                    G2(1); G3(1)
                    G4(0); G5(0, w_gr)
                    G2(2); G3(2)
                    G4(1); G5(1, w_gr)
                    G2(3); G3(3)
                    G4(2); G5(2, w_gr)
                    G4(3); G5(3, w_gr)
                    chk(9)
                    sw = {}

                    def SA(i):
                        tb, g = i // 2, i % 2
                        sA, sB = psV[(i % 2) * 2], psV[(i % 2) * 2 + 1]
                        for j in range(4):
                            h = 4 * g + j
                            ch, po_ = h // 2, (h % 2) * 64
                            bank = sA if j % 2 == 0 else sB
                            P.mm(bank[:, (j // 2) * 256:(j // 2 + 1) * 256], qs[ch][po_:po_ + 64, tb * 128:(tb + 1) * 128],
                                 kT[l][g][po_:po_ + 64, tb * 128:tb * 128 + 256])
                        sc = scpool()
                        sc4 = V(sc.ap.rearrange("p (j k) -> p j k", j=4), sc.bufs)
                        sw4 = V(swab.ap[:, g, :].rearrange("p (j k) -> p j k", j=4), swab.bufs)
                        P.tt("dve", sc4[:, 0::2, :], V(sA.ap.rearrange("p (j k) -> p j k", j=2), sA.bufs), sw4[:, 0::2, :], ALU.add)
                        P.tt("dve", sc4[:, 1::2, :], V(sB.ap.rearrange("p (j k) -> p j k", j=2), sB.bufs), sw4[:, 1::2, :], ALU.add)
                        if first and tb == 0:
                            P.ts("pool", sc4[:, :, 0:128], sc4[:, :, 0:128], NEG, ALU.add)
                        sm, sm2, sm3 = small(), small(), small()
                        mx, negm = sm[:, 0:4], sm[:, 4:8]
                        rsum, dd = sm2[:, 0:4], sm2[:, 4:8]
                        es, rinv = sm3[:, 0:4], sm3[:, 4:8]
                        sk_ = sinks[:, l * 8 + g * 4:l * 8 + g * 4 + 4]
                        P.reduce(mx, sc4, ALU.max)
                        P.tt("dve", mx, mx, sk_, ALU.max)
                        P.ts("dve", negm, mx, -1.0, ALU.mult)
                        pn = ppool()
                        for j in range(4):
                            P.act(pn[:, j * 256:(j + 1) * 256], sc[:, j * 256:(j + 1) * 256], AF.Exp,
                                  bias=negm[:, j:j + 1], accum=rsum[:, j:j + 1])
                        P.tt("dve", dd, sk_, mx, ALU.subtract)
                        P.act(es, dd, AF.Exp)
                        P.tt("dve", es, es, rsum, ALU.add)
                        P.recip(rinv, es)
                        pn4 = V(pn.ap.rearrange("p (j k) -> p j k", j=4), pn.bufs)
                        P.tt("dve", pn4, pn4, V(rinv.ap.unsqueeze(2).broadcast_to([128, 4, 256]), rinv.bufs), ALU.mult)
                        sw[i] = pn

                    def SB(i):
                        tb, g = i // 2, i % 2
                        pn = sw.pop(i)
                        posw = psV[4 + (tb % 2)]
                        for j in range(4):
                            for kb in range(2):
                                P.tr(PB[:, (kb * 4 + j) * 128:(kb * 4 + j + 1) * 128],
                                     pn[:, j * 256 + kb * 128: j * 256 + (kb + 1) * 128], identb)
                        ptt = ptpool()
                        P.copy("act", ptt, PB)
                        for j in range(4):
                            h = 4 * g + j
                            ch, po_ = h // 2, (h % 2) * 64
                            for kb in range(2):
                                P.mm(posw[po_:po_ + 64, ch * 128:(ch + 1) * 128], Vr[l][tb + kb][:, g * 64:(g + 1) * 64],
                                     ptt[:, (kb * 4 + j) * 128:(kb * 4 + j + 1) * 128], start=(kb == 0), stop=(kb == 1))
                        if g == 1:
                            dst = V(ar_t[:, 8:12, tb * 128:(tb + 1) * 128], [oswT[c4].bufs[0] for c4 in range(4)])
                            P.copy("act", dst, V(posw.ap.rearrange("p (c n) -> p c n", c=4), posw.bufs))

                    SA(0)
                    for i in range(8):
                        if i + 1 < 8:
                            SA(i + 1)
                        SB(i)
                    for g in range(2):
                        P.copy("pool", kT[l][g][:, 0:128], kT[l][g][:, 512:640])
                    P.copy("pool", Vr[l][0], Vr[l][4])
                    if t == 0 and l == 0:
                        dump("go0", V(ar_t[:, 0, :], goT[0].bufs))
                        dump("cv0", V(ar_t[:, 4, :], cvT[0].bufs))
                        dump("osw0", V(ar_t[:, 8, :], oswT[0].bufs))


                    bo_names = ["w_gla_o", "w_conv_o", "w_swa_o"]
                    brs = [goT, cvT, oswT]
                    for b in range(3):
                        for mgp in range(2):
                            wg = wload("w_in", l, 0, 8, 4368 + b * 1024 + mgp * 512, 512)
                            for m4 in range(4):
                                pgt = proj(wg, m4 * 128, 128, ring=gatering)
                                e = fpool()
                                P.act(e[:, 0:NT], pgt, AF.Exp, scale=-1.0)
                                P.act(e[:, 0:NT], e[:, 0:NT], AF.Ln, bias=1.0)
                                P.act(sigb[b * 8 + mgp * 4 + m4], e[:, 0:NT], AF.Exp, scale=-1.0)
                    for mgp in range(2):
                        wbo = [wload(bo_names[b], l, 0, 4, mgp * 512, 512) for b in range(3)]
                        for m4 in range(4):
                            m = mgp * 4 + m4
                            terms = []
                            for b in range(3):
                                py = proj(wbo[b], m4 * 128, 128, kcs=4, rhs_list=brs[b])
                                tm = fpool()
                                P.tt("dve", tm[:, 0:NT], py, sigb[b * 8 + m], ALU.mult)
                                terms.append(tm)
                            P.tt("pool", terms[0][:, 0:NT], terms[0][:, 0:NT], terms[1][:, 0:NT], ALU.add)
                            P.tt("pool", mgT[m], terms[0][:, 0:NT], terms[2][:, 0:NT], ALU.add)
                    chk(11)
                    for mgp in range(2):
                        wo = wload("w_o", l, 0, 8, mgp * 512, 512)
                        for m4 in range(4):
                            m = mgp * 4 + m4
                            pt = proj(wo, m4 * 128, 128, rhs_list=mgT)
                            P.tt("dve", hT[m], pt, hT[m], ALU.add)
                    if t == 0 and l == 0:
                        dump("h1", V(hT_t[:, 0, :], hT[0].bufs))

                    chk(12)
                    norm_to_uT((L + l) * 8)
                    for pg in range(6):
                        npair = 4 if pg < 5 else 2
                        wa = wload("w_up", l, 0, 8, pg * 512, npair * 128)
                        wb = wload("w_up", l, 0, 8, DFF + pg * 512, npair * 128)
                        for pi in range(npair):
                            c = pg * 4 + pi
                            accs = []
                            for (wsl, cidx) in ((wa, c), (wb, c + 22)):
                                ph = proj(wsl, pi * 128, 128)
                                hb = fpool()
                                P.copy("pool", hb[:, 0:2], halo_f[l][cidx])
                                P.copy("act", hb[:, 2:2 + NT], ph)
                                P.copy("pool", halo_f[l][cidx], hb[:, NT:NT + 2])
                                acc = fpool()
                                P.act(acc[:, 0:NT], ph, AF.Copy, scale=fconvw[:, l, cidx, 2:3])
                                P.stt(acc[:, 0:NT], hb[:, 0:NT], fconvw[:, l, cidx, 0:1], acc[:, 0:NT], ALU.mult, ALU.add)
                                P.stt(acc[:, 0:NT], hb[:, 1:1 + NT], fconvw[:, l, cidx, 1:2], acc[:, 0:NT], ALU.mult, ALU.add)
                                accs.append(acc)
                            sa = fpool()
                            P.act(sa[:, 0:NT], accs[0][:, 0:NT], AF.Silu)
                            P.tt("dve", gT[c], sa[:, 0:NT], accs[1][:, 0:NT], ALU.mult)
                    chk(13)
                    for mgp in range(2):
                        banks = [pspool() for _ in range(4)]
                        for (k0, nk) in ((0, 8), (8, 8), (16, 6)):
                            wd = wload("w_down", l, k0, nk, mgp * 512, 512)
                            for m4 in range(4):
                                for kk in range(nk):
                                    k = k0 + kk
                                    P.mm(banks[m4], wd[:, kk, m4 * 128:(m4 + 1) * 128], gT[k], start=(k == 0), stop=(k == 21))
                        for m4 in range(4):
                            m = mgp * 4 + m4
                            P.tt("dve", hT[m], banks[m4], hT[m], ALU.add)
                    if t == 0 and l == 0:
                        dump("h2", V(hT_t[:, 0, :], hT[0].bufs))

                chk(14)
                if final_norm:
                    r = rms_stats(hT, 1.0 / D)
                ofm = []
                for c in range(8):
                    o = fpool()
                    if final_norm:
                        P.stt(o[:, 0:NT], hT[c], gv[:, 2 * L * 8 + c:2 * L * 8 + c + 1], r[:, 0:NT], ALU.mult, ALU.mult)
                    else:
                        P.copy("pool", o[:, 0:NT], hT[c])
                    ofm.append(o)
                for tb in range(4):
                    xo = scpool()
                    for half in range(2):
                        pt = pspool()
                        for cc in range(4):
                            c = half * 4 + cc
                            P.tr(pt[:, cc * 128:(cc + 1) * 128], ofm[c][:, tb * 128:(tb + 1) * 128], identf)
                        P.copy("act" if half == 0 else "dve", xo[:, half * 512:(half + 1) * 512], pt)
                    P.dma(V(out_d[tok0 + tb * 128: tok0 + (tb + 1) * 128, :], [Buf("out%d_%d" % (t, tb))]), xo, final=True)

        try:
            _body()
        except _Stop:
            pass
        P.emit()
        nc_marks = P.marks
        nc_counts = {e: len(v) for e, v in P.streams.items()}
    _MARKS[id(nc)] = (nc_marks, nc_counts)
    return nc


_WNAMES = ["w_in", "w_gla_o", "w_conv_o", "w_swa_o", "w_o", "w_up", "w_down"]


def make_in_maps(x, params, L):
    consts = _consts()
    lay = _layout_params(L, params["g_mix"], params["g_ffn"], params["g_final"], params["gla_norm_g"], params["conv_w"],
                         params["ffn_conv_w"], params["swa_sinks"], params["gla_w_alpha"], params["gla_b_alpha"])
    shared = {}
    shared.update(consts)
    shared.update(lay)
    for k in _WNAMES:
        shared[k] = np.ascontiguousarray(params[k][:L], dtype=np.float32)
    maps = []
    for b in range(x.shape[0]):
        m = dict(shared)
        m["x"] = np.ascontiguousarray(x[b], dtype=np.float32)
        maps.append(m)
    return maps


_NC_CACHE = {}


def kernel(x, g_mix, w_in, gla_w_alpha, gla_b_alpha, gla_norm_g, conv_w, swa_sinks, w_gla_o, w_conv_o, w_swa_o, w_o,
           g_ffn, w_up, ffn_conv_w, w_down, g_final):
    x = np.asarray(x)
    B, S, _ = x.shape
    L = int(np.asarray(g_mix).shape[0])
    params = dict(g_mix=g_mix, w_in=w_in, gla_w_alpha=gla_w_alpha, gla_b_alpha=gla_b_alpha, gla_norm_g=gla_norm_g,
                  conv_w=conv_w, swa_sinks=swa_sinks, w_gla_o=w_gla_o, w_conv_o=w_conv_o, w_swa_o=w_swa_o, w_o=w_o,
                  g_ffn=g_ffn, w_up=w_up, ffn_conv_w=ffn_conv_w, w_down=w_down, g_final=g_final)
    params = {k: np.asarray(v, dtype=np.float32) for k, v in params.items()}
    key = (S, L)
    if key not in _NC_CACHE:
        _NC_CACHE[key] = build_nc(S, L)
    nc = _NC_CACHE[key]
    in_maps = make_in_maps(x, params, L)
    res = run_bass_kernel_spmd(nc, in_maps, core_ids=list(range(B)))
    out = np.stack([np.asarray(r["out"]) for r in res.results], axis=0)
    return out.astype(np.float32)
```

```python
from contextlib import ExitStack
import numpy as np
import ml_dtypes
import concourse.bass as bass
import concourse.mybir as mybir
from concourse.bass_utils import run_bass_kernel_spmd

F32 = mybir.dt.float32
BF16 = mybir.dt.bfloat16
AF = mybir.ActivationFunctionType
ALU = mybir.AluOpType
AX = mybir.AxisListType

ENGS = ("pe", "act", "dve", "pool", "sp")
SAME_ENGINE_SYNC = {"pe": False, "act": True, "dve": True, "pool": True, "sp": True}

D = 1024
DIN = 7440
DFF = 2816
NT = 512
EPS = 1e-6
NEG = -30000.0


class Buf:
    __slots__ = ("name", "last_writer", "readers", "sem", "sem_cnt")

    def __init__(self, name):
        self.name = name
        self.last_writer = None
        self.readers = []
        self.sem = None
        self.sem_cnt = 0


class V:
    __slots__ = ("ap", "bufs")

    def __init__(self, ap, bufs):
        self.ap = ap
        self.bufs = tuple(bufs)

    def __getitem__(self, k):
        return V(self.ap[k], self.bufs)


class Op:
    __slots__ = ("eng", "fn", "idx", "pidx", "preds", "succs", "waits", "dma_waits", "signal", "signo", "is_dma",
                 "sem", "sem_val", "dur", "tab", "nbytes", "npend", "ready", "finish", "tag", "start", "prio")

    def __init__(self, eng, fn, is_dma=False):
        self.eng = eng
        self.fn = fn
        self.idx = -1
        self.pidx = -1
        self.preds = []
        self.succs = []
        self.waits = []
        self.dma_waits = []
        self.signal = False
        self.signo = 0
        self.is_dma = is_dma
        self.sem = None
        self.sem_val = 0
        self.dur = 100.0
        self.tab = None
        self.nbytes = 0
        self.npend = 0
        self.ready = 0.0
        self.finish = 0.0
        self.tag = None
        self.start = 0.0
        self.prio = 0.0


_ACT_SETS = {}


def _act_set(func):
    if func in (AF.Exp, AF.Ln):
        return "explog"
    if func == AF.Silu:
        return "silu"
    if func == AF.Sigmoid:
        return "sigmoid"
    return None


def _fsize(ap):
    n = 1
    for d in ap.shape[1:]:
        n *= d
    return n


class Prog:
    REORDER = ("pe", "act", "dve")

    def __init__(self, nc, stack):
        self.nc = nc
        self.stack = stack
        self.all_ops = []
        self.streams = {e: [] for e in ENGS}
        self.esem = {}
        for e in ("pe", "act", "dve", "pool"):
            self.esem[e] = stack.enter_context(nc.semaphore("es_" + e))
        self.final_dma = []
        self.marks = []
        self.sched = True
        self.prio_mode = "bl"
        self.cur_tag = None

    def mark(self, label):
        self.marks.append((label, len(self.streams["pe"])))
        self.cur_tag = (label, len(self.marks))

    def new_sem(self, name):
        return self.stack.enter_context(self.nc.semaphore(name))

    def op(self, eng, fn, reads=(), writes=(), dma=False, sem_buf=None, nowaw=False, dur=100.0, tab=None, nbytes=0):
        X = Op(eng, fn, is_dma=dma)
        X.pidx = len(self.all_ops)
        X.dur = dur
        X.tab = tab
        X.nbytes = nbytes
        X.tag = self.cur_tag
        if dma:
            b = sem_buf
            if b.sem is None:
                b.sem = self.new_sem("ds_" + b.name)
            b.sem_cnt += 16
            X.sem = b.sem
            X.sem_val = b.sem_cnt
        deps = []
        for r in reads:
            deps.append(r.last_writer)
        for w in writes:
            if not nowaw:
                deps.append(w.last_writer)
            deps.extend(w.readers)
        seen = set()
        for Y in deps:
            if Y is None or Y is X or id(Y) in seen:
                continue
            seen.add(id(Y))
            X.preds.append(Y)
        for r in reads:
            r.readers.append(X)
        for w in writes:
            w.last_writer = X
            w.readers = []
        self.all_ops.append(X)
        self.streams[eng].append(X)
        return X

    def dma(self, out, in_, queue="sp", final=False, nowaw=False):
        nb = 1
        for d in out.ap.shape:
            nb *= d
        nb *= 2 if out.ap.dtype == BF16 else 4
        X = self.op(queue, lambda e: e.dma_start(out=out.ap, in_=in_.ap), reads=in_.bufs, writes=out.bufs,
                    dma=True, sem_buf=out.bufs[0], nowaw=nowaw, dur=60.0, nbytes=nb)
        if final:
            self.final_dma.append(X)
        return X

    def mm(self, out, lhsT, rhs, start=True, stop=True):
        n = _fsize(rhs.ap)
        passes = 4 if rhs.ap.dtype == F32 else 1
        return self.op("pe", lambda e: e.matmul(out.ap, lhsT.ap, rhs.ap, start=start, stop=stop),
                       reads=lhsT.bufs + rhs.bufs, writes=out.bufs, dur=passes * (60.0 + 0.36 * max(n, 64)))

    def tr(self, out, in_, ident):
        return self.op("pe", lambda e: e.transpose(out.ap, in_.ap, ident.ap), reads=in_.bufs + ident.bufs, writes=out.bufs,
                       dur=110.0)

    def act(self, out, in_, func, bias=None, scale=None, accum=None):
        reads = list(in_.bufs)
        kw = {}
        if bias is not None:
            if isinstance(bias, V):
                reads += bias.bufs
                kw["bias"] = bias.ap
            else:
                kw["bias"] = bias
        if scale is not None:
            if isinstance(scale, V):
                reads += scale.bufs
                kw["scale"] = scale.ap
            else:
                kw["scale"] = scale
        writes = list(out.bufs)
        d = 180.0 + _fsize(in_.ap) / 1.2
        if accum is not None:
            writes += accum.bufs
            kw["accum_out"] = accum.ap
            d += 100
        return self.op("act", lambda e: e.activation(out=out.ap, in_=in_.ap, func=func, **kw), reads=reads, writes=writes,
                       dur=d, tab=_act_set(func))

    def copy(self, eng, out, in_):
        n = _fsize(in_.ap)
        if eng == "act":
            return self.op("act", lambda e: e.copy(out.ap, in_.ap), reads=in_.bufs, writes=out.bufs, dur=180.0 + n / 1.2)
        d = (100.0 + 1.15 * n) if eng == "dve" else (350.0 + 1.0 * n)
        return self.op(eng, lambda e: e.tensor_copy(out.ap, in_.ap), reads=in_.bufs, writes=out.bufs, dur=d)

    def tt(self, eng, out, in0, in1, op):
        n = _fsize(in0.ap)
        d = (100.0 + 1.15 * n) if eng == "dve" else (150.0 + 2.2 * n)
        return self.op(eng, lambda e: e.tensor_tensor(out=out.ap, in0=in0.ap, in1=in1.ap, op=op),
                       reads=in0.bufs + in1.bufs, writes=out.bufs, dur=d)

    def ts(self, eng, out, in0, s1, op0, s2=None, op1=None):
        reads = list(in0.bufs)
        a1 = s1
        if isinstance(s1, V):
            reads += s1.bufs
            a1 = s1.ap
        a2 = s2
        if isinstance(s2, V):
            reads += s2.bufs
            a2 = s2.ap
        n = _fsize(in0.ap)
        d = (100.0 + 1.15 * n) if eng == "dve" else (300.0 + 11.0 * n)
        if op1 is None:
            return self.op(eng, lambda e: e.tensor_scalar(out=out.ap, in0=in0.ap, scalar1=a1, scalar2=None, op0=op0),
                           reads=reads, writes=out.bufs, dur=d)
        return self.op(eng, lambda e: e.tensor_scalar(out=out.ap, in0=in0.ap, scalar1=a1, scalar2=a2, op0=op0, op1=op1),
                       reads=reads, writes=out.bufs, dur=d)

    def stt(self, out, in0, scalar, in1, op0, op1):
        reads = list(in0.bufs) + list(in1.bufs)
        sc = scalar
        if isinstance(scalar, V):
            reads += scalar.bufs
            sc = scalar.ap
        return self.op("dve", lambda e: e.scalar_tensor_tensor(out=out.ap, in0=in0.ap, scalar=sc, in1=in1.ap, op0=op0, op1=op1),
                       reads=reads, writes=out.bufs, dur=120.0 + 1.2 * _fsize(in0.ap))

    def reduce(self, out, in_, op, axis=AX.X):
        return self.op("dve", lambda e: e.tensor_reduce(out=out.ap, in_=in_.ap, axis=axis, op=op), reads=in_.bufs, writes=out.bufs,
                       dur=100.0 + 1.1 * _fsize(in_.ap))

    def recip(self, out, in_):
        return self.op("dve", lambda e: e.reciprocal(out.ap, in_.ap), reads=in_.bufs, writes=out.bufs,
                       dur=100.0 + 8.4 * _fsize(in_.ap))

    def memset(self, eng, out, val):
        return self.op(eng, lambda e: e.memset(out.ap, val), writes=out.bufs, dur=200.0)

    def schedule(self):
        import heapq
        ops = self.all_ops
        for X in ops:
            X.succs = []
        for X in ops:
            for Y in X.preds:
                Y.succs.append(X)
        fixed_prev = {}
        extra = {}
        for X in ops:
            if X.eng not in self.REORDER:
                pv = fixed_prev.get(X.eng)
                if pv is not None:
                    extra[id(X)] = pv
                    pv.succs.append(X)
                fixed_prev[X.eng] = X
        for X in ops:
            X.npend = len(X.preds) + (1 if id(X) in extra else 0)
            X.ready = 0.0
        if self.prio_mode == "bl":
            bl = {}
            for X in reversed(ops):
                m = 0.0
                for Z in X.succs:
                    v = bl[id(Z)] + 200.0
                    if v > m:
                        m = v
                bl[id(X)] = m + (X.dur if not X.is_dma else X.nbytes / 260.0 + 2000.0)
            for X in ops:
                X.prio = -bl[id(X)]
        else:
            for X in ops:
                X.prio = float(X.pidx)
        HOP = 200.0
        free_at = {e: 0.0 for e in ENGS}
        pending = {e: [] for e in ENGS}
        avail = {e: [] for e in ENGS}
        last_tab = {"act": None}
        dma_free = [0.0]
        for X in ops:
            if X.npend == 0:
                heapq.heappush(pending[X.eng], (0.0, X.pidx, X))
        order = {e: [] for e in ENGS}
        nleft = len(ops)
        while nleft:
            best = None
            for e in ENGS:
                T = free_at[e]
                pq, av = pending[e], avail[e]
                while pq and pq[0][0] <= T:
                    r, pi, X = heapq.heappop(pq)
                    heapq.heappush(av, (X.prio, pi, X))
                if av:
                    st = T
                elif pq:
                    st = pq[0][0]
                else:
                    continue
                if best is None or st < best[0]:
                    best = (st, e)
            st, e = best
            if avail[e]:
                pr, pi, X = heapq.heappop(avail[e])
            else:
                r, pi, X = heapq.heappop(pending[e])
            d = X.dur
            if e == "act" and X.tab is not None and X.tab != last_tab["act"]:
                d += 1300.0
                last_tab["act"] = X.tab
            if X.is_dma:
                free_at[e] = st + d
                t0 = max(dma_free[0], st + d)
                t1 = t0 + X.nbytes / 260.0
                dma_free[0] = t1
                X.finish = t1 + 2000.0
            else:
                X.finish = st + d
                free_at[e] = X.finish
            X.start = st
            order[e].append(X)
            nleft -= 1
            for Z in X.succs:
                Z.npend -= 1
                rt = X.finish + (HOP if Z.eng != X.eng or X.is_dma else 60.0)
                if rt > Z.ready:
                    Z.ready = rt
                if Z.npend == 0:
                    heapq.heappush(pending[Z.eng], (Z.ready, Z.pidx, Z))
        self.streams = order
        self.est_ns = max(free_at.values())

    def resolve(self):
        for e in ENGS:
            for i, X in enumerate(self.streams[e]):
                X.idx = i
        waited = {e: {f: -1 for f in ENGS} for e in ENGS}
        waited_dma = {e: {} for e in ENGS}
        for e in ENGS:
            for X in self.streams[e]:
                best = {}
                for Y in X.preds:
                    if Y.is_dma:
                        cur = waited_dma[e].get(Y.sem, 0)
                        if cur < Y.sem_val:
                            waited_dma[e][Y.sem] = Y.sem_val
                            X.dma_waits.append((Y.sem, Y.sem_val))
                        continue
                    if Y.eng == e:
                        assert Y.idx < X.idx, "same-engine order violated"
                        if not SAME_ENGINE_SYNC[e]:
                            continue
                    cur = best.get(Y.eng)
                    if cur is None or Y.idx > cur.idx:
                        best[Y.eng] = Y
                for f, Y in best.items():
                    if waited[e][f] >= Y.idx:
                        continue
                    waited[e][f] = Y.idx
                    Y.signal = True
                    X.waits.append(Y)

    def emit(self):
        nc = self.nc
        if self.sched:
            self.schedule()
        self.resolve()
        for e in ("pe", "act", "dve", "pool"):
            n = 0
            for X in self.streams[e]:
                if X.signal:
                    n += 1
                    X.signo = n
        with nc.Block() as block:
            def make(e):
                def body(eng):
                    for X in self.streams[e]:
                        for Y in X.waits:
                            eng.wait_ge(self.esem[Y.eng], Y.signo)
                        for (s, v) in X.dma_waits:
                            eng.wait_ge(s, v)
                        ins = X.fn(eng)
                        if X.is_dma:
                            ins.then_inc(X.sem, 16)
                        elif X.signal:
                            ins.then_inc(self.esem[e], 1)
                    if e == "sp":
                        for X in self.final_dma:
                            eng.wait_ge(X.sem, X.sem_val)
                return body
            block.tensor(make("pe"))
            block.scalar(make("act"))
            block.vector(make("dve"))
            block.gpsimd(make("pool"))
            block.sync(make("sp"))


class Ring:
    def __init__(self, items):
        self.items = items
        self.i = 0

    def __call__(self):
        v = self.items[self.i % len(self.items)]
        self.i += 1
        return v


def _consts():
    c = {}
    c["identf"] = np.eye(128, dtype=np.float32)
    c["identb"] = np.eye(128, dtype=np.float32).astype(ml_dtypes.bfloat16)
    c["onesf"] = np.ones((128, 128), np.float32)
    j = np.arange(128)[:, None]
    i = np.arange(128)[None, :]
    same = (j // 64) == (i // 64)
    tri = (same & (j <= i)).astype(np.float32)
    ref = (i // 64) * 64 + 31
    last = (i // 64) * 64 + 63
    t_ref = (same & (j <= ref)).astype(np.float32)
    t_last = (same & (j <= last)).astype(np.float32)
    sc = -1.0 / 16.0
    T0 = sc * tri
    T1 = sc * (tri - t_ref)
    T2 = sc * (t_last - tri)
    c["tri"] = np.ascontiguousarray(np.stack([T0, T1, T2], axis=1)).astype(np.float32)
    c["gmask"] = np.ascontiguousarray(np.tile(tri, (1, 4))).astype(np.float32)
    slopes = np.exp2(-(np.arange(1, 9, dtype=np.float64))).astype(np.float32)
    iq = np.arange(128)[:, None]
    jk = np.arange(256)[None, :]
    dist = 128 + iq - jk
    valid = (dist >= 0) & (dist < 128)
    bias = np.zeros((128, 2, 4, 256), np.float32)
    for g in range(2):
        for jh in range(4):
            h = 4 * g + jh
            bias[:, g, jh, :] = np.where(valid, -slopes[h] * dist.astype(np.float32), NEG)
    c["swab"] = np.ascontiguousarray(bias.reshape(128, 2, 1024))
    return c


def _chunkcols(v):
    n = v.shape[0] // 128
    return np.ascontiguousarray(v.reshape(n, 128).T)


def _layout_params(L, g_mix, g_ffn, g_final, gla_norm_g, conv_w, ffn_conv_w, swa_sinks, gla_w_alpha, gla_b_alpha):
    gv = np.concatenate([_chunkcols(g_mix[l]) for l in range(L)] + [_chunkcols(g_ffn[l]) for l in range(L)]
                        + [_chunkcols(g_final)], axis=1).astype(np.float32)
    glag = np.concatenate([_chunkcols(gla_norm_g[l]) for l in range(L)], axis=1).astype(np.float32)
    cw = np.stack([np.stack([_chunkcols(conv_w[l, k]) for k in range(3)], axis=2) for l in range(L)], axis=1)
    fw = np.stack([np.stack([_chunkcols(ffn_conv_w[l, k]) for k in range(3)], axis=2) for l in range(L)], axis=1)
    sinks = np.ascontiguousarray(np.broadcast_to(swa_sinks[:L].reshape(1, L * 8), (128, L * 8))).astype(np.float32)
    wal = np.concatenate([gla_w_alpha[:L], gla_b_alpha[:L, None, :]], axis=1).astype(np.float32)
    wal = np.ascontiguousarray(np.transpose(wal, (1, 0, 2)))
    return {"gv": gv, "glag": glag, "convw": np.ascontiguousarray(cw.astype(np.float32)),
            "fconvw": np.ascontiguousarray(fw.astype(np.float32)), "sinks": sinks, "walpha": wal}


class _Stop(Exception):
    pass


_MARKS = {}


def build_nc(S, L, final_norm=True, dbg=None, stage=None):
    assert S % NT == 0
    NTILES = S // NT
    nc = bass.Bass("TRN2", target_bir_lowering=False)

    def dram(name, shape, dt, kind="ExternalInput"):
        return nc.dram_tensor(name, list(shape), dt, kind=kind).ap()

    x_d = dram("x", [S, D], F32)
    out_d = dram("out", [S, D], F32, kind="ExternalOutput")
    wspec = {"w_in": (D, DIN), "w_gla_o": (512, D), "w_conv_o": (512, D), "w_swa_o": (512, D),
             "w_o": (D, D), "w_up": (D, 2 * DFF), "w_down": (DFF, D)}
    w_d = {k: dram(k, [L, r, c], F32) for k, (r, c) in wspec.items()}
    wb_d = {k: dram(k + "_bf", [L, r, c], BF16, kind="Internal") for k, (r, c) in wspec.items()}
    gv_d = dram("gv", [128, (2 * L + 1) * 8], F32)
    glag_d = dram("glag", [128, L * 4], F32)
    convw_d = dram("convw", [128, L, 4, 3], F32)
    fconvw_d = dram("fconvw", [128, L, 44, 3], F32)
    sinks_d = dram("sinks", [128, L * 8], F32)
    walpha_d = dram("walpha", [17, L, 512], F32)
    identf_d = dram("identf", [128, 128], F32)
    identb_d = dram("identb", [128, 128], BF16)
    onesf_d = dram("onesf", [128, 128], F32)
    tri_d = dram("tri", [128, 3, 128], F32)
    gmask_d = dram("gmask", [128, 512], F32)
    swab_d = dram("swab", [128, 2, 1024], F32)
    dbg_d = {}
    if dbg:
        for name, (shape, dt) in dbg.items():
            dbg_d[name] = dram("dbg_" + name, shape, dt, kind="ExternalOutput")

    with ExitStack() as st:
        P = Prog(nc, st)

        def chk(n):
            P.mark(n)
            if stage is not None and stage == n:
                raise _Stop()

        def sb(name, shape, dt):
            return st.enter_context(nc.sbuf_tensor("s_" + name, list(shape), dt))

        def ps(name, shape, dt):
            return st.enter_context(nc.psum_tensor(name, list(shape), dt))

        def VT(name, shape, dt):
            t = sb(name, shape, dt)
            return V(t[:], [Buf(name)])

        def chunked(name, n, cols, dt):
            t = sb(name, [128, n, cols], dt)
            return t, [V(t[:, i, :], [Buf("%s%d" % (name, i))]) for i in range(n)]

        hT_t, hT = chunked("hT", 8, NT, F32)
        uT_t, uT = chunked("uT", 8, NT, BF16)
        ar_t, ar = chunked("arena", 22, NT, BF16)
        goT, cvT, oswT, mgT, gT = ar[0:4], ar[4:8], ar[8:12], ar[12:20], ar
        qs_t, qs = chunked("qs", 4, NT, BF16)
        sg_t, sigb = chunked("sigb", 24, NT, BF16)
        vt_t, vt = chunked("vt", 4, NT, BF16)
        kT = [[VT("kT%d_%d" % (l, g), [128, 640], BF16) for g in range(2)] for l in range(L)]
        Vr = []
        for l in range(L):
            t = sb("Vr%d" % l, [128, 5, 128], BF16)
            Vr.append([V(t[:, i, :], [Buf("Vr%d_%d" % (l, i))]) for i in range(5)])
        swab = VT("swab", [128, 2, 1024], F32)
        Sst = [[VT("Sst%d_%d" % (l, h), [128, 128], F32) for h in range(4)] for l in range(L)]
        Sbf = []
        for i in range(2):
            t = sb("Sbf%d" % i, [128, 9, 128], BF16)
            Sbf.append([V(t[:, c, :], [Buf("Sbf%d_%d" % (i, c))]) for c in range(9)])
        hc_t = sb("halo_c", [128, L, 4, 2], F32)
        halo_c = [[V(hc_t[:, l, c, :], [Buf("hc%d_%d" % (l, c))]) for c in range(4)] for l in range(L)]
        hf_t = sb("halo_f", [128, L, 44, 2], F32)
        halo_f = [[V(hf_t[:, l, c, :], [Buf("hf%d_%d" % (l, c))]) for c in range(44)] for l in range(L)]
        halo_all = [V(hc_t[:], [b for l in range(L) for c in range(4) for b in halo_c[l][c].bufs]),
                    V(hf_t[:], [b for l in range(L) for c in range(44) for b in halo_f[l][c].bufs])]
        identf = VT("identf", [128, 128], F32)
        identb = VT("identb", [128, 128], BF16)
        onesb = VT("onesb", [128, 128], BF16)
        sq_t = sb("sqpool", [128, 3, NT], BF16)
        sqpool = Ring([V(sq_t[:, i, :], [Buf("sq%d" % i)]) for i in range(3)])
        tri = VT("tri", [128, 3, 128], F32)
        gmask = VT("gmask", [128, 512], F32)
        gv = VT("gv", [128, (2 * L + 1) * 8], F32)
        glag = VT("glag", [128, L * 4], F32)
        convw = VT("convw", [128, L, 4, 3], F32)
        fconvw = VT("fconvw", [128, L, 44, 3], F32)
        sinks = VT("sinks", [128, L * 8], F32)
        walpha = VT("walpha", [17, L, 512], BF16)
        gaaug = VT("gaaug", [32, 512], BF16)
        epsb = VT("epsb", [128, 1], F32)
        NF = 9
        fp_t = sb("fpool", [128, NF, 514], F32)
        fpool = Ring([V(fp_t[:, i, :], [Buf("fp%d" % i)]) for i in range(NF)])
        NB = 20
        bp_t = sb("bpool", [128, NB, 512], BF16)
        bpool = Ring([V(bp_t[:, i, :], [Buf("bp%d" % i)]) for i in range(NB)])
        sc_t = sb("scpool", [128, 2, 1024], F32)
        scV = [V(sc_t[:, i, :], [Buf("sc%d" % i)]) for i in range(2)]
        scpool = Ring(scV)
        ltokb = [scV[tb // 2][:, (tb % 2) * 512:(tb % 2 + 1) * 512] for tb in range(4)]
        ltokh = [V(sc_t[:, tb // 2, :].bitcast(BF16)[:, (tb % 2) * 1024:(tb % 2) * 1024 + 512], scV[tb // 2].bufs)
                 for tb in range(4)]
        trib = VT("trib", [128, 3, 128], BF16)
        pp_t = sb("ppool", [128, 2, 1024], BF16)
        ppool = Ring([V(pp_t[:, i, :], [Buf("pp%d" % i)]) for i in range(2)])
        pt_t = sb("ptpool", [128, 2, 1024], BF16)
        ptpool = Ring([V(pt_t[:, i, :], [Buf("pt%d" % i)]) for i in range(2)])
        kt_t = sb("ktok", [128, 2, 4, 128], BF16)
        ktpool = Ring([(i, [V(kt_t[:, i, tb, :], [Buf("ktok%d_%d" % (i, tb))]) for tb in range(4)]) for i in range(2)])
        NS = 24
        sm_t = sb("small", [128, NS, 8], F32)
        small = Ring([V(sm_t[:, i, :], [Buf("sm%d" % i)]) for i in range(NS)])
        NW = 4
        wr_t = sb("wring", [128, NW, 8, 512], BF16)
        wring = Ring([V(wr_t[:, i, :, :], [Buf("wr%d" % i)]) for i in range(NW)])
        NPS = 7
        psb = [ps("ps%d" % i, [128, 512], F32) for i in range(NPS)]
        psV = [V(psb[i][:], [Buf("ps%d" % i)]) for i in range(NPS)]
        pspool = Ring(psV)
        mixring = Ring(psV[0:5])
        poring = Ring(psV[5:6])
        gatering = Ring(psV[6:7])
        pbt = ps("psbf", [128, 1024], BF16)
        PB = V(pbt[:], [Buf("psbf")])

        def _body():
            for dst, src in ((identf, identf_d), (identb, identb_d), (tri, tri_d), (gmask, gmask_d),
                             (swab, swab_d), (gv, gv_d), (glag, glag_d), (convw, convw_d), (fconvw, fconvw_d),
                             (sinks, sinks_d)):
                P.dma(dst, V(src, []))
            P.dma(walpha, V(walpha_d, []), queue="pool")
            P.memset("pool", epsb, EPS)
            P.copy("pool", trib, tri)
            P.memset("pool", onesb, 1.0)
            P.memset("pool", gaaug, 1.0)
            P.memset("pool", halo_all[0], 0.0)
            P.memset("pool", halo_all[1], 0.0)
            for l in range(L):
                for h in range(4):
                    P.memset("pool", Sst[l][h], 0.0)
                for g in range(2):
                    P.memset("pool", kT[l][g], 0.0)
                P.memset("pool", Vr[l][0], 0.0)

            chk(1)
            wbuf = {}
            order = ["w_in", "w_gla_o", "w_conv_o", "w_swa_o", "w_o", "w_up", "w_down"]
            for l in range(L):
                for k in order:
                    r, c = wspec[k]
                    b = Buf("%s_bf%d" % (k, l))
                    wbuf[(k, l)] = b
                    for r0 in range(0, r, 128):
                        P.dma(V(wb_d[k][l, r0:r0 + 128, :], [b]), V(w_d[k][l, r0:r0 + 128, :], []), queue="pool", nowaw=True)

            chk(2)
            def wsrc(k, l, r0, nk, c0, ncol):
                ap = wb_d[k][l, r0 * 128:(r0 + nk) * 128, c0:c0 + ncol].rearrange("(kc p) n -> p kc n", p=128)
                return V(ap, [wbuf[(k, l)]])

            def wload(k, l, r0, nk, c0, ncol):
                slot = wring()
                P.dma(slot[:, 0:nk, 0:ncol], wsrc(k, l, r0, nk, c0, ncol), final=(stage in (61, 62)))
                return slot

            def dump(name, v):
                if dbg and name in dbg_d:
                    P.dma(V(dbg_d[name], [Buf("dbg_" + name)]), v, final=True)

            def rms_stats(srcs, nparts_scale, ring=None):
                pst = (ring or pspool)()
                n = len(srcs)
                for i, s in enumerate(srcs):
                    sq = sqpool()
                    P.act(sq, s, AF.Square)
                    P.mm(pst, onesb, sq, start=(i == 0), stop=(i == n - 1))
                ln = fpool()
                P.act(ln[:, 0:NT], pst, AF.Ln, bias=epsb, scale=nparts_scale)
                r = fpool()
                P.act(r[:, 0:NT], ln[:, 0:NT], AF.Exp, scale=-0.5)
                return r

            def norm_to_uT(gcol0):
                r = rms_stats(hT, 1.0 / D)
                for c in range(8):
                    P.stt(uT[c], hT[c], gv[:, gcol0 + c:gcol0 + c + 1], r[:, 0:NT], ALU.mult, ALU.mult)

            def proj(wslot, col0, ncols_m, kcs=8, rhs_list=None, ring=None):
                rhs_list = rhs_list if rhs_list is not None else uT
                pt = (ring or pspool)()
                for kc in range(kcs):
                    P.mm(pt[0:ncols_m, :], wslot[:, kc, col0:col0 + ncols_m], rhs_list[kc], start=(kc == 0), stop=(kc == kcs - 1))
                return pt

            for t in range(NTILES):
                tok0 = t * NT
                for tb in range(4):
                    xs = scpool()
                    P.dma(xs, V(x_d[tok0 + tb * 128: tok0 + (tb + 1) * 128, :], []))
                    for half in range(2):
                        pt = pspool()
                        for cc in range(4):
                            c = half * 4 + cc
                            P.tr(pt[:, cc * 128:(cc + 1) * 128], xs[:, c * 128:(c + 1) * 128], identf)
                        dst = V(hT_t[:, half * 4:(half + 1) * 4, tb * 128:(tb + 1) * 128],
                                [hT[half * 4 + cc].bufs[0] for cc in range(4)])
                        src = V(pt.ap.rearrange("p (c n) -> p c n", c=4), pt.bufs)
                        P.copy("act" if half == 0 else "dve", dst, src)

                chk(3)
                for l in range(L):
                    first = (t == 0)
                    norm_to_uT(l * 8)
                    if t == 0 and l == 0:
                        dump("uT0", V(uT_t[:, 0, :], uT[0].bufs))

                    chk(4)
                    gring = mixring
                    wsm = wring()
                    P.dma(wsm[:, :, 0:16], wsrc("w_in", l, 0, 8, 2048, 16), nowaw=True)
                    for g in range(2):
                        for d2 in range(2):
                            P.dma(wsm[:, :, 16 + g * 128 + d2 * 64: 16 + g * 128 + (d2 + 1) * 64],
                                  wsrc("w_in", l, 0, 8, 4112 + g * 64, 64), nowaw=True)
                    P.dma(wsm[:, :, 272:400], wsrc("w_in", l, 0, 8, 4240, 128), nowaw=True)
                    pga = gring()
                    for kc in range(8):
                        P.mm(pga[0:16, :], wsm[:, kc, 0:16], uT[kc], start=(kc == 0), stop=(kc == 7))
                    P.copy("act", gaaug[0:16, :], pga[0:16, :])
                    ltok = []
                    for tb in range(4):
                        px = gring()
                        P.mm(px, gaaug[0:17, tb * 128:(tb + 1) * 128], walpha[0:17, l, :])
                        e = ltokb[tb]
                        P.act(e, px, AF.Exp, scale=-1.0)
                        ltok.append(e)
                    for tb in range(4):
                        P.act(ltokh[tb], ltok[tb], AF.Ln, bias=1.0)
                    chk(5)
                    for g in range(2):
                        pk = gring()
                        for kc in range(8):
                            P.mm(pk, wsm[:, kc, 16 + g * 128:16 + (g + 1) * 128], uT[kc], start=(kc == 0), stop=(kc == 7))
                        P.copy("act", kT[l][g][:, 128:640], pk)
                    for tb in range(4):
                        pv = gring()
                        for kc in range(8):
                            P.mm(pv[:, 0:128], uT[kc][:, tb * 128:(tb + 1) * 128], wsm[:, kc, 272:400], start=(kc == 0), stop=(kc == 7))
                        P.copy("act", Vr[l][tb + 1], pv[:, 0:128])
                    w_sq = wload("w_in", l, 0, 8, 3600, 512)
                    for c4 in range(4):
                        pq = proj(w_sq, c4 * 128, 128, ring=gring)
                        P.act(qs[c4], pq, AF.Copy, scale=0.125)
                    chk(8)
                    w_cx = wload("w_in", l, 0, 8, 2064, 512)
                    w_cb = wload("w_in", l, 0, 8, 2576, 512)
                    w_cc = wload("w_in", l, 0, 8, 3088, 512)
                    for c4 in range(4):
                        pcx = proj(w_cx, c4 * 128, 128, ring=gring)
                        cxs = fpool()
                        P.copy("act", cxs[:, 0:NT], pcx)
                        pcc = proj(w_cc, c4 * 128, 128, ring=gring)
                        pbuf = fpool()
                        P.copy("pool", pbuf[:, 0:2], halo_c[l][c4])
                        P.tt("dve", pbuf[:, 2:2 + NT], pcc, cxs[:, 0:NT], ALU.mult)
                        P.copy("pool", halo_c[l][c4], pbuf[:, NT:NT + 2])
                        acc = fpool()
                        P.act(acc[:, 0:NT], pbuf[:, 0:NT], AF.Copy, scale=convw[:, l, c4, 0:1])
                        P.stt(acc[:, 0:NT], pbuf[:, 1:1 + NT], convw[:, l, c4, 1:2], acc[:, 0:NT], ALU.mult, ALU.add)
                        P.stt(acc[:, 0:NT], pbuf[:, 2:2 + NT], convw[:, l, c4, 2:3], acc[:, 0:NT], ALU.mult, ALU.add)
                        pcb = proj(w_cb, c4 * 128, 128, ring=gring)
                        P.tt("dve", cvT[c4], pcb, acc[:, 0:NT], ALU.mult)
                    chk(6)
                    w_gv = wload("w_in", l, 0, 8, 1024, 512)
                    for tb in range(4):
                        pv = gring()
                        for kc in range(8):
                            P.mm(pv, uT[kc][:, tb * 128:(tb + 1) * 128], w_gv[:, kc, :], start=(kc == 0), stop=(kc == 7))
                        P.copy("act", vt[tb], pv)
                    w_gq = wload("w_in", l, 0, 8, 0, 512)
                    w_gk = wload("w_in", l, 0, 8, 512, 512)
                    HS = [slice(hh * 128, (hh + 1) * 128) for hh in range(4)]
                    gl = [dict() for _ in range(4)]

                    def G1(hh):
                        hs = HS[hh]
                        d = gl[hh]
                        pA, pB, pb = gring(), gring(), gring()
                        for tb in range(4):
                            ts_ = slice(tb * 128, (tb + 1) * 128)
                            P.mm(pA[:, ts_], ltokh[tb][:, hs], trib[:, 1, :])
                            P.mm(pB[:, ts_], ltokh[tb][:, hs], trib[:, 2, :])
                            P.mm(pb[:, ts_], ltokh[tb][:, hs], trib[:, 0, :])
                        E1, E2, E3, E4 = fpool(), fpool(), fpool(), fpool()
                        P.act(E1[:, 0:NT], pA, AF.Exp)
                        P.act(E2[:, 0:NT], pA, AF.Exp, scale=-1.0)
                        P.act(E3[:, 0:NT], pB, AF.Exp)
                        P.act(E4[:, 0:NT], pb, AF.Exp)
                        pq = proj(w_gq, hh * 128, 128, ring=gring)
                        d["qa"], d["qb"] = bpool(), bpool()
                        P.stt(d["qa"], pq, 128.0 ** -0.5, E1[:, 0:NT], ALU.mult, ALU.mult)
                        P.stt(d["qb"], pq, 128.0 ** -0.5, E4[:, 0:NT], ALU.mult, ALU.mult)
                        pk = proj(w_gk, hh * 128, 128, ring=gring)
                        d["ka"], d["kb"] = bpool(), bpool()
                        P.tt("dve", d["ka"], pk, E2[:, 0:NT], ALU.mult)
                        P.tt("dve", d["kb"], pk, E3[:, 0:NT], ALU.mult)
                        d["dec"] = small()
                        P.copy("pool", d["dec"], V(E4.ap[:, 0:NT].rearrange("p (c k) -> p c k", c=8)[:, :, 63], E4.bufs))
                        if t == 0 and l == 0 and hh == 0:
                            dump("E4", E4[:, 0:NT])

                    def G2(hh):
                        hs = HS[hh]
                        d = gl[hh]
                        ktok = ktpool()
                        for tb in range(4):
                            P.tr(PB[:, tb * 128:(tb + 1) * 128], d["kb"][:, tb * 128:(tb + 1) * 128], identb)
                        P.copy("act", V(kt_t[:, ktok[0], :, :], [b for tb in range(4) for b in ktok[1][tb].bufs]),
                               V(PB.ap[:, 0:512].rearrange("p (a b) -> p a b", a=4), PB.bufs))
                        ktok = ktok[1]
                        pat = gring()
                        for tb in range(4):
                            ts_ = slice(tb * 128, (tb + 1) * 128)
                            P.mm(pat[:, ts_], d["ka"][:, ts_], d["qa"][:, ts_])
                        d["am"] = bpool()
                        P.tt("dve", d["am"], pat, gmask, ALU.mult)
                        pkv = [gring(), gring()]
                        for c in range(8):
                            tb, r0 = c // 2, (c % 2) * 64
                            P.mm(pkv[c % 2][:, tb * 128:(tb + 1) * 128], ktok[tb][r0:r0 + 64, :], vt[tb][r0:r0 + 64, hs])
                        d["pkv"] = pkv

                    def G3(hh):
                        d = gl[hh]
                        sb_ = Sbf[hh % 2]
                        pkv = d["pkv"]
                        P.copy("dve", sb_[0], Sst[l][hh])
                        for c in range(8):
                            tb = c // 2
                            P.stt(Sst[l][hh], Sst[l][hh], d["dec"][:, c:c + 1], pkv[c % 2][:, tb * 128:(tb + 1) * 128],
                                  ALU.mult, ALU.add)
                            P.copy("dve", sb_[c + 1], Sst[l][hh])

                    def G4(hh):
                        hs = HS[hh]
                        d = gl[hh]
                        sb_ = Sbf[hh % 2]
                        po = poring()
                        for tb in range(4):
                            ts_ = slice(tb * 128, (tb + 1) * 128)
                            P.mm(po[:, ts_], vt[tb][:, hs], d["am"][:, ts_], start=True, stop=False)
                            for c2 in range(2):
                                c = tb * 2 + c2
                                cs = slice(c * 64, (c + 1) * 64)
                                P.mm(po[:, cs], sb_[c], d["qb"][:, cs], start=False, stop=(c2 == 1))
                        d["po"] = po

                    def G5(hh, w_gr):
                        d = gl[hh]
                        po = d["po"]
                        pg = proj(w_gr, hh * 128, 128, ring=gring)
                        sg = fpool()
                        P.act(sg[:, 0:NT], pg, AF.Exp, scale=-1.0)
                        P.act(sg[:, 0:NT], sg[:, 0:NT], AF.Ln, bias=1.0)
                        P.act(sg[:, 0:NT], sg[:, 0:NT], AF.Exp, scale=-1.0)
                        P.stt(sg[:, 0:NT], pg, 1.0, sg[:, 0:NT], ALU.mult, ALU.mult)
                        r = rms_stats([po], 1.0 / 128, ring=gring)
                        tmp = fpool()
                        P.stt(tmp[:, 0:NT], po, glag[:, l * 4 + hh:l * 4 + hh + 1], r[:, 0:NT], ALU.mult, ALU.mult)
                        P.tt("pool", goT[hh], tmp[:, 0:NT], sg[:, 0:NT], ALU.mult)
                        if t == 0 and l == 0 and hh == 0:
                            dump("tmp_o", tmp[:, 0:NT])

                    for hh in range(4):
                        G1(hh)
                    chk(7)
                    w_gr = wload("w_in", l, 0, 8, 1536, 512)
                    G2(0); G3(0)
                    G2(1); G3(1)
                    G4(0); G5(0, w_gr)
                    G2(2); G3(2)
                    G4(1); G5(1, w_gr)
                    G2(3); G3(3)
                    G4(2); G5(2, w_gr)
                    G4(3); G5(3, w_gr)
                    chk(9)
                    sw = {}

                    def SA(i):
                        tb, g = i // 2, i % 2
                        sA, sB = psV[(i % 2) * 2], psV[(i % 2) * 2 + 1]
                        for j in range(4):
                            h = 4 * g + j
                            ch, po_ = h // 2, (h % 2) * 64
                            bank = sA if j % 2 == 0 else sB
                            P.mm(bank[:, (j // 2) * 256:(j // 2 + 1) * 256], qs[ch][po_:po_ + 64, tb * 128:(tb + 1) * 128],
                                 kT[l][g][po_:po_ + 64, tb * 128:tb * 128 + 256])
                        sc = scpool()
                        sc4 = V(sc.ap.rearrange("p (j k) -> p j k", j=4), sc.bufs)
                        sw4 = V(swab.ap[:, g, :].rearrange("p (j k) -> p j k", j=4), swab.bufs)
                        P.tt("dve", sc4[:, 0::2, :], V(sA.ap.rearrange("p (j k) -> p j k", j=2), sA.bufs), sw4[:, 0::2, :], ALU.add)
                        P.tt("dve", sc4[:, 1::2, :], V(sB.ap.rearrange("p (j k) -> p j k", j=2), sB.bufs), sw4[:, 1::2, :], ALU.add)
                        if first and tb == 0:
                            P.ts("pool", sc4[:, :, 0:128], sc4[:, :, 0:128], NEG, ALU.add)
                        sm, sm2, sm3 = small(), small(), small()
                        mx, negm = sm[:, 0:4], sm[:, 4:8]
                        rsum, dd = sm2[:, 0:4], sm2[:, 4:8]
                        es, rinv = sm3[:, 0:4], sm3[:, 4:8]
                        sk_ = sinks[:, l * 8 + g * 4:l * 8 + g * 4 + 4]
                        P.reduce(mx, sc4, ALU.max)
                        P.tt("dve", mx, mx, sk_, ALU.max)
                        P.ts("dve", negm, mx, -1.0, ALU.mult)
                        pn = ppool()
                        for j in range(4):
                            P.act(pn[:, j * 256:(j + 1) * 256], sc[:, j * 256:(j + 1) * 256], AF.Exp,
                                  bias=negm[:, j:j + 1], accum=rsum[:, j:j + 1])
                        P.tt("dve", dd, sk_, mx, ALU.subtract)
                        P.act(es, dd, AF.Exp)
                        P.tt("dve", es, es, rsum, ALU.add)
                        P.recip(rinv, es)
                        pn4 = V(pn.ap.rearrange("p (j k) -> p j k", j=4), pn.bufs)
                        P.tt("dve", pn4, pn4, V(rinv.ap.unsqueeze(2).broadcast_to([128, 4, 256]), rinv.bufs), ALU.mult)
                        sw[i] = pn

                    def SB(i):
                        tb, g = i // 2, i % 2
                        pn = sw.pop(i)
                        posw = psV[4 + (tb % 2)]
                        for j in range(4):
                            for kb in range(2):
                                P.tr(PB[:, (kb * 4 + j) * 128:(kb * 4 + j + 1) * 128],
                                     pn[:, j * 256 + kb * 128: j * 256 + (kb + 1) * 128], identb)
                        ptt = ptpool()
                        P.copy("act", ptt, PB)
                        for j in range(4):
                            h = 4 * g + j
                            ch, po_ = h // 2, (h % 2) * 64
                            for kb in range(2):
                                P.mm(posw[po_:po_ + 64, ch * 128:(ch + 1) * 128], Vr[l][tb + kb][:, g * 64:(g + 1) * 64],
                                     ptt[:, (kb * 4 + j) * 128:(kb * 4 + j + 1) * 128], start=(kb == 0), stop=(kb == 1))
                        if g == 1:
                            dst = V(ar_t[:, 8:12, tb * 128:(tb + 1) * 128], [oswT[c4].bufs[0] for c4 in range(4)])
                            P.copy("act", dst, V(posw.ap.rearrange("p (c n) -> p c n", c=4), posw.bufs))

                    SA(0)
                    for i in range(8):
                        if i + 1 < 8:
                            SA(i + 1)
                        SB(i)
                    for g in range(2):
                        P.copy("pool", kT[l][g][:, 0:128], kT[l][g][:, 512:640])
                    P.copy("pool", Vr[l][0], Vr[l][4])
                    if t == 0 and l == 0:
                        dump("go0", V(ar_t[:, 0, :], goT[0].bufs))
                        dump("cv0", V(ar_t[:, 4, :], cvT[0].bufs))
                        dump("osw0", V(ar_t[:, 8, :], oswT[0].bufs))


                    bo_names = ["w_gla_o", "w_conv_o", "w_swa_o"]
                    brs = [goT, cvT, oswT]
                    for b in range(3):
                        for mgp in range(2):
                            wg = wload("w_in", l, 0, 8, 4368 + b * 1024 + mgp * 512, 512)
                            for m4 in range(4):
                                pgt = proj(wg, m4 * 128, 128, ring=gatering)
                                e = fpool()
                                P.act(e[:, 0:NT], pgt, AF.Exp, scale=-1.0)
                                P.act(e[:, 0:NT], e[:, 0:NT], AF.Ln, bias=1.0)
                                P.act(sigb[b * 8 + mgp * 4 + m4], e[:, 0:NT], AF.Exp, scale=-1.0)
                    for mgp in range(2):
                        wbo = [wload(bo_names[b], l, 0, 4, mgp * 512, 512) for b in range(3)]
                        for m4 in range(4):
                            m = mgp * 4 + m4
                            terms = []
                            for b in range(3):
                                py = proj(wbo[b], m4 * 128, 128, kcs=4, rhs_list=brs[b])
                                tm = fpool()
                                P.tt("dve", tm[:, 0:NT], py, sigb[b * 8 + m], ALU.mult)
                                terms.append(tm)
                            P.tt("pool", terms[0][:, 0:NT], terms[0][:, 0:NT], terms[1][:, 0:NT], ALU.add)
                            P.tt("pool", mgT[m], terms[0][:, 0:NT], terms[2][:, 0:NT], ALU.add)
                    chk(11)
                    for mgp in range(2):
                        wo = wload("w_o", l, 0, 8, mgp * 512, 512)
                        for m4 in range(4):
                            m = mgp * 4 + m4
                            pt = proj(wo, m4 * 128, 128, rhs_list=mgT)
                            P.tt("dve", hT[m], pt, hT[m], ALU.add)
                    if t == 0 and l == 0:
                        dump("h1", V(hT_t[:, 0, :], hT[0].bufs))

                    chk(12)
                    norm_to_uT((L + l) * 8)
                    for pg in range(6):
                        npair = 4 if pg < 5 else 2
                        wa = wload("w_up", l, 0, 8, pg * 512, npair * 128)
                        wb = wload("w_up", l, 0, 8, DFF + pg * 512, npair * 128)
                        for pi in range(npair):
                            c = pg * 4 + pi
                            accs = []
                            for (wsl, cidx) in ((wa, c), (wb, c + 22)):
                                ph = proj(wsl, pi * 128, 128)
                                hb = fpool()
                                P.copy("pool", hb[:, 0:2], halo_f[l][cidx])
                                P.copy("act", hb[:, 2:2 + NT], ph)
                                P.copy("pool", halo_f[l][cidx], hb[:, NT:NT + 2])
                                acc = fpool()
                                P.act(acc[:, 0:NT], ph, AF.Copy, scale=fconvw[:, l, cidx, 2:3])
                                P.stt(acc[:, 0:NT], hb[:, 0:NT], fconvw[:, l, cidx, 0:1], acc[:, 0:NT], ALU.mult, ALU.add)
                                P.stt(acc[:, 0:NT], hb[:, 1:1 + NT], fconvw[:, l, cidx, 1:2], acc[:, 0:NT], ALU.mult, ALU.add)
                                accs.append(acc)
                            sa = fpool()
                            P.act(sa[:, 0:NT], accs[0][:, 0:NT], AF.Silu)
                            P.tt("dve", gT[c], sa[:, 0:NT], accs[1][:, 0:NT], ALU.mult)
                    chk(13)
                    for mgp in range(2):
                        banks = [pspool() for _ in range(4)]
                        for (k0, nk) in ((0, 8), (8, 8), (16, 6)):
                            wd = wload("w_down", l, k0, nk, mgp * 512, 512)
                            for m4 in range(4):
                                for kk in range(nk):
                                    k = k0 + kk
                                    P.mm(banks[m4], wd[:, kk, m4 * 128:(m4 + 1) * 128], gT[k], start=(k == 0), stop=(k == 21))
                        for m4 in range(4):
                            m = mgp * 4 + m4
                            P.tt("dve", hT[m], banks[m4], hT[m], ALU.add)
                    if t == 0 and l == 0:
                        dump("h2", V(hT_t[:, 0, :], hT[0].bufs))

                chk(14)
                if final_norm:
                    r = rms_stats(hT, 1.0 / D)
                ofm = []
                for c in range(8):
                    o = fpool()
                    if final_norm:
                        P.stt(o[:, 0:NT], hT[c], gv[:, 2 * L * 8 + c:2 * L * 8 + c + 1], r[:, 0:NT], ALU.mult, ALU.mult)
                    else:
                        P.copy("pool", o[:, 0:NT], hT[c])
                    ofm.append(o)
                for tb in range(4):
                    xo = scpool()
                    for half in range(2):
                        pt = pspool()
                        for cc in range(4):
                            c = half * 4 + cc
                            P.tr(pt[:, cc * 128:(cc + 1) * 128], ofm[c][:, tb * 128:(tb + 1) * 128], identf)
                        P.copy("act" if half == 0 else "dve", xo[:, half * 512:(half + 1) * 512], pt)
                    P.dma(V(out_d[tok0 + tb * 128: tok0 + (tb + 1) * 128, :], [Buf("out%d_%d" % (t, tb))]), xo, final=True)

        try:
            _body()
        except _Stop:
            pass
        P.emit()
        nc_marks = P.marks
        nc_counts = {e: len(v) for e, v in P.streams.items()}
    _MARKS[id(nc)] = (nc_marks, nc_counts)
    return nc


_WNAMES = ["w_in", "w_gla_o", "w_conv_o", "w_swa_o", "w_o", "w_up", "w_down"]


def make_in_maps(x, params, L):
    consts = _consts()
    lay = _layout_params(L, params["g_mix"], params["g_ffn"], params["g_final"], params["gla_norm_g"], params["conv_w"],
                         params["ffn_conv_w"], params["swa_sinks"], params["gla_w_alpha"], params["gla_b_alpha"])
    shared = {}
    shared.update(consts)
    shared.update(lay)
    for k in _WNAMES:
        shared[k] = np.ascontiguousarray(params[k][:L], dtype=np.float32)
    maps = []
    for b in range(x.shape[0]):
        m = dict(shared)
        m["x"] = np.ascontiguousarray(x[b], dtype=np.float32)
        maps.append(m)
    return maps


_NC_CACHE = {}


def kernel(x, g_mix, w_in, gla_w_alpha, gla_b_alpha, gla_norm_g, conv_w, swa_sinks, w_gla_o, w_conv_o, w_swa_o, w_o,
           g_ffn, w_up, ffn_conv_w, w_down, g_final):
    x = np.asarray(x)
    B, S, _ = x.shape
    L = int(np.asarray(g_mix).shape[0])
    params = dict(g_mix=g_mix, w_in=w_in, gla_w_alpha=gla_w_alpha, gla_b_alpha=gla_b_alpha, gla_norm_g=gla_norm_g,
                  conv_w=conv_w, swa_sinks=swa_sinks, w_gla_o=w_gla_o, w_conv_o=w_conv_o, w_swa_o=w_swa_o, w_o=w_o,
                  g_ffn=g_ffn, w_up=w_up, ffn_conv_w=ffn_conv_w, w_down=w_down, g_final=g_final)
    params = {k: np.asarray(v, dtype=np.float32) for k, v in params.items()}
    key = (S, L)
    if key not in _NC_CACHE:
        _NC_CACHE[key] = build_nc(S, L)
    nc = _NC_CACHE[key]
    in_maps = make_in_maps(x, params, L)
    res = run_bass_kernel_spmd(nc, in_maps, core_ids=list(range(B)))
    out = np.stack([np.asarray(r["out"]) for r in res.results], axis=0)
    return out.astype(np.float32)
```
